# Optimizing a Trainium2 kernel written in Bass

```python
import math
import jax, jax.numpy as jnp
from jax import lax
import numpy as np

D_MODEL = 2048
BATCH = 2
SEQ = 4096
DEPTH = 4

N_EVEN = (DEPTH + 1) // 2
N_ODD = DEPTH // 2

SB_HEADS = 8
SB_HEAD_DIM = 128
SB_WIDTH = SB_HEADS * SB_HEAD_DIM
SB_BLOCK = 128
GLA_HEADS = 8
GLA_DK = 64
GLA_DV = 128
GLA_K_WIDTH = GLA_HEADS * GLA_DK
GLA_V_WIDTH = GLA_HEADS * GLA_DV
GLA_GATE_RANK = 16
GLA_GATE_TAU = 16.0
GLA_CHUNK = 64
IN_SPLITS = [SB_WIDTH, 2 * SB_WIDTH, 3 * SB_WIDTH,
             3 * SB_WIDTH + GLA_K_WIDTH,
             3 * SB_WIDTH + 2 * GLA_K_WIDTH,
             3 * SB_WIDTH + 2 * GLA_K_WIDTH + GLA_V_WIDTH,
             3 * SB_WIDTH + 2 * GLA_K_WIDTH + 2 * GLA_V_WIDTH]
IN_WIDTH = 3 * SB_WIDTH + 2 * GLA_K_WIDTH + 2 * GLA_V_WIDTH + GLA_GATE_RANK
MIX_WIDTH = SB_WIDTH + GLA_V_WIDTH
S5_GROUP = 16
S5_GROUPS = D_MODEL // S5_GROUP
S5_STATE = 64
S5_DT_MIN = 1e-3
S5_DT_MAX = 1e-1
D_FF = ((8 * D_MODEL + 3 * 256 - 1) // (3 * 256)) * 256
NORM_EPS = 1e-6

kernel_name = "hybrid_stickbreak_gla_s5_trunk"


def rms_norm(x, g):
    xf = x.astype(jnp.float32)
    return xf * lax.rsqrt(jnp.mean(xf * xf, axis=-1, keepdims=True) + NORM_EPS) * g.astype(jnp.float32)


def split_heads(t, n_heads):
    b_, l_, w_ = t.shape
    return t.reshape(b_, l_, n_heads, w_ // n_heads).transpose(0, 2, 1, 3)


def merge_heads(t):
    b_, h_, l_, d_ = t.shape
    return t.transpose(0, 2, 1, 3).reshape(b_, l_, h_ * d_)


def stick_breaking_attention(q, k, v):
    b_, h_, L, d = q.shape
    nblk = L // SB_BLOCK
    scale = d ** -0.5
    q_blocks = q.reshape(b_, h_, nblk, SB_BLOCK, d).transpose(2, 0, 1, 3, 4)
    starts = jnp.arange(nblk, dtype=jnp.int32) * SB_BLOCK
    kpos = jnp.arange(L, dtype=jnp.int32)

    def one_block(args):
        qb, start = args
        qpos = start + jnp.arange(SB_BLOCK, dtype=jnp.int32)
        mask = kpos[None, :] < qpos[:, None]
        z = jnp.einsum('bhqd,bhkd->bhqk', qb, k) * scale
        log_keep = jnp.where(mask, jax.nn.log_sigmoid(-z), 0.0)
        later = lax.cumsum(log_keep, axis=3, reverse=True) - log_keep
        w = jnp.where(mask, jnp.exp(jax.nn.log_sigmoid(z) + later), 0.0)
        return jnp.einsum('bhqk,bhkd->bhqd', w, v)

    out = lax.map(one_block, (q_blocks, starts))
    return out.transpose(1, 2, 0, 3, 4).reshape(b_, h_, L, d)


def gla_chunked(q, k, v, log_a):
    b_, h_, L, dk = q.shape
    dv = v.shape[-1]
    nc = L // GLA_CHUNK

    def to_chunks(t):
        return t.reshape(b_, h_, nc, GLA_CHUNK, t.shape[-1]).transpose(2, 0, 1, 3, 4)

    tpos = jnp.arange(GLA_CHUNK)
    incl = (tpos[None, :] <= tpos[:, None])[:, :, None]

    def step(S, xs):
        qc, kc, vc, lac = xs
        cum = jnp.cumsum(lac, axis=2)
        cum_last = cum[:, :, -1:, :]
        inter = jnp.einsum('bhtd,bhde->bhte', qc * jnp.exp(cum), S)
        diff = cum[:, :, :, None, :] - cum[:, :, None, :, :]
        decay = jnp.exp(jnp.where(incl, diff, -jnp.inf))
        scores = jnp.einsum('bhtsd,bhsd->bhts', qc[:, :, :, None, :] * decay, kc)
        intra = jnp.einsum('bhts,bhse->bhte', scores, vc)
        S_new = jnp.exp(cum_last[:, :, 0, :, None]) * S + jnp.einsum(
            'bhsd,bhse->bhde', kc * jnp.exp(cum_last - cum), vc)
        return S_new, inter + intra

    S0 = jnp.zeros((b_, h_, dk, dv), jnp.float32)
    _, out = lax.scan(step, S0, (to_chunks(q), to_chunks(k), to_chunks(v), to_chunks(log_a)))
    return out.transpose(1, 2, 0, 3, 4).reshape(b_, h_, L, dv)


def sb_gla_mixer(h, w_in, w_gate_up, b_gate, gla_norm_gain, w_out):
    proj = (h @ w_in).astype(jnp.float32)
    sb_q, sb_k, sb_v, g_q, g_k, g_v, g_r, g_lr = jnp.split(proj, IN_SPLITS, axis=-1)
    o_sb = stick_breaking_attention(split_heads(sb_q, SB_HEADS),
                                    split_heads(sb_k, SB_HEADS),
                                    split_heads(sb_v, SB_HEADS))
    o_sb = merge_heads(o_sb)
    gate_logits = g_lr @ w_gate_up.astype(jnp.float32) + b_gate.astype(jnp.float32)
    log_a = jax.nn.log_sigmoid(gate_logits) / GLA_GATE_TAU
    o_gla = gla_chunked(split_heads(g_q, GLA_HEADS) * (GLA_DK ** -0.5),
                        split_heads(g_k, GLA_HEADS),
                        split_heads(g_v, GLA_HEADS),
                        split_heads(log_a, GLA_HEADS))
    o_gla = o_gla * lax.rsqrt(jnp.mean(o_gla * o_gla, axis=-1, keepdims=True) + NORM_EPS)
    o_gla = merge_heads(o_gla) * gla_norm_gain.astype(jnp.float32) * jax.nn.silu(g_r)
    return jnp.concatenate([o_sb, o_gla], axis=-1) @ w_out


def s5_mixer(h, lam_re, lam_im, log_step, b_re, b_im, c_re, c_im, d_skip, w_glu):
    b_, L, _ = h.shape
    u = h.astype(jnp.float32).reshape(b_, L, S5_GROUPS, S5_GROUP)
    lam_re = lam_re.astype(jnp.float32)
    lam_im = lam_im.astype(jnp.float32)
    dt = jnp.exp(log_step.astype(jnp.float32))[:, None]
    mag = jnp.exp(dt * lam_re)
    ang = dt * lam_im
    lb_re, lb_im = mag * jnp.cos(ang), mag * jnp.sin(ang)
    den = lam_re * lam_re + lam_im * lam_im
    nr, ni = lb_re - 1.0, lb_im
    cr = (nr * lam_re + ni * lam_im) / den
    ci = (ni * lam_re - nr * lam_im) / den
    b_re = b_re.astype(jnp.float32)
    b_im = b_im.astype(jnp.float32)
    bb_re = cr[..., None] * b_re - ci[..., None] * b_im
    bb_im = cr[..., None] * b_im + ci[..., None] * b_re
    bu_re = jnp.einsum('blgh,gph->blgp', u, bb_re)
    bu_im = jnp.einsum('blgh,gph->blgp', u, bb_im)
    a_re = jnp.broadcast_to(lb_re, (1, L) + lb_re.shape)
    a_im = jnp.broadcast_to(lb_im, (1, L) + lb_im.shape)

    def combine(e1, e2):
        a1r, a1i, x1r, x1i = e1
        a2r, a2i, x2r, x2i = e2
        return (a1r * a2r - a1i * a2i,
                a1r * a2i + a1i * a2r,
                a2r * x1r - a2i * x1i + x2r,
                a2r * x1i + a2i * x1r + x2i)

    _, _, s_re, s_im = lax.associative_scan(combine, (a_re, a_im, bu_re, bu_im), axis=1)
    y = (jnp.einsum('ghp,blgp->blgh', c_re.astype(jnp.float32), s_re)
         - jnp.einsum('ghp,blgp->blgh', c_im.astype(jnp.float32), s_im)
         + d_skip.astype(jnp.float32).reshape(S5_GROUPS, S5_GROUP) * u)
    y = jax.nn.gelu(y.reshape(b_, L, D_MODEL), approximate=False)
    val, gate = jnp.split(y @ w_glu, 2, axis=-1)
    return val * jax.nn.sigmoid(gate)


def swiglu_ffn(h, w_ffn_in, w_ffn_out):
    gate, up = jnp.split(h @ w_ffn_in, 2, axis=-1)
    return (jax.nn.silu(gate) * up) @ w_ffn_out


def setup_inputs(seed: int = 0) -> dict:
    key = jax.random.key(seed)
    ks = jax.random.split(key, 20)
    f32 = jnp.float32
    n = lambda k, shape, s: jax.random.normal(k, shape, f32) * s
    x = jax.random.normal(ks[0], (BATCH, SEQ, D_MODEL), f32)
    norm_gains = 1.0 + n(ks[1], (DEPTH, 4, D_MODEL), 0.02)
    w_in = n(ks[2], (N_EVEN, D_MODEL, IN_WIDTH), D_MODEL ** -0.5)
    w_gate_up = n(ks[3], (N_EVEN, GLA_GATE_RANK, GLA_K_WIDTH), GLA_GATE_RANK ** -0.5)
    b_gate = n(ks[4], (N_EVEN, GLA_K_WIDTH), 0.1)
    gla_norm_gain = 1.0 + n(ks[5], (N_EVEN, GLA_V_WIDTH), 0.02)
    w_out = n(ks[6], (N_EVEN, MIX_WIDTH, D_MODEL), MIX_WIDTH ** -0.5)
    s5_lambda_re = -0.5 * jnp.exp(n(ks[7], (N_ODD, S5_GROUPS, S5_STATE), 0.05))
    s5_lambda_im = (math.pi * jnp.arange(S5_STATE, dtype=f32))[None, None, :] \
        + n(ks[8], (N_ODD, S5_GROUPS, S5_STATE), 0.01)
    s5_log_step = jax.random.uniform(ks[9], (N_ODD, S5_GROUPS), f32,
                                     math.log(S5_DT_MIN), math.log(S5_DT_MAX))
    bs = (2.0 * S5_GROUP) ** -0.5
    s5_b_re = n(ks[10], (N_ODD, S5_GROUPS, S5_STATE, S5_GROUP), bs)
    s5_b_im = n(ks[11], (N_ODD, S5_GROUPS, S5_STATE, S5_GROUP), bs)
    cs = (2.0 * S5_STATE) ** -0.5
    s5_c_re = n(ks[12], (N_ODD, S5_GROUPS, S5_GROUP, S5_STATE), cs)
    s5_c_im = n(ks[13], (N_ODD, S5_GROUPS, S5_GROUP, S5_STATE), cs)
    s5_d = n(ks[14], (N_ODD, D_MODEL), 1.0)
    w_glu = n(ks[15], (N_ODD, D_MODEL, 2 * D_MODEL), D_MODEL ** -0.5)
    w_ffn_in = n(ks[16], (DEPTH, D_MODEL, 2 * D_FF), D_MODEL ** -0.5)
    w_ffn_out = n(ks[17], (DEPTH, D_FF, D_MODEL), D_FF ** -0.5)
    return {"x": x, "norm_gains": norm_gains, "w_in": w_in, "w_gate_up": w_gate_up,
            "b_gate": b_gate, "gla_norm_gain": gla_norm_gain, "w_out": w_out,
            "s5_lambda_re": s5_lambda_re, "s5_lambda_im": s5_lambda_im,
            "s5_log_step": s5_log_step, "s5_b_re": s5_b_re, "s5_b_im": s5_b_im,
            "s5_c_re": s5_c_re, "s5_c_im": s5_c_im, "s5_d": s5_d, "w_glu": w_glu,
            "w_ffn_in": w_ffn_in, "w_ffn_out": w_ffn_out}


def reference(x, norm_gains, w_in, w_gate_up, b_gate, gla_norm_gain, w_out,
              s5_lambda_re, s5_lambda_im, s5_log_step, s5_b_re, s5_b_im,
              s5_c_re, s5_c_im, s5_d, w_glu, w_ffn_in, w_ffn_out):
    h = x.astype(jnp.float32)
    for layer in range(DEPTH):
        g = norm_gains[layer]
        i = layer // 2
        y = rms_norm(h, g[0])
        if layer % 2 == 0:
            y = sb_gla_mixer(y, w_in[i], w_gate_up[i], b_gate[i], gla_norm_gain[i], w_out[i])
        else:
            y = s5_mixer(y, s5_lambda_re[i], s5_lambda_im[i], s5_log_step[i],
                         s5_b_re[i], s5_b_im[i], s5_c_re[i], s5_c_im[i], s5_d[i], w_glu[i])
        h = h + rms_norm(y, g[1])
        y = swiglu_ffn(rms_norm(h, g[2]), w_ffn_in[layer], w_ffn_out[layer])
        h = h + rms_norm(y, g[3])
    return h.astype(x.dtype)
```

```python
import contextlib
import numpy as np
import ml_dtypes
import concourse.bass as bass
import concourse.mybir as mybir
from concourse.bass_utils import run_bass_kernel_spmd

F32 = mybir.dt.float32
BF16 = mybir.dt.bfloat16
AF = mybir.ActivationFunctionType
ALU = mybir.AluOpType

D_MODEL = 2048
SEQ = 4096
BATCH = 2
DEPTH = 4
D_FF = 5632
IN_WIDTH = 6160
EPS = 1e-6
NCH = D_MODEL // 128
TG = 512

import os
_DBG = int(os.environ.get('ODD_DBG', '0'))
_STOP = int(os.environ.get('ODD_STOP', '0'))
_SKIP = int(os.environ.get('ODD_SKIP', '0'))
SAME_ENGINE_SYNC = True
SEM_LIMIT = int(os.environ.get('SEM_LIMIT', '8000'))


class Op:
    __slots__ = ("eng", "fn", "reads", "writes", "dma", "key", "deps", "signal",
                 "sem_i", "sem_v", "ndma")

    def __init__(self, eng, fn, reads, writes, dma=False, key=None, ndma=1):
        self.eng = eng
        self.fn = fn
        self.reads = tuple(reads)
        self.writes = tuple(writes)
        self.dma = dma
        self.key = key
        self.deps = []
        self.signal = False
        self.sem_i = None
        self.sem_v = None
        self.ndma = ndma


class Prog:
    ENGS = ("pe", "act", "dve", "pool", "sp")

    def __init__(self, nc):
        self.nc = nc
        self.ops = []
        self.last_w = {}
        self.readers = {}
        self.last_dma = {}

    def op(self, eng, fn, reads=(), writes=()):
        o = Op(eng, fn, reads, writes)
        self._track(o)
        return o

    def dma(self, eng, fn, reads=(), writes=(), key=None, ndma=1):
        o = Op(eng, fn, reads, writes, dma=True, key=key, ndma=ndma)
        prev = self.last_dma.get(key)
        if prev is not None:
            o.deps.append(prev)
        self.last_dma[key] = o
        self._track(o)
        return o

    def _track(self, o):
        deps = o.deps
        for k in o.reads:
            w = self.last_w.get(k)
            if w is not None:
                deps.append(w)
        for k in o.writes:
            w = self.last_w.get(k)
            if w is not None:
                deps.append(w)
            deps.extend(self.readers.get(k, ()))
        for k in o.reads:
            self.readers.setdefault(k, []).append(o)
        for k in o.writes:
            self.last_w[k] = o
            self.readers[k] = []
        seen = set()
        dd = []
        for d in deps:
            if d is o or id(d) in seen:
                continue
            seen.add(id(d))
            dd.append(d)
        o.deps = dd
        self.ops.append(o)

    def simulate(self, per_eng, sems_of):
        semv = {}
        pos = {e: 0 for e in self.ENGS}
        total = sum(len(v) for v in per_eng.values())
        done = 0
        while done < total:
            prog = False
            for e in self.ENGS:
                while pos[e] < len(per_eng[e]):
                    o = per_eng[e][pos[e]]
                    ok = True
                    for d in o.deps:
                        k = (sems_of[id(d)]["name"], d.sem_i)
                        if semv.get(k, 0) < d.sem_v:
                            ok = False
                            break
                    if not ok:
                        break
                    if o.dma:
                        k = (sems_of[id(o)]["name"], o.sem_i)
                        semv[k] = semv.get(k, 0) + 16 * o.ndma
                        assert semv[k] == o.sem_v, (k, semv[k], o.sem_v)
                    elif o.signal:
                        k = (sems_of[id(o)]["name"], o.sem_i)
                        semv[k] = semv.get(k, 0) + 1
                        assert semv[k] == o.sem_v, (k, semv[k], o.sem_v)
                    pos[e] += 1
                    done += 1
                    prog = True
            if not prog:
                print("DEADLOCK at", {e: pos[e] for e in self.ENGS})
                for e in self.ENGS:
                    if pos[e] < len(per_eng[e]):
                        o = per_eng[e][pos[e]]
                        print(" ", e, "reads", o.reads[:4], "writes", o.writes[:4],
                              [(sems_of[id(d)]["name"], d.sem_i, d.sem_v, d.eng, d.writes[:2]) for d in o.deps][:6])
                raise RuntimeError("deadlock")
        print("PROG_SIM ok: %d ops, sems %d" % (total, len(semv)))

    def emit(self, stack):
        nc = self.nc
        for o in self.ops:
            nd = []
            for d in o.deps:
                if not d.dma and d.eng == o.eng:
                    if d.eng == "pe" or not SAME_ENGINE_SYNC or o.dma:
                        continue
                d.signal = True
                nd.append(d)
            o.deps = nd
        streams = {}

        def stream(name):
            s = streams.get(name)
            if s is None:
                s = {"sems": [], "val": 0, "name": name}
                streams[name] = s
            return s

        def bump(s, inc):
            if not s["sems"] or s["val"] + inc > SEM_LIMIT:
                s["sems"].append(stack.enter_context(
                    nc.semaphore("s_%s_%d" % (s["name"], len(s["sems"])))))
                s["val"] = 0
            s["val"] += inc
            return len(s["sems"]) - 1, s["val"]

        sems_of = {}
        for o in self.ops:
            if o.dma:
                s = stream("d_" + str(o.key))
                o.sem_i, o.sem_v = bump(s, 16 * o.ndma)
                sems_of[id(o)] = s
            elif o.signal:
                s = stream("e_" + o.eng)
                o.sem_i, o.sem_v = bump(s, 1)
                sems_of[id(o)] = s
        final = []
        for name, s in streams.items():
            final.append((s, len(s["sems"]) - 1, s["val"]))

        per_eng = {e: [o for o in self.ops if o.eng == e] for e in self.ENGS}
        if os.environ.get("PROG_SIM"):
            self.simulate(per_eng, sems_of)
        block = stack.enter_context(nc.Block())

        def run(eng_name, eng):
            waited = {}
            for o in per_eng[eng_name]:
                for d in o.deps:
                    s = sems_of[id(d)]
                    sem = s["sems"][d.sem_i]
                    k = (s["name"], d.sem_i)
                    if waited.get(k, 0) >= d.sem_v:
                        continue
                    waited[k] = d.sem_v
                    eng.wait_ge(sem, d.sem_v)
                r = o.fn(eng)
                if o.dma:
                    sem = sems_of[id(o)]["sems"][o.sem_i]
                    if not isinstance(r, (list, tuple)):
                        r = [r]
                    assert len(r) == o.ndma, (len(r), o.ndma)
                    for ins in r:
                        ins.then_inc(sem, 16)
                elif o.signal:
                    sem = sems_of[id(o)]["sems"][o.sem_i]
                    r.then_inc(sem, 1)
            if eng_name == "sp":
                for s, i, v in final:
                    if s["name"].startswith("d_"):
                        eng.wait_ge(s["sems"][i], v)

        @block.tensor
        def _(e):
            run("pe", e)

        @block.scalar
        def _(e):
            run("act", e)

        @block.vector
        def _(e):
            run("dve", e)

        @block.gpsimd
        def _(e):
            run("pool", e)

        @block.sync
        def _(e):
            run("sp", e)


class Builder:
    def __init__(self, L, layers):
        self.L = L
        self.layers = layers
        self.nc = bass.Bass("TRN2", target_bir_lowering=False)
        self.P = Prog(self.nc)
        self.uid = 0
        self.psum_rr = 0
        self.in_names = []

    def dram_in(self, name, shape, dt=F32):
        self.in_names.append(name)
        return self.nc.dram_tensor(name, list(shape), dt, kind="ExternalInput").ap()

    def dram_out(self, name, shape, dt=F32):
        return self.nc.dram_tensor(name, list(shape), dt, kind="ExternalOutput").ap()

    def dram_tmp(self, name, shape, dt):
        return self.nc.dram_tensor(name, list(shape), dt, kind="Internal").ap()

    def sb(self, stack, name, shape, dt):
        return stack.enter_context(self.nc.sbuf_tensor("sb_" + name, list(shape), dt))

    def ps(self, stack, name, shape, dt=F32):
        return stack.enter_context(self.nc.psum_tensor("ps_" + name, list(shape), dt))


def _act(P, out, in_, func, reads, writes, eng="act", **kw):
    return P.op(eng, lambda e: e.activation(out=out, in_=in_, func=func, **kw), reads, writes)


def build(L, layers, n_in_layers=None):
    B = Builder(L, layers)
    nc, P = B.nc, B.P
    n_tg = L // TG
    n_even = sum(1 for l in layers if l == "even")
    n_odd = sum(1 for l in layers if l == "odd")
    depth = len(layers)

    xT = B.dram_in("xT", [D_MODEL, L])
    gains = B.dram_in("gains", [128, depth * 4 * NCH])
    w_ffn_in = B.dram_in("w_ffn_in", [depth, D_MODEL, 2 * D_FF])
    w_ffn_out = B.dram_in("w_ffn_out", [depth, D_FF, D_MODEL])
    ones_in = B.dram_in("ones_bf", [128, 128], BF16)
    yT = B.dram_out("yT", [D_MODEL, L])
    hT = B.dram_tmp("hT", [D_MODEL, L], F32)

    stack = contextlib.ExitStack()
    with stack:
        hA = B.sb(stack, "hA", [128, NCH, TG], F32)
        xn = B.sb(stack, "xn", [128, NCH, TG], BF16)
        hid = B.sb(stack, "hid", [128, D_FF // 128, TG], BF16)
        yS = B.sb(stack, "yS", [128, NCH, TG], F32)
        NWI = 2
        wi_flat = [B.sb(stack, "wi%d" % i, [128, 4096], BF16) for i in range(NWI)]
        wi = [w[:].rearrange("p (a c f) -> p a c f", a=2, c=NCH) for w in wi_flat]
        wiB = [w[:].rearrange("p (c n) -> p c n", c=NCH) for w in wi_flat]
        NWO = 2
        wo = [B.sb(stack, "wo%d" % i, [128, 4, 1024], BF16) for i in range(NWO)]
        sq = [B.sb(stack, "sq%d" % i, [128, TG], BF16) for i in range(2)]
        sg = [B.sb(stack, "sg%d" % i, [128, TG], F32) for i in range(2)]
        tmp = [B.sb(stack, "tmp%d" % i, [128, TG], F32) for i in range(2)]
        rstd = B.sb(stack, "rstd", [128, TG], F32)
        xt16 = B.sb(stack, "xt16", [128, 2, 512], BF16)
        sT16 = B.sb(stack, "sT16", [128, 2, 512], BF16)
        gsb = B.sb(stack, "gsb", [128, depth * 4 * NCH], F32)
        ones = B.sb(stack, "ones", [128, 128], BF16)
        pb = [B.ps(stack, "pb%d" % i, [128, TG], F32) for i in range(8)]

        P.dma("sp", lambda e: e.dma_start(out=gsb[:], in_=gains[:, :]), [], ["gsb"], key="c0")
        P.dma("sp", lambda e: e.dma_start(out=ones[:], in_=ones_in[:, :]), [], ["ones"], key="c1")

        def gain(layer, k, c):
            i = (layer * 4 + k) * NCH + c
            return gsb[:, i:i + 1]

        cnt = {"wi": 0, "wo": 0, "sq": 0, "sg": 0, "tmp": 0, "pbk": 0}

        def hk(a, b):
            return [("hid", j) for j in range(a, b)]

        def pk(n, q=None):
            if q is None:
                return [("pb", n, k) for k in range(4)]
            return [("pb", n, q)]

        def rot(name, n):
            i = cnt[name] % n
            cnt[name] += 1
            return i

        def rms_stats(src, src_key, pbank):
            for c in range(NCH):
                i = rot("sq", 2)
                _act(P, sq[i][:], src[:, c, :], AF.Square, [(src_key, c)], [("sq", i)])
                P.op("pe", lambda e, i=i, c=c: e.matmul(pb[pbank][:], ones[:], sq[i][:],
                                                        start=(c == 0), stop=(c == NCH - 1)),
                     [("sq", i), "ones"], pk(pbank))
            P.op("dve", lambda e: e.tensor_scalar(out=rstd[:], in0=pb[pbank][:], scalar1=1.0 / D_MODEL,
                                                  scalar2=EPS, op0=ALU.mult, op1=ALU.add),
                 pk(pbank), ["rstd"])
            _act(P, rstd[:], rstd[:], AF.Sqrt, ["rstd"], ["rstd"])
            P.op("dve", lambda e: e.reciprocal(out=rstd[:], in_=rstd[:]), ["rstd"], ["rstd"])

        def load_h(src_ap, g):
            v = src_ap.rearrange("(c p) t -> p c t", p=128)
            P.dma("sp", lambda e: e.dma_start(out=hA[:], in_=v[:, :, g * TG:(g + 1) * TG]),
                  [("dram_h", g)], [("hA", c) for c in range(NCH)], key="hA")

        def store_h(dst_ap, g):
            v = dst_ap.rearrange("(c p) t -> p c t", p=128)
            P.dma("sp", lambda e: e.dma_start(out=v[:, :, g * TG:(g + 1) * TG], in_=hA[:]),
                  [("hA", c) for c in range(NCH)], [("dram_h", g)], key="hst")

        def prenorm(layer, k):
            rms_stats(hA, "hA", 0)
            for c in range(NCH):
                P.op("dve", lambda e, c=c: e.scalar_tensor_tensor(
                    out=xn[:, c, :], in0=hA[:, c, :], scalar=gain(layer, k, c), in1=rstd[:],
                    op0=ALU.mult, op1=ALU.mult), [("hA", c), "rstd", "gsb"], [("xn", c)])

        def post_residual(layer, k):
            rms_stats(yS, "yS", 0)
            for c in range(NCH):
                i = rot("tmp", 2)
                P.op("dve", lambda e, c=c, i=i: e.scalar_tensor_tensor(
                    out=tmp[i][:], in0=yS[:, c, :], scalar=gain(layer, k, c), in1=rstd[:],
                    op0=ALU.mult, op1=ALU.mult), [("yS", c), "rstd", "gsb"], [("tmp", i)])
                P.op("dve", lambda e, c=c, i=i: e.tensor_tensor(
                    out=hA[:, c, :], in0=hA[:, c, :], in1=tmp[i][:], op=ALU.add),
                    [("tmp", i), ("hA", c)], [("hA", c)])

        def dense_out(act_buf, act_key, kc, w_ap):
            wv = w_ap.rearrange("(j p) n -> p j n", p=128)
            for half in range(2):
                for j0 in range(0, kc, 4):
                    nj = min(4, kc - j0)
                    i = rot("wo", NWO)
                    P.dma("pool", lambda e, i=i, j0=j0, nj=nj, half=half: e.dma_start(
                        out=wo[i][:, 0:nj, :], in_=wv[:, j0:j0 + nj, half * 1024:(half + 1) * 1024]),
                        [], [("wo", i)], key="wo%d" % i)
                    for jj in range(nj):
                        j = j0 + jj
                        for dc in range(8):
                            P.op("pe", lambda e, i=i, jj=jj, j=j, dc=dc: e.matmul(
                                pb[dc][:], wo[i][:, jj, dc * 128:(dc + 1) * 128], act_buf[:, j, :],
                                start=(j == 0), stop=(j == kc - 1)),
                                [("wo", i), (act_key, j)], pk(dc))
                for dc in range(8):
                    c = half * 8 + dc
                    P.op("act", lambda e, c=c, dc=dc: e.copy(out=yS[:, c, :], in_=pb[dc][:]),
                         pk(dc), [("yS", c)])

        def ffn(layer):
            nf = D_FF // 128
            wv = w_ffn_in[layer].rearrange("(c p) n -> p c n", p=128)
            for j in range(nf):
                i = rot("wi", NWI)
                P.dma("pool", lambda e, i=i, j=j: [
                    e.dma_start(out=wi[i][:, 0, :, :], in_=wv[:, :, j * 128:(j + 1) * 128]),
                    e.dma_start(out=wi[i][:, 1, :, :], in_=wv[:, :, D_FF + j * 128:D_FF + (j + 1) * 128])],
                    [], [("wi", i)], key="wi%d" % i, ndma=2)
                bg = (2 * j) % 4 + 4
                bu = bg + 1
                for c in range(NCH):
                    P.op("pe", lambda e, i=i, c=c, bg=bg: e.matmul(
                        pb[bg][:], wi[i][:, 0, c, :], xn[:, c, :], start=(c == 0), stop=(c == NCH - 1)),
                        [("wi", i), ("xn", c)], pk(bg))
                for c in range(NCH):
                    P.op("pe", lambda e, i=i, c=c, bu=bu: e.matmul(
                        pb[bu][:], wi[i][:, 1, c, :], xn[:, c, :], start=(c == 0), stop=(c == NCH - 1)),
                        [("wi", i), ("xn", c)], pk(bu))
                si = rot("sg", 2)
                _act(P, sg[si][:], pb[bg][:], AF.Silu, pk(bg), [("sg", si)])
                P.op("dve", lambda e, si=si, j=j, bu=bu: e.tensor_tensor(
                    out=hid[:, j, :], in0=sg[si][:], in1=pb[bu][:], op=ALU.mult),
                    [("sg", si)] + pk(bu), [("hid", j)])
            dense_out(hid, "hid", nf, w_ffn_out[layer])


        if n_even:
            w_in = B.dram_in("w_in", [n_even, D_MODEL, IN_WIDTH])
            wgu_in = B.dram_in("wgu", [n_even, 17, 512])
            glag_in = B.dram_in("gla_gain", [128, n_even * 8])
            w_out = B.dram_in("w_out", [n_even, D_MODEL, D_MODEL])
            cm_in = B.dram_in("cm128", [128, 4, 128], BF16)
            sbm_in = B.dram_in("sbmask", [128, 4, TG], BF16)
            qT_d = B.dram_tmp("qT_d", [8, 128, L], BF16)
            kT_d = B.dram_tmp("kT_d", [8, 128, L], BF16)
            v_d = B.dram_tmp("v_d", [L, 1024], BF16)
            mixT_d = B.dram_tmp("mixT_d", [16, 128, L], BF16)

            cm = B.sb(stack, "cm", [128, 4, 128], BF16)
            sbm = B.sb(stack, "sbm", [128, 4, TG], BF16)
            la = B.sb(stack, "la", [128, 4, 512], BF16)
            sbv = B.sb(stack, "sbv", [128, 4, 256], BF16)
            lrT = B.sb(stack, "lrT", [17, TG], BF16)
            wgu = B.sb(stack, "wgu", [17, 512], BF16)
            S32 = B.sb(stack, "S32", [64, 8, 128], F32)
            Sb = B.sb(stack, "Sb", [64, 8, 128], BF16)
            dec = B.sb(stack, "dec", [64, 8], F32)
            glag = B.sb(stack, "glag", [128, n_even * 8], F32)
            khat = sT16[:, 1, :]
            qtl = B.sb(stack, "qtl", [64, 128], BF16)
            ktl = B.sb(stack, "ktl", [64, 128], BF16)
            PT = B.sb(stack, "PT", [128, 128], BF16)
            e1 = B.sb(stack, "e1", [64, 128], F32)
            e2 = B.sb(stack, "e2", [64, 128], F32)
            at_e = sg[0]
            at_sp = sg[1]
            at_er = tmp[0]
            at_S32 = tmp[1]
            at_sp16 = xt16[:, 0, :]
            at_S16 = xt16[:, 1, :]
            at_A16 = sT16[:, 0, :]

            P.dma("sp", lambda e: e.dma_start(out=cm[:], in_=cm_in[:, :, :]), [], ["cm"], key="c2")
            P.dma("sp", lambda e: e.dma_start(out=sbm[:], in_=sbm_in[:, :, :]), [], ["sbm"], key="c3")
            P.dma("sp", lambda e: e.dma_start(out=glag[:], in_=glag_in[:, :]), [], ["glag"], key="c4")
            P.op("dve", lambda e: e.memset(lrT[:], 1.0), [], ["lrT"])


            def tmv(tt):
                return hid[:, 32 + 3 * tt:35 + 3 * tt, :].rearrange("p a b -> p (a b)")

            def even_m1(li, ei, g, src):
                load_h(src, g)
                prenorm(li, 0)
                wv = w_in[ei].rearrange("(c p) n -> p c n", p=128)
                gs = slice(g * TG, (g + 1) * TG)

                def fm_proj(col0, M, evac):
                    i = rot("wi", NWI)
                    P.dma("pool", lambda e: e.dma_start(out=wi[i][:, 0, :, 0:M], in_=wv[:, :, col0:col0 + M]),
                          [], [("wi", i)], key="wi%d" % i)
                    bank = 4 + rot("pbk", 4)
                    for c in range(NCH):
                        P.op("pe", lambda e, c=c: e.matmul(pb[bank][0:M, :], wi[i][:, 0, c, 0:M], xn[:, c, :],
                                                           start=(c == 0), stop=(c == NCH - 1)),
                             [("wi", i), ("xn", c)], pk(bank))
                    evac(bank)

                for h in range(8):
                    fm_proj(h * 128, 128, lambda bank, h=h: P.op(
                        "act", lambda e: e.mul(out=hid[:, h, :], in_=pb[bank][:], mul=128.0 ** -0.5),
                        pk(bank), hk(h, h + 1)))
                qv = qT_d.rearrange("h p t -> p h t")
                P.dma("sp", lambda e: e.dma_start(out=qv[:, :, gs], in_=hid[:, 0:8, :]), hk(0, 8), ["qT_d"], key="st0")
                for h in range(8):
                    fm_proj(1024 + h * 128, 128, lambda bank, h=h: P.op(
                        "act", lambda e: e.copy(out=hid[:, 8 + h, :], in_=pb[bank][:]),
                        pk(bank), hk(8 + h, 9 + h)))
                kv = kT_d.rearrange("h p t -> p h t")
                P.dma("sp", lambda e: e.dma_start(out=kv[:, :, gs], in_=hid[:, 8:16, :]), hk(8, 16), ["kT_d"], key="st1")
                for h in range(8):
                    fm_proj(3072 + h * 64, 64, lambda bank, h=h: P.op(
                        "act", lambda e: e.mul(out=hid[0:64, 16 + h, :], in_=pb[bank][0:64, :], mul=64.0 ** -0.5),
                        pk(bank), hk(16 + h, 17 + h)))
                    fm_proj(3584 + h * 64, 64, lambda bank, h=h: P.op(
                        "act", lambda e: e.copy(out=hid[0:64, 24 + h, :], in_=pb[bank][0:64, :]),
                        pk(bank), hk(24 + h, 25 + h)))
                for h in range(8):
                    fm_proj(5120 + h * 128, 128, lambda bank, h=h: _act(
                        P, yS[:, h, :], pb[bank][:], AF.Silu, pk(bank), [("yS", h)]))
                fm_proj(6144, 16, lambda bank: P.op(
                    "act", lambda e: e.copy(out=lrT[0:16, :], in_=pb[bank][0:16, :]), pk(bank), ["lrT"]))

                for blk in range(10):
                    if blk < 4:
                        col0 = 2048 + 256 * blk
                    elif blk < 8:
                        col0 = 4096 + 256 * (blk - 4)
                    else:
                        col0 = 3584 + 256 * (blk - 8)
                    i = rot("wi", NWI)
                    P.dma("pool", lambda e, i=i, col0=col0: e.dma_start(out=wiB[i][:, :, :], in_=wv[:, :, col0:col0 + 256]),
                          [], [("wi", i)], key="wi%d" % i)
                    for tt in range(4):
                        bank = 4 + rot("pbk", 4)
                        for c in range(NCH):
                            P.op("pe", lambda e, c=c, i=i, tt=tt, bank=bank: e.matmul(
                                pb[bank][:, 0:256], xn[:, c, tt * 128:(tt + 1) * 128], wiB[i][:, c, :],
                                start=(c == 0), stop=(c == NCH - 1)),
                                [("wi", i), ("xn", c)], pk(bank))
                        if blk < 4:
                            P.op("act", lambda e, tt=tt, bank=bank: e.copy(out=sbv[:, tt, :], in_=pb[bank][:, 0:256]),
                                 pk(bank), ["sbv"])
                        else:
                            o0 = 256 * (blk - 4)
                            P.op("act", lambda e, tt=tt, bank=bank, o0=o0: e.copy(
                                out=tmv(tt)[:, o0:o0 + 256], in_=pb[bank][:, 0:256]),
                                pk(bank), hk(32 + 3 * tt, 35 + 3 * tt))
                    if blk < 4:
                        dv = v_d[g * TG:(g + 1) * TG, blk * 256:(blk + 1) * 256].rearrange("(tt p) n -> p tt n", p=128)
                        P.dma("sp", lambda e, dv=dv: e.dma_start(out=dv, in_=sbv[:]), ["sbv"], ["v_d"], key="st2")

                for tt in range(4):
                    ts_ = slice(tt * 128, (tt + 1) * 128)
                    P.op("pe", lambda e, ts_=ts_: e.matmul(pb[1][:, :], lrT[0:17, ts_], wgu[0:17, :], start=True, stop=True),
                         ["lrT", "wgu"], pk(1))
                    si = rot("sg", 2)
                    _act(P, sg[si][:], pb[1][:], AF.Exp, pk(1), [("sg", si)], scale=-1.0)
                    _act(P, la[:, tt, :], sg[si][:], AF.Ln, [("sg", si)], [("la", tt)], bias=1.0)
                    P.op("pe", lambda e, tt=tt: e.matmul(pb[2][:, :], cm[:, 2, :], la[:, tt, :], start=True, stop=True),
                         ["cm", ("la", tt)], pk(2))
                    sj = rot("sg", 2)
                    _act(P, sg[sj][:], pb[2][:], AF.Exp, pk(2), [("sg", sj)])
                    P.op("dve", lambda e, sj=sj, tt=tt: e.tensor_tensor(
                        out=khat, in0=sg[sj][:], in1=tmv(tt)[:, 1024:1536], op=ALU.mult),
                        [("sg", sj)] + hk(32 + 3 * tt, 35 + 3 * tt), [("sT16", 1)])
                    for h in range(8):
                        ob = 5 + h // 4
                        oq = h % 4
                        osl = slice(oq * 128, (oq + 1) * 128)
                        vh = tmv(tt)[:, h * 128:(h + 1) * 128]
                        hkv = hk(32 + 3 * tt, 35 + 3 * tt)
                        P.op("pe", lambda e, h=h, tt=tt: e.matmul(pb[3][0:64, 0:128], la[:, tt, h * 64:(h + 1) * 64],
                                                                  cm[:, 1, :], start=True, stop=True),
                             ["cm", ("la", tt)], pk(3, 0))
                        _act(P, e1[:], pb[3][0:64, 0:128], AF.Exp, pk(3, 0), ["e1"])
                        _act(P, e2[:], pb[3][0:64, 0:128], AF.Exp, pk(3, 0), ["e2"], scale=-1.0)
                        _act(P, dec[:, h:h + 1], pb[3][0:64, 127:128], AF.Exp, pk(3, 0), [("dec", h)])
                        P.op("dve", lambda e, h=h, ts_=ts_: e.tensor_tensor(
                            out=qtl[:], in0=hid[0:64, 16 + h, ts_], in1=e1[:], op=ALU.mult),
                            ["e1"] + hk(16 + h, 17 + h), ["qtl"])
                        P.op("dve", lambda e, h=h, ts_=ts_: e.tensor_tensor(
                            out=ktl[:], in0=hid[0:64, 24 + h, ts_], in1=e2[:], op=ALU.mult),
                            ["e2"] + hk(24 + h, 25 + h), ["ktl"])
                        P.op("pe", lambda e: e.matmul(pb[4][:, 0:128], ktl[:], qtl[:], start=True, stop=True),
                             ["ktl", "qtl"], pk(4, 0))
                        P.op("dve", lambda e: e.tensor_tensor(out=PT[:], in0=pb[4][:, 0:128], in1=cm[:, 3, :], op=ALU.mult),
                             pk(4, 0) + ["cm"], ["PT"])
                        P.op("pe", lambda e, ob=ob, osl=osl, vh=vh: e.matmul(pb[ob][:, osl], vh, PT[:], start=True, stop=False),
                             ["PT"] + hkv, pk(ob, oq))
                        P.op("pe", lambda e, ob=ob, osl=osl, h=h: e.matmul(pb[ob][:, osl], Sb[:, h, :], qtl[:], start=False, stop=True),
                             [("Sb", h), "qtl"], pk(ob, oq))
                        P.op("pe", lambda e, h=h, vh=vh: e.matmul(pb[7][0:64, 0:128], khat[:, h * 64:(h + 1) * 64], vh,
                                                                  start=True, stop=True),
                             [("sT16", 1)] + hkv, pk(7, 0))
                        P.op("dve", lambda e, h=h: e.scalar_tensor_tensor(
                            out=S32[:, h, :], in0=S32[:, h, :], scalar=dec[:, h:h + 1], in1=pb[7][0:64, 0:128],
                            op0=ALU.mult, op1=ALU.add), [("S32", h), ("dec", h)] + pk(7, 0), [("S32", h)])
                        P.op("act", lambda e, h=h: e.copy(out=Sb[:, h, :], in_=S32[:, h, :]), [("S32", h)], [("Sb", h)])
                    for hb in range(2):
                        ob = 5 + hb
                        i = rot("sq", 2)
                        _act(P, sq[i][:], pb[ob][:], AF.Square, pk(ob), [("sq", i)])
                        P.op("pe", lambda e, i=i: e.matmul(pb[0][:], ones[:], sq[i][:], start=True, stop=True),
                             [("sq", i), "ones"], pk(0))
                        P.op("dve", lambda e: e.tensor_scalar(out=rstd[:], in0=pb[0][:], scalar1=1.0 / 128, scalar2=EPS,
                                                              op0=ALU.mult, op1=ALU.add), pk(0), ["rstd"])
                        _act(P, rstd[:], rstd[:], AF.Sqrt, ["rstd"], ["rstd"])
                        P.op("dve", lambda e: e.reciprocal(out=rstd[:], in_=rstd[:]), ["rstd"], ["rstd"])
                        ti = rot("tmp", 2)
                        P.op("dve", lambda e, ti=ti, ob=ob: e.tensor_tensor(out=tmp[ti][:], in0=pb[ob][:], in1=rstd[:], op=ALU.mult),
                             pk(ob) + ["rstd"], [("tmp", ti)])
                        for oq in range(4):
                            h = hb * 4 + oq
                            P.op("dve", lambda e, ti=ti, oq=oq, h=h, ts_=ts_: e.scalar_tensor_tensor(
                                out=hid[:, h, ts_], in0=tmp[ti][:, oq * 128:(oq + 1) * 128],
                                scalar=glag[:, ei * 8 + h:ei * 8 + h + 1], in1=yS[:, h, ts_],
                                op0=ALU.mult, op1=ALU.mult),
                                [("tmp", ti), "glag", ("yS", h)], hk(h, h + 1))
                mv = mixT_d[8:16].rearrange("h p t -> p h t")
                P.dma("sp", lambda e: e.dma_start(out=mv[:, :, gs], in_=hid[:, 0:8, :]), hk(0, 8), ["mixT_d"], key="st3")

            def sb_attention():
                nb = L // 128
                KT = hid[:, 0:8, :].rearrange("p a b -> p (a b)")
                QT = hid[:, 8:16, :].rearrange("p a b -> p (a b)")
                VV = hid[:, 16:24, :].rearrange("p a (b e) -> p (a b) e", e=128)
                OT = hid[:, 24:32, :].rearrange("p a b -> p (a b)")
                streams = [
                    dict(e=sg[0][:], ke=[("sg", 0)], sp=sg[1][:], ksp=[("sg", 1)], er=tmp[0][:], ker=[("tmp", 0)],
                         S32=tmp[1][:], kS32=[("tmp", 1)], sp16=xt16[:, 0, :], ksp16=[("xt16", 0)],
                         S16=xt16[:, 1, :], kS16=[("xt16", 1)], A16=sT16[:, 0, :], kA16=[("sT16", 0)],
                         bz=4, br=4, bo=0)]
                for si_ in range(3):
                    y0, h0 = 4 * si_, 32 + 3 * si_
                    streams.append(dict(
                        e=yS[:, y0, :], ke=[("yS", y0)], sp=yS[:, y0 + 1, :], ksp=[("yS", y0 + 1)],
                        er=yS[:, y0 + 2, :], ker=[("yS", y0 + 2)], S32=yS[:, y0 + 3, :], kS32=[("yS", y0 + 3)],
                        sp16=hid[:, h0, :], ksp16=hk(h0, h0 + 1), S16=hid[:, h0 + 1, :], kS16=hk(h0 + 1, h0 + 2),
                        A16=hid[:, h0 + 2, :], kA16=hk(h0 + 2, h0 + 3),
                        bz=5 + si_, br=5 + si_, bo=1 + si_))

                def step(st, G, kb, first):
                    di = kb - 4 * G
                    bz, br, bo = st["bz"], st["br"], st["bo"]
                    P.op("pe", lambda e: e.matmul(pb[bz][:], KT[:, kb * 128:(kb + 1) * 128],
                                                  QT[:, G * TG:(G + 1) * TG], start=True, stop=True),
                         hk(0, 16), pk(bz))
                    yield
                    _act(P, st["e"], pb[bz][:], AF.Exp, pk(bz), st["ke"])
                    yield
                    if di >= 0:
                        _act(P, st["sp"], st["e"], AF.Ln, st["ke"], st["ksp"], bias=1.0)
                        P.op("dve", lambda e: e.tensor_tensor(out=st["sp16"], in0=st["sp"], in1=sbm[:, di, :], op=ALU.mult),
                             st["ksp"] + ["sbm"], st["ksp16"])
                    else:
                        _act(P, st["sp16"], st["e"], AF.Ln, st["ke"], st["ksp16"], bias=1.0)
                    yield
                    P.op("pe", lambda e: e.matmul(pb[br][:], cm[:, 0, :], st["sp16"], start=True, stop=first),
                         ["cm"] + st["ksp16"], pk(br))
                    if not first:
                        P.op("pe", lambda e: e.matmul(pb[br][:], ones[:], st["S16"], start=False, stop=True),
                             ["ones"] + st["kS16"], pk(br))
                    yield
                    _act(P, st["er"], pb[br][:], AF.Exp, pk(br), st["ker"], scale=-1.0)
                    yield
                    P.op("dve", lambda e: e.tensor_tensor(out=st["A16"], in0=st["e"], in1=st["er"], op=ALU.mult),
                         st["ke"] + st["ker"], st["kA16"])
                    if di >= 0:
                        P.op("dve", lambda e: e.tensor_tensor(out=st["A16"], in0=st["A16"], in1=sbm[:, di, :], op=ALU.mult),
                             st["kA16"] + ["sbm"], st["kA16"])
                    yield
                    P.op("pe", lambda e: e.matmul(pb[bo][:], VV[:, kb, :], st["A16"], start=first, stop=(kb == 0)),
                         hk(16, 24) + st["kA16"], pk(bo))
                    if kb > 0:
                        if first:
                            P.op("pool", lambda e: e.tensor_copy(out=st["S32"], in_=st["sp16"]), st["ksp16"], st["kS32"])
                        else:
                            P.op("pool", lambda e: e.tensor_tensor(out=st["S32"], in0=st["S32"], in1=st["sp16"], op=ALU.add),
                                 st["ksp16"] + st["kS32"], st["kS32"])
                        P.op("pool", lambda e: e.tensor_copy(out=st["S16"], in_=st["S32"]), st["kS32"], st["kS16"])
                    if kb == 0:
                        P.op("act", lambda e: e.copy(out=OT[:, G * TG:(G + 1) * TG], in_=pb[bo][:]),
                             pk(bo), hk(24 + G, 25 + G))
                    yield

                def run_lockstep(gens):
                    gens = list(gens)
                    while gens:
                        nxt = []
                        for g_ in gens:
                            try:
                                next(g_)
                                nxt.append(g_)
                            except StopIteration:
                                pass
                        gens = nxt

                for h in range(8):
                    P.dma("sp", lambda e, h=h: e.dma_start(out=KT[:, 0:L], in_=kT_d[h]), ["kT_d"], hk(0, 8), key="ld0")
                    P.dma("sp", lambda e, h=h: e.dma_start(out=QT[:, 0:L], in_=qT_d[h]), ["qT_d"], hk(8, 16), key="ld1")
                    P.dma("sp", lambda e, h=h: e.dma_start(
                        out=VV[:, 0:nb, :], in_=v_d[:, h * 128:(h + 1) * 128].rearrange("(kb p) e -> p kb e", p=128)),
                        ["v_d"], hk(16, 24), key="ld2")
                    nG = L // TG
                    for G0 in range(0, nG, 4):
                        Gs = list(range(G0, min(G0 + 4, nG)))
                        for kb in range(4 * Gs[-1] + 3, -1, -1):
                            act_ = [(si_, G) for si_, G in enumerate(Gs) if 4 * G + 3 >= kb]
                            run_lockstep([step(streams[si_], G, kb, kb == 4 * G + 3) for si_, G in act_])
                    P.dma("sp", lambda e, h=h: e.dma_start(out=mixT_d[h], in_=OT[:, 0:L]), hk(24, 32), ["mixT_d"], key="st4")

            def even_m3(li, ei, g, src):
                mv = mixT_d.rearrange("c p t -> p c t")
                P.dma("sp", lambda e: e.dma_start(out=xn[:], in_=mv[:, :, g * TG:(g + 1) * TG]),
                      ["mixT_d"], [("xn", c) for c in range(NCH)], key="ld3")
                load_h(src, g)
                dense_out(xn, "xn", NCH, w_out[ei])
                post_residual(li, 1)
                store_h(hT, g)

            def even_mixer(li, ei, src):
                P.dma("pool", lambda e: e.dma_start(out=wgu[:], in_=wgu_in[ei]), [], ["wgu"], key="c5")
                P.op("dve", lambda e: e.memset(S32[:], 0.0), [], [("S32", h) for h in range(8)])
                P.op("dve", lambda e: e.memset(Sb[:], 0.0), [], [("Sb", h) for h in range(8)])
                for g in range(n_tg):
                    even_m1(li, ei, g, src)
                sb_attention()
                for g in range(n_tg):
                    even_m3(li, ei, g, src)


        if n_odd:
            PI = float(np.pi)
            s5_lam = B.dram_in("s5_lam", [n_odd, 3, 8192])
            s5_lamT = B.dram_in("s5_lamT", [n_odd, 128, 3, 64])
            s5_lamB = B.dram_in("s5_lamB", [n_odd, 128, 3, NCH, 64])
            s5_bT = B.dram_in("s5_bT", [n_odd, 128, 2, NCH, 64])
            s5_cw = B.dram_in("s5_cw", [n_odd, NCH, 128, 8, 128])
            s5_dT = B.dram_in("s5_dT", [128, n_odd * NCH])
            w_glu = B.dram_in("w_glu", [n_odd, D_MODEL, 2 * D_MODEL])
            jc_in = B.dram_in("jconst", [128, 138])
            tle_in = B.dram_in("trile", [128, 128], BF16)
            xnT_d = B.dram_tmp("xnT_d", [NCH, 128, L], BF16)
            geT_d = B.dram_tmp("geT_d", [NCH, 128, L], BF16)

            jc = B.sb(stack, "jc", [128, 138], F32)
            tle = B.sb(stack, "tle", [128, 128], BF16)
            lamT = B.sb(stack, "lamT", [128, 3, 64], F32)
            arT = B.sb(stack, "arT", [128, 64], F32)
            aiT = B.sb(stack, "aiT", [128, 64], F32)
            lbr = B.sb(stack, "lbr", [128, 64], F32)
            lbi = B.sb(stack, "lbi", [128, 64], F32)
            dsk = B.sb(stack, "dsk", [128, n_odd * NCH], F32)
            car = B.sb(stack, "car", [128, 2, 4], F32)
            lam128 = B.sb(stack, "lam128", [128, 2, 4], F32)
            zt = [B.sb(stack, "zt%d" % i, [128, 64], F32) for i in range(10)]
            lamB = B.sb(stack, "lamB", [128, 3, 64], F32)
            bTs = B.sb(stack, "bTs", [128, 2, 64], F32)
            Bblk = B.sb(stack, "Bblk", [128, 2, 8, 64], BF16)
            Cw = B.sb(stack, "Cw", [128, 8, 128], BF16)
            k4 = [B.sb(stack, "k4_%d" % i, [128, 4], F32) for i in range(4)]
            ytmp = B.sb(stack, "ytmp", [128, 128], F32)
            Ptab = yS[:, 4:6, :]
            Qtab = yS[:, 6:8, :]
            lb3 = yS[:, 12:15, :]
            u1b, u2b = yS[:, 15, :], yS[:, 3, :]
            pmg = yS[:, 2, :]
            jrow = jc[:, 0:128]
            jcol = jc[:, 128:129]

            if not (_SKIP & 1):
                P.dma("sp", lambda e: e.dma_start(out=jc[:], in_=jc_in[:, :]), [], ["jc"], key="c6")
                P.dma("sp", lambda e: e.dma_start(out=tle[:], in_=tle_in[:, :]), [], ["tle"], key="c7")
                P.dma("sp", lambda e: e.dma_start(out=dsk[:], in_=s5_dT[:, :]), [], ["dsk"], key="c8")

            def dv(out, in0, in1, op, r, w):
                P.op("dve", lambda e: e.tensor_tensor(out=out, in0=in0, in1=in1, op=op), r, w)

            def ds(out, in0, s1, s2, op0, op1, r, w, eng="dve"):
                if s2 is None:
                    P.op(eng, lambda e: e.tensor_scalar(out=out, in0=in0, scalar1=s1, scalar2=None, op0=op0), r, w)
                else:
                    P.op(eng, lambda e: e.tensor_scalar(out=out, in0=in0, scalar1=s1, scalar2=s2, op0=op0, op1=op1), r, w)

            MAGIC = 12582912.0

            def red_sin(buf, tb, kb, kt):
                ds(tb, buf, 1.0 / (2 * PI), MAGIC, ALU.mult, ALU.add, kb, kt)
                ds(tb, tb, -MAGIC, None, ALU.add, None, kt, kt)
                P.op("dve", lambda e: e.scalar_tensor_tensor(out=buf, in0=tb, scalar=-2 * PI, in1=buf,
                                                             op0=ALU.mult, op1=ALU.add), kt + kb, kb)
                ds(buf, buf, -3.14159, 3.14159, ALU.max, ALU.min, kb, kb)
                _act(P, buf, buf, AF.Sin, kb, kb)

            def sincos(o1, o2, ang_in, scal, rk, wk1, wk2, tb=None, kt=None):
                if tb is None:
                    tb, kt = zt[9][:], [("zt", 9)]
                ds(o1, ang_in, scal, None, ALU.mult, None, rk, [wk1])
                ds(o2, o1, 1.5 * PI, None, ALU.add, None, [wk1], [wk2])
                ds(o1, o1, PI, None, ALU.add, None, [wk1], [wk1])
                red_sin(o1, tb, [wk1], kt)
                red_sin(o2, tb, [wk2], kt)

            def odd_mixer(li, oi, src):
                xv = xnT_d.rearrange("c p t -> p c t")
                for g in range(n_tg if not (_SKIP & 2) else 0):
                    load_h(src, g)
                    prenorm(li, 0)
                    P.dma("sp", lambda e, g=g: e.dma_start(out=xv[:, :, g * TG:(g + 1) * TG], in_=xn[:]),
                          [("xn", c) for c in range(NCH)], ["xnT_d"], key="st5")
                def trivial_o3():
                    for g in range(n_tg if not (_SKIP & 4) else 0):
                        load_h(src, g)
                        store_h(hT, g)
                if _STOP == 1:
                    return trivial_o3()
                P.dma("sp", lambda e: e.dma_start(out=lamT[:], in_=s5_lamT[oi]), [], ["lamT"], key="c9")
                _act(P, zt[0][:], lamT[:, 2, :], AF.Exp, ["lamT"], [("zt", 0)])
                dv(arT[:], zt[0][:], lamT[:, 0, :], ALU.mult, [("zt", 0), "lamT"], ["arT"])
                dv(aiT[:], zt[0][:], lamT[:, 1, :], ALU.mult, [("zt", 0), "lamT"], ["aiT"])
                _act(P, zt[1][:], arT[:], AF.Exp, ["arT"], [("zt", 1)])
                sincos(zt[2][:], zt[3][:], aiT[:], 1.0, ["aiT"], ("zt", 2), ("zt", 3))
                P.op("dve", lambda e: e.scalar_tensor_tensor(out=lbr[:], in0=zt[1][:], scalar=-1.0, in1=zt[3][:],
                                                             op0=ALU.mult, op1=ALU.mult), [("zt", 1), ("zt", 3)], ["lbr"])
                P.op("dve", lambda e: e.scalar_tensor_tensor(out=lbi[:], in0=zt[1][:], scalar=-1.0, in1=zt[2][:],
                                                             op0=ALU.mult, op1=ALU.mult), [("zt", 1), ("zt", 2)], ["lbi"])
                if _STOP == 2:
                    return trivial_o3()
                UC = hid[:, 0:8, :].rearrange("p a b -> p (a b)")
                GS = hid[:, 8:16, :].rearrange("p a b -> p (a b)")
                yk = lambda a, b: [("yS", j) for j in range(a, b)]
                for c in range(NCH):
                    c4 = slice(c * 4, (c + 1) * 4)
                    P.dma("sp", lambda e, c=c: e.dma_start(out=lamB[:], in_=s5_lamB[oi][:, :, c, :]), [], ["lamB"], key="ld5")
                    P.dma("sp", lambda e, c=c: e.dma_start(out=bTs[:], in_=s5_bT[oi][:, :, c, :]), [], ["bTs"], key="ld6")
                    P.dma("pool", lambda e, c=c: e.dma_start(out=Cw[:], in_=s5_cw[oi][c]), [], ["Cw"], key="ld7")
                    for gp_ in range(4):
                        P.op("act", lambda e, gp_=gp_: e.mul(out=Cw[:, gp_ * 2 + 1, :], in_=Cw[:, gp_ * 2 + 1, :], mul=-1.0),
                             ["Cw"], ["Cw"])
                    P.dma("sp", lambda e, c=c: e.dma_start(out=UC[:, 0:L], in_=xnT_d[c]), ["xnT_d"], hk(0, 8), key="ld8")
                    z = lambda i: zt[i][:]
                    zk = lambda i: ("zt", i)
                    _act(P, z(0), lamB[:, 2, :], AF.Exp, ["lamB"], [zk(0)])
                    dv(z(1), z(0), lamB[:, 0, :], ALU.mult, [zk(0), "lamB"], [zk(1)])
                    dv(z(2), z(0), lamB[:, 1, :], ALU.mult, [zk(0), "lamB"], [zk(2)])
                    _act(P, z(1), z(1), AF.Exp, [zk(1)], [zk(1)])
                    sincos(z(3), z(4), z(2), 1.0, [zk(2)], zk(3), zk(4))
                    P.op("dve", lambda e: e.scalar_tensor_tensor(out=z(5), in0=z(1), scalar=-1.0, in1=z(4),
                                                                 op0=ALU.mult, op1=ALU.mult), [zk(1), zk(4)], [zk(5)])
                    P.op("dve", lambda e: e.scalar_tensor_tensor(out=z(6), in0=z(1), scalar=-1.0, in1=z(3),
                                                                 op0=ALU.mult, op1=ALU.mult), [zk(1), zk(3)], [zk(6)])
                    ds(z(5), z(5), -1.0, None, ALU.add, None, [zk(5)], [zk(5)])
                    dv(z(0), lamB[:, 0, :], lamB[:, 0, :], ALU.mult, ["lamB"], [zk(0)])
                    dv(z(1), lamB[:, 1, :], lamB[:, 1, :], ALU.mult, ["lamB"], [zk(1)])
                    dv(z(0), z(0), z(1), ALU.add, [zk(0), zk(1)], [zk(0)])
                    P.op("dve", lambda e: e.reciprocal(out=z(0), in_=z(0)), [zk(0)], [zk(0)])
                    dv(z(1), z(5), lamB[:, 0, :], ALU.mult, [zk(5), "lamB"], [zk(1)])
                    dv(z(2), z(6), lamB[:, 1, :], ALU.mult, [zk(6), "lamB"], [zk(2)])
                    dv(z(1), z(1), z(2), ALU.add, [zk(1), zk(2)], [zk(1)])
                    dv(z(7), z(1), z(0), ALU.mult, [zk(1), zk(0)], [zk(7)])
                    dv(z(1), z(6), lamB[:, 0, :], ALU.mult, [zk(6), "lamB"], [zk(1)])
                    dv(z(2), z(5), lamB[:, 1, :], ALU.mult, [zk(5), "lamB"], [zk(2)])
                    dv(z(1), z(1), z(2), ALU.subtract, [zk(1), zk(2)], [zk(1)])
                    dv(z(8), z(1), z(0), ALU.mult, [zk(1), zk(0)], [zk(8)])
                    dv(z(1), z(7), bTs[:, 0, :], ALU.mult, [zk(7), "bTs"], [zk(1)])
                    dv(z(2), z(8), bTs[:, 1, :], ALU.mult, [zk(8), "bTs"], [zk(2)])
                    dv(z(3), z(1), z(2), ALU.subtract, [zk(1), zk(2)], [zk(3)])
                    dv(z(1), z(7), bTs[:, 1, :], ALU.mult, [zk(7), "bTs"], [zk(1)])
                    dv(z(2), z(8), bTs[:, 0, :], ALU.mult, [zk(8), "bTs"], [zk(2)])
                    dv(z(4), z(1), z(2), ALU.add, [zk(1), zk(2)], [zk(4)])
                    for ri in range(2):
                        for gg in range(8):
                            ds(Bblk[:, ri, gg, :], z(3 + ri), jc[:, 130 + gg:131 + gg], None, ALU.mult, None,
                               [zk(3 + ri), "jc"], ["Bblk"])
                    if _STOP == 3:
                        continue
                    lv = s5_lam[oi].rearrange("k (c n) -> k c n", c=NCH)
                    for k3 in range(3):
                        if _DBG == 2:
                            P.dma("sp", lambda e, k3=k3, c=c: e.dma_start(
                                out=lb3[0:1, k3, :], in_=lv[k3, c:c + 1, :]),
                                [], yk(12 + k3, 13 + k3), key="ld9_%d" % k3)
                        else:
                            P.dma("pool" if _DBG == 1 else "sp", lambda e, k3=k3, c=c: e.dma_start(
                                out=lb3[:, k3, :], in_=lv[k3, c:c + 1, :].partition_broadcast(128)),
                                [], yk(12 + k3, 13 + k3), key="ld9_%d" % k3)
                    _act(P, lb3[:, 2, :], lb3[:, 2, :], AF.Exp, yk(14, 15), yk(14, 15))
                    dv(lb3[:, 0, :], lb3[:, 0, :], lb3[:, 2, :], ALU.mult, yk(12, 13) + yk(14, 15), yk(12, 13))
                    dv(lb3[:, 1, :], lb3[:, 1, :], lb3[:, 2, :], ALU.mult, yk(13, 14) + yk(14, 15), yk(13, 14))
                    ds(pmg, lb3[:, 0, :], jcol, None, ALU.mult, None, yk(12, 13) + ["jc"], yk(2, 3))
                    _act(P, pmg, pmg, AF.Exp, yk(2, 3), yk(2, 3), scale=-1.0)
                    sincos(u1b, u2b, lb3[:, 1, :], jcol, yk(13, 14) + ["jc"], ("yS", 15), ("yS", 3), tb=sg[0][:], kt=[("sg", 0)])
                    P.op("dve", lambda e: e.scalar_tensor_tensor(out=Ptab[:, 0, :], in0=pmg, scalar=-1.0, in1=u2b,
                                                                 op0=ALU.mult, op1=ALU.mult), yk(2, 4), yk(4, 5))
                    dv(Ptab[:, 1, :], pmg, u1b, ALU.mult, yk(2, 3) + yk(15, 16), yk(5, 6))
                    for gp in range(4):
                        cg = c * 4 + gp
                        qs = slice(gp * 128, (gp + 1) * 128)
                        ds(pmg[:, qs], jrow, arT[:, cg:cg + 1], None, ALU.mult, None, ["jc", "arT"], yk(2, 3))
                        ds(u1b[:, qs], jrow, aiT[:, cg:cg + 1], None, ALU.mult, None, ["jc", "aiT"], yk(15, 16))
                    _act(P, pmg, pmg, AF.Exp, yk(2, 3), yk(2, 3))
                    ds(u2b, u1b, 1.5 * PI, None, ALU.add, None, yk(15, 16), yk(3, 4))
                    ds(u1b, u1b, PI, None, ALU.add, None, yk(15, 16), yk(15, 16))
                    red_sin(u1b, sg[0][:], yk(15, 16), [("sg", 0)])
                    red_sin(u2b, sg[0][:], yk(3, 4), [("sg", 0)])
                    P.op("dve", lambda e: e.scalar_tensor_tensor(out=Qtab[:, 0, :], in0=pmg, scalar=-1.0, in1=u2b,
                                                                 op0=ALU.mult, op1=ALU.mult), yk(2, 4), yk(6, 7))
                    P.op("dve", lambda e: e.scalar_tensor_tensor(out=Qtab[:, 1, :], in0=pmg, scalar=-1.0, in1=u1b,
                                                                 op0=ALU.mult, op1=ALU.mult), yk(2, 3) + yk(15, 16), yk(7, 8))
                    if _STOP == 4:
                        continue
                    q127r = Qtab[:, 0, :].rearrange("p (g j) -> p g j", j=128)[:, :, 127]
                    q127i = Qtab[:, 1, :].rearrange("p (g j) -> p g j", j=128)[:, :, 127]
                    dv(k4[0][:], lbr[:, c4], q127r, ALU.mult, ["lbr"] + yk(6, 7), [("k4", 0)])
                    dv(k4[1][:], lbi[:, c4], q127i, ALU.mult, ["lbi"] + yk(7, 8), [("k4", 1)])
                    dv(k4[2][:], lbr[:, c4], q127i, ALU.mult, ["lbr"] + yk(7, 8), [("k4", 2)])
                    dv(k4[3][:], lbi[:, c4], q127r, ALU.mult, ["lbi"] + yk(6, 7), [("k4", 3)])
                    dv(lam128[:, 0, :], k4[0][:], k4[1][:], ALU.subtract, [("k4", 0), ("k4", 1)], ["lam128"])
                    dv(lam128[:, 1, :], k4[2][:], k4[3][:], ALU.add, [("k4", 2), ("k4", 3)], ["lam128"])
                    P.op("dve", lambda e: e.memset(car[:], 0.0), [], ["car"])
                    def s5_front(k, c=c, c4=c4):
                        ks = slice(k * 128, (k + 1) * 128)
                        for ri in range(2):
                            P.op("pe", lambda e, ri=ri, ks=ks: e.matmul(
                                pb[1 + ri][:], UC[:, ks], Bblk[:, ri, :, :].rearrange("p a b -> p (a b)"),
                                start=True, stop=True), hk(0, 8) + ["Bblk"], pk(1 + ri))
                        t1, t2 = sg[0][:], sg[1][:]
                        dv(t1, pb[1][:], Ptab[:, 0, :], ALU.mult, pk(1) + yk(4, 5), [("sg", 0)])
                        dv(t2, pb[2][:], Ptab[:, 1, :], ALU.mult, pk(2) + yk(5, 6), [("sg", 1)])
                        dv(xt16[:, 0, :], t1, t2, ALU.subtract, [("sg", 0), ("sg", 1)], [("xt16", 0)])
                        dv(t1, pb[2][:], Ptab[:, 0, :], ALU.mult, pk(2) + yk(4, 5), [("sg", 0)])
                        dv(t2, pb[1][:], Ptab[:, 1, :], ALU.mult, pk(1) + yk(5, 6), [("sg", 1)])
                        dv(xt16[:, 1, :], t1, t2, ALU.add, [("sg", 0), ("sg", 1)], [("xt16", 1)])
                        cb = 3 if k % 2 == 0 else 6
                        for ri in range(2):
                            for gp in range(4):
                                P.op("pe", lambda e, ri=ri, gp=gp, cb=cb: e.matmul(
                                    pb[cb + ri][:, gp * 128:(gp + 1) * 128], xt16[:, ri, gp * 128:(gp + 1) * 128], tle[:],
                                    start=True, stop=True), [("xt16", ri), "tle"], pk(cb + ri))
                    def s5_mid(k, c=c, c4=c4):
                        par = k % 2
                        ccr_, cci_ = yS[:, 8 + 2 * par, :], yS[:, 9 + 2 * par, :]
                        kcr, kci = yk(8 + 2 * par, 9 + 2 * par), yk(9 + 2 * par, 10 + 2 * par)
                        for gp in range(4):
                            qs = slice(gp * 128, (gp + 1) * 128)
                            cb = 3 if par == 0 else 6
                            _act(P, ccr_[:, qs], pb[cb][:, qs], AF.Identity, pk(cb) + ["car"], kcr, bias=car[:, 0, gp:gp + 1])
                            _act(P, cci_[:, qs], pb[cb + 1][:, qs], AF.Identity, pk(cb + 1) + ["car"], kci, bias=car[:, 1, gp:gp + 1])
                        cr127 = ccr_.rearrange("p (g j) -> p g j", j=128)[:, :, 127]
                        ci127 = cci_.rearrange("p (g j) -> p g j", j=128)[:, :, 127]
                        dv(k4[0][:], lam128[:, 0, :], cr127, ALU.mult, ["lam128"] + kcr, [("k4", 0)])
                        dv(k4[1][:], lam128[:, 1, :], ci127, ALU.mult, ["lam128"] + kci, [("k4", 1)])
                        dv(k4[2][:], lam128[:, 0, :], ci127, ALU.mult, ["lam128"] + kci, [("k4", 2)])
                        dv(k4[3][:], lam128[:, 1, :], cr127, ALU.mult, ["lam128"] + kcr, [("k4", 3)])
                        dv(car[:, 0, :], k4[0][:], k4[1][:], ALU.subtract, [("k4", 0), ("k4", 1)], ["car"])
                        dv(car[:, 1, :], k4[2][:], k4[3][:], ALU.add, [("k4", 2), ("k4", 3)], ["car"])
                    def s5_back(k, c=c, c4=c4):
                        ks = slice(k * 128, (k + 1) * 128)
                        par = k % 2
                        ccr_, cci_ = yS[:, 8 + 2 * par, :], yS[:, 9 + 2 * par, :]
                        kcr, kci = yk(8 + 2 * par, 9 + 2 * par), yk(9 + 2 * par, 10 + 2 * par)
                        t3, t4 = tmp[0][:], tmp[1][:]
                        pv = lambda out, in0, in1, op, r, w: P.op(
                            "pool", lambda e: e.tensor_tensor(out=out, in0=in0, in1=in1, op=op), r, w)
                        pv(t3, ccr_, Qtab[:, 0, :], ALU.mult, kcr + yk(6, 7), [("tmp", 0)])
                        pv(t4, cci_, Qtab[:, 1, :], ALU.mult, kci + yk(7, 8), [("tmp", 1)])
                        pv(sT16[:, 0, :], t3, t4, ALU.subtract, [("tmp", 0), ("tmp", 1)], [("sT16", 0)])
                        pv(t3, cci_, Qtab[:, 0, :], ALU.mult, kci + yk(6, 7), [("tmp", 0)])
                        pv(t4, ccr_, Qtab[:, 1, :], ALU.mult, kcr + yk(7, 8), [("tmp", 1)])
                        pv(sT16[:, 1, :], t3, t4, ALU.add, [("tmp", 0), ("tmp", 1)], [("sT16", 1)])
                        n = 0
                        for gp in range(4):
                            for ri in range(2):
                                P.op("pe", lambda e, gp=gp, ri=ri, n=n: e.matmul(
                                    pb[5][:, 0:128], Cw[:, gp * 2 + ri, :], sT16[:, ri, gp * 128:(gp + 1) * 128],
                                    start=(n == 0), stop=(n == 7)), ["Cw", ("sT16", ri)], pk(5, 0))
                                n += 1
                        P.op("dve", lambda e, ks=ks, c=c: e.scalar_tensor_tensor(
                            out=ytmp[:], in0=UC[:, ks], scalar=dsk[:, oi * NCH + c:oi * NCH + c + 1], in1=pb[5][:, 0:128],
                            op0=ALU.mult, op1=ALU.add), hk(0, 8) + ["dsk"] + pk(5, 0), ["ytmp"])
                        _act(P, GS[:, ks], ytmp[:], AF.Gelu, ["ytmp"], hk(8 + k // 4, 9 + k // 4))
                    nk = L // 128
                    s5_front(0)
                    for k in range(nk):
                        if k + 1 < nk:
                            s5_front(k + 1)
                        s5_mid(k)
                        if k >= 1:
                            s5_back(k - 1)
                    s5_back(nk - 1)
                    if _STOP:
                        continue
                    P.dma("sp", lambda e, c=c: e.dma_start(out=geT_d[c], in_=GS[:, 0:L]), hk(8, 16), ["geT_d"], key="st6")
                if _STOP:
                    return trivial_o3()
                gv = geT_d.rearrange("c p t -> p c t")
                wv = w_glu[oi].rearrange("(c p) n -> p c n", p=128)
                for g in range(n_tg):
                    P.dma("sp", lambda e, g=g: e.dma_start(out=xn[:], in_=gv[:, :, g * TG:(g + 1) * TG]),
                          ["geT_d"], [("xn", c) for c in range(NCH)], key="ld3")
                    load_h(src, g)
                    for j in range(NCH):
                        i = rot("wi", NWI)
                        P.dma("pool", lambda e, i=i, j=j: [
                            e.dma_start(out=wi[i][:, 0, :, :], in_=wv[:, :, j * 128:(j + 1) * 128]),
                            e.dma_start(out=wi[i][:, 1, :, :], in_=wv[:, :, D_MODEL + j * 128:D_MODEL + (j + 1) * 128])],
                            [], [("wi", i)], key="wi%d" % i, ndma=2)
                        bv = (2 * j) % 4 + 4
                        bgt = bv + 1
                        for a, bank in ((0, bv), (1, bgt)):
                            for c in range(NCH):
                                P.op("pe", lambda e, i=i, c=c, a=a, bank=bank: e.matmul(
                                    pb[bank][:], wi[i][:, a, c, :], xn[:, c, :], start=(c == 0), stop=(c == NCH - 1)),
                                    [("wi", i), ("xn", c)], pk(bank))
                        si = rot("sg", 2)
                        _act(P, sg[si][:], pb[bgt][:], AF.Sigmoid, pk(bgt), [("sg", si)])
                        P.op("dve", lambda e, si=si, j=j, bv=bv: e.tensor_tensor(
                            out=yS[:, j, :], in0=sg[si][:], in1=pb[bv][:], op=ALU.mult),
                            [("sg", si)] + pk(bv), [("yS", j)])
                    post_residual(li, 1)
                    store_h(hT, g)

        ei_map, oi_map = {}, {}
        for li, kind in enumerate(layers):
            if kind == "even":
                ei_map[li] = len(ei_map)
            elif kind == "odd":
                oi_map[li] = len(oi_map)
        for li, kind in enumerate(layers):
            src = xT if li == 0 else hT
            last = (li == depth - 1)
            if kind == "even":
                even_mixer(li, ei_map[li], src)
                src = hT
            elif kind == "odd":
                odd_mixer(li, oi_map[li], src)
                src = hT
            for g in range(n_tg):
                load_h(src, g)
                prenorm(li, 2)
                ffn(li)
                post_residual(li, 3)
                store_h(yT if last else hT, g)

        P.emit(stack)
    nc.in_names_ = list(B.in_names)
    return nc


def _bf16(a):
    return np.asarray(a, dtype=np.float32).astype(ml_dtypes.bfloat16)


def host_consts():
    i = np.arange(128)
    cm = np.zeros((128, 4, 128), np.float32)
    cm[:, 0, :] = (i[:, None] >= i[None, :])
    cm[:, 1, :] = (i[:, None] <= i[None, :]) * (-1.0 / 16)
    cm[:, 2, :] = (i[:, None] > i[None, :]) * (-1.0 / 16)
    cm[:, 3, :] = (i[:, None] <= i[None, :])
    sbm = np.zeros((128, 4, TG), np.float32)
    for d in range(4):
        for tb in range(4):
            if tb > d:
                sbm[:, d, tb * 128:(tb + 1) * 128] = 1.0
            elif tb == d:
                sbm[:, d, tb * 128:(tb + 1) * 128] = (i[:, None] < i[None, :])
    return {"cm128": _bf16(cm), "sbmask": _bf16(sbm), "ones_bf": _bf16(np.ones((128, 128)))}


def host_layout(inputs, layers):
    depth = len(layers)
    f = lambda k: np.asarray(inputs[k], dtype=np.float32)
    g = f("norm_gains")[:depth]
    m = {"gains": np.ascontiguousarray(g.reshape(depth * 4, NCH, 128).transpose(2, 0, 1).reshape(128, depth * 4 * NCH)),
         "w_ffn_in": f("w_ffn_in")[:depth], "w_ffn_out": f("w_ffn_out")[:depth]}
    ne = sum(1 for l in layers if l == "even")
    no = sum(1 for l in layers if l == "odd")
    if ne:
        m["w_in"] = f("w_in")[:ne]
        m["w_out"] = f("w_out")[:ne]
        m["wgu"] = np.ascontiguousarray(np.concatenate([f("w_gate_up")[:ne], f("b_gate")[:ne, None, :]], axis=1))
        gg = f("gla_norm_gain")[:ne]
        m["gla_gain"] = np.ascontiguousarray(gg.reshape(ne, 8, 128).transpose(2, 0, 1).reshape(128, ne * 8))
    if no:
        lre, lim, ls = f("s5_lambda_re")[:no], f("s5_lambda_im")[:no], f("s5_log_step")[:no]
        lse = np.broadcast_to(ls[:, :, None], lre.shape)
        lam3 = np.stack([lre, lim, lse], axis=1)
        m["s5_lam"] = np.ascontiguousarray(lam3.reshape(no, 3, 8192))
        t = lam3.reshape(no, 3, 64, 2, 64)
        m["s5_lamT"] = np.ascontiguousarray(t.transpose(0, 3, 4, 1, 2).reshape(no, 128, 3, 64))
        t = lam3.reshape(no, 3, NCH, 8, 1, 64)
        t = np.broadcast_to(t, (no, 3, NCH, 8, 16, 64))
        m["s5_lamB"] = np.ascontiguousarray(t.transpose(0, 3, 4, 1, 2, 5).reshape(no, 128, 3, NCH, 64))
        b2 = np.stack([f("s5_b_re")[:no], f("s5_b_im")[:no]], axis=1)
        t = b2.reshape(no, 2, NCH, 8, 64, 16)
        m["s5_bT"] = np.ascontiguousarray(t.transpose(0, 3, 5, 1, 2, 4).reshape(no, 128, 2, NCH, 64))
        c2 = np.stack([f("s5_c_re")[:no], f("s5_c_im")[:no]], axis=1)
        cw = np.zeros((no, NCH, 2, 64, 4, 2, 8, 16), np.float32)
        t = c2.reshape(no, 2, NCH, 4, 2, 16, 64)
        for gp in range(4):
            for g2 in range(2):
                cw[:, :, g2, :, gp, :, 2 * gp + g2, :] = t[:, :, :, gp, g2].transpose(0, 2, 4, 1, 3)
        m["s5_cw"] = np.ascontiguousarray(cw.reshape(no, NCH, 128, 8, 128))
        m["s5_dT"] = np.ascontiguousarray(f("s5_d")[:no].reshape(no, NCH, 128).transpose(2, 0, 1).reshape(128, no * NCH))
        m["w_glu"] = f("w_glu")[:no]
        jc = np.zeros((128, 138), np.float32)
        jc[:, 0:128] = np.arange(128)[None, :]
        jc[:, 128] = np.arange(128)
        jc[:, 129] = 1.0
        jc[:, 130:138] = (np.arange(128)[:, None] // 16 == np.arange(8)[None, :])
        m["jconst"] = jc
        i = np.arange(128)
        m["trile"] = _bf16((i[:, None] <= i[None, :]).astype(np.float32))
    m.update(host_consts())
    return m, no


_CACHE = {}


def kernel(**inputs):
    layers = ["even", "odd"] * (DEPTH // 2)
    x = np.asarray(inputs["x"], dtype=np.float32)
    m, _ = host_layout(inputs, layers)
    if "nc" not in _CACHE:
        _CACHE["nc"] = build(SEQ, layers)
    nc = _CACHE["nc"]
    in_maps = []
    for b in range(BATCH):
        mm = dict(m)
        mm["xT"] = np.ascontiguousarray(x[b].T)
        in_maps.append(mm)
    res = run_bass_kernel_spmd(nc, in_maps, core_ids=list(range(BATCH)))
    out = np.stack([np.ascontiguousarray(res.results[b]["yT"].T) for b in range(BATCH)], axis=0)
    return out.astype(np.float32)
```

```python
import contextlib
import numpy as np
import ml_dtypes
import concourse.bass as bass
import concourse.mybir as mybir
from concourse.bass_utils import run_bass_kernel_spmd

F32 = mybir.dt.float32
BF16 = mybir.dt.bfloat16
AF = mybir.ActivationFunctionType
ALU = mybir.AluOpType

D_MODEL = 2048
SEQ = 4096
BATCH = 2
DEPTH = 4
D_FF = 5632
IN_WIDTH = 6160
EPS = 1e-6
NCH = D_MODEL // 128
TG = 512

import os
_DBG = int(os.environ.get('ODD_DBG', '0'))
_STOP = int(os.environ.get('ODD_STOP', '0'))
_SKIP = int(os.environ.get('ODD_SKIP', '0'))
SAME_ENGINE_SYNC = True
SEM_LIMIT = int(os.environ.get('SEM_LIMIT', '8000'))


class Op:
    __slots__ = ("eng", "fn", "reads", "writes", "dma", "key", "deps", "signal",
                 "sem_i", "sem_v", "ndma")

    def __init__(self, eng, fn, reads, writes, dma=False, key=None, ndma=1):
        self.eng = eng
        self.fn = fn
        self.reads = tuple(reads)
        self.writes = tuple(writes)
        self.dma = dma
        self.key = key
        self.deps = []
        self.signal = False
        self.sem_i = None
        self.sem_v = None
        self.ndma = ndma


class Prog:
    ENGS = ("pe", "act", "dve", "pool", "sp")

    def __init__(self, nc):
        self.nc = nc
        self.ops = []
        self.last_w = {}
        self.readers = {}
        self.last_dma = {}

    def op(self, eng, fn, reads=(), writes=()):
        o = Op(eng, fn, reads, writes)
        self._track(o)
        return o

    def dma(self, eng, fn, reads=(), writes=(), key=None, ndma=1):
        o = Op(eng, fn, reads, writes, dma=True, key=key, ndma=ndma)
        prev = self.last_dma.get(key)
        if prev is not None:
            o.deps.append(prev)
        self.last_dma[key] = o
        self._track(o)
        return o

    def _track(self, o):
        deps = o.deps
        for k in o.reads:
            w = self.last_w.get(k)
            if w is not None:
                deps.append(w)
        for k in o.writes:
            w = self.last_w.get(k)
            if w is not None:
                deps.append(w)
            deps.extend(self.readers.get(k, ()))
        for k in o.reads:
            self.readers.setdefault(k, []).append(o)
        for k in o.writes:
            self.last_w[k] = o
            self.readers[k] = []
        seen = set()
        dd = []
        for d in deps:
            if d is o or id(d) in seen:
                continue
            seen.add(id(d))
            dd.append(d)
        o.deps = dd
        self.ops.append(o)

    def simulate(self, per_eng, sems_of):
        semv = {}
        pos = {e: 0 for e in self.ENGS}
        total = sum(len(v) for v in per_eng.values())
        done = 0
        while done < total:
            prog = False
            for e in self.ENGS:
                while pos[e] < len(per_eng[e]):
                    o = per_eng[e][pos[e]]
                    ok = True
                    for d in o.deps:
                        k = (sems_of[id(d)]["name"], d.sem_i)
                        if semv.get(k, 0) < d.sem_v:
                            ok = False
                            break
                    if not ok:
                        break
                    if o.dma:
                        k = (sems_of[id(o)]["name"], o.sem_i)
                        semv[k] = semv.get(k, 0) + 16 * o.ndma
                        assert semv[k] == o.sem_v, (k, semv[k], o.sem_v)
                    elif o.signal:
                        k = (sems_of[id(o)]["name"], o.sem_i)
                        semv[k] = semv.get(k, 0) + 1
                        assert semv[k] == o.sem_v, (k, semv[k], o.sem_v)
                    pos[e] += 1
                    done += 1
                    prog = True
            if not prog:
                print("DEADLOCK at", {e: pos[e] for e in self.ENGS})
                for e in self.ENGS:
                    if pos[e] < len(per_eng[e]):
                        o = per_eng[e][pos[e]]
                        print(" ", e, "reads", o.reads[:4], "writes", o.writes[:4],
                              [(sems_of[id(d)]["name"], d.sem_i, d.sem_v, d.eng, d.writes[:2]) for d in o.deps][:6])
                raise RuntimeError("deadlock")
        print("PROG_SIM ok: %d ops, sems %d" % (total, len(semv)))

    def emit(self, stack):
        nc = self.nc
        for o in self.ops:
            nd = []
            for d in o.deps:
                if not d.dma and d.eng == o.eng:
                    if d.eng == "pe" or not SAME_ENGINE_SYNC or o.dma:
                        continue
                d.signal = True
                nd.append(d)
            o.deps = nd
        streams = {}

        def stream(name):
            s = streams.get(name)
            if s is None:
                s = {"sems": [], "val": 0, "name": name}
                streams[name] = s
            return s

        def bump(s, inc):
            if not s["sems"] or s["val"] + inc > SEM_LIMIT:
                s["sems"].append(stack.enter_context(
                    nc.semaphore("s_%s_%d" % (s["name"], len(s["sems"])))))
                s["val"] = 0
            s["val"] += inc
            return len(s["sems"]) - 1, s["val"]

        sems_of = {}
        for o in self.ops:
            if o.dma:
                s = stream("d_" + str(o.key))
                o.sem_i, o.sem_v = bump(s, 16 * o.ndma)
                sems_of[id(o)] = s
            elif o.signal:
                s = stream("e_" + o.eng)
                o.sem_i, o.sem_v = bump(s, 1)
                sems_of[id(o)] = s
        final = []
        for name, s in streams.items():
            final.append((s, len(s["sems"]) - 1, s["val"]))

        per_eng = {e: [o for o in self.ops if o.eng == e] for e in self.ENGS}
        if os.environ.get("PROG_SIM"):
            self.simulate(per_eng, sems_of)
        block = stack.enter_context(nc.Block())

        def run(eng_name, eng):
            waited = {}
            for o in per_eng[eng_name]:
                for d in o.deps:
                    s = sems_of[id(d)]
                    sem = s["sems"][d.sem_i]
                    k = (s["name"], d.sem_i)
                    if waited.get(k, 0) >= d.sem_v:
                        continue
                    waited[k] = d.sem_v
                    eng.wait_ge(sem, d.sem_v)
                r = o.fn(eng)
                if o.dma:
                    sem = sems_of[id(o)]["sems"][o.sem_i]
                    if not isinstance(r, (list, tuple)):
                        r = [r]
                    assert len(r) == o.ndma, (len(r), o.ndma)
                    for ins in r:
                        ins.then_inc(sem, 16)
                elif o.signal:
                    sem = sems_of[id(o)]["sems"][o.sem_i]
                    r.then_inc(sem, 1)
            if eng_name == "sp":
                for s, i, v in final:
                    if s["name"].startswith("d_"):
                        eng.wait_ge(s["sems"][i], v)

        @block.tensor
        def _(e):
            run("pe", e)

        @block.scalar
        def _(e):
            run("act", e)

        @block.vector
        def _(e):
            run("dve", e)

        @block.gpsimd
        def _(e):
            run("pool", e)

        @block.sync
        def _(e):
            run("sp", e)


class Builder:
    def __init__(self, L, layers):
        self.L = L
        self.layers = layers
        self.nc = bass.Bass("TRN2", target_bir_lowering=False)
        self.P = Prog(self.nc)
        self.uid = 0
        self.psum_rr = 0
        self.in_names = []

    def dram_in(self, name, shape, dt=F32):
        self.in_names.append(name)
        return self.nc.dram_tensor(name, list(shape), dt, kind="ExternalInput").ap()

    def dram_out(self, name, shape, dt=F32):
        return self.nc.dram_tensor(name, list(shape), dt, kind="ExternalOutput").ap()

    def dram_tmp(self, name, shape, dt):
        return self.nc.dram_tensor(name, list(shape), dt, kind="Internal").ap()

    def sb(self, stack, name, shape, dt):
        return stack.enter_context(self.nc.sbuf_tensor("sb_" + name, list(shape), dt))

    def ps(self, stack, name, shape, dt=F32):
        return stack.enter_context(self.nc.psum_tensor("ps_" + name, list(shape), dt))


def _act(P, out, in_, func, reads, writes, eng="act", **kw):
    return P.op(eng, lambda e: e.activation(out=out, in_=in_, func=func, **kw), reads, writes)


def build(L, layers, n_in_layers=None):
    B = Builder(L, layers)
    nc, P = B.nc, B.P
    n_tg = L // TG
    n_even = sum(1 for l in layers if l == "even")
    n_odd = sum(1 for l in layers if l == "odd")
    depth = len(layers)

    xT = B.dram_in("xT", [D_MODEL, L])
    gains = B.dram_in("gains", [128, depth * 4 * NCH])
    w_ffn_in = B.dram_in("w_ffn_in", [depth, 2 * D_FF // 128, 128, D_MODEL])
    w_ffn_out = B.dram_in("w_ffn_out", [depth, D_FF, D_MODEL])
    ones_in = B.dram_in("ones_bf", [128, 128], BF16)
    yT = B.dram_out("yT", [D_MODEL, L])
    hT = B.dram_tmp("hT", [D_MODEL, L], F32)

    stack = contextlib.ExitStack()
    with stack:
        hA = B.sb(stack, "hA", [128, NCH, TG], F32)
        xn = B.sb(stack, "xn", [128, NCH, TG], BF16)
        hid = B.sb(stack, "hid", [128, D_FF // 128, TG], BF16)
        yS = B.sb(stack, "yS", [128, NCH, TG], F32)
        NWI = 2
        wi_flat = [B.sb(stack, "wi%d" % i, [128, 4096], BF16) for i in range(NWI)]
        wi = [w[:].rearrange("p (a c f) -> p a c f", a=2, c=NCH) for w in wi_flat]
        wiB = [w[:].rearrange("p (c n) -> p c n", c=NCH) for w in wi_flat]
        NWO = 2
        wo = [B.sb(stack, "wo%d" % i, [128, 4, 1024], BF16) for i in range(NWO)]
        sq = [B.sb(stack, "sq%d" % i, [128, TG], BF16) for i in range(2)]
        sg = [B.sb(stack, "sg%d" % i, [128, TG], F32) for i in range(2)]
        tmp = [B.sb(stack, "tmp%d" % i, [128, TG], F32) for i in range(2)]
        rstd = B.sb(stack, "rstd", [128, TG], F32)
        xt16 = B.sb(stack, "xt16", [128, 2, 512], BF16)
        sT16 = B.sb(stack, "sT16", [128, 2, 512], BF16)
        gsb = B.sb(stack, "gsb", [128, depth * 4 * NCH], F32)
        ones = B.sb(stack, "ones", [128, 128], BF16)
        pb = [B.ps(stack, "pb%d" % i, [128, TG], F32) for i in range(8)]

        P.dma("sp", lambda e: e.dma_start(out=gsb[:], in_=gains[:, :]), [], ["gsb"], key="c0")
        P.dma("sp", lambda e: e.dma_start(out=ones[:], in_=ones_in[:, :]), [], ["ones"], key="c1")

        def gain(layer, k, c):
            i = (layer * 4 + k) * NCH + c
            return gsb[:, i:i + 1]

        cnt = {"wi": 0, "wo": 0, "sq": 0, "sg": 0, "tmp": 0, "pbk": 0}

        def hk(a, b):
            return [("hid", j) for j in range(a, b)]

        def pk(n, q=None):
            if q is None:
                return [("pb", n, k) for k in range(4)]
            return [("pb", n, q)]

        def rot(name, n):
            i = cnt[name] % n
            cnt[name] += 1
            return i

        def rms_stats(src, src_key, pbank):
            for c in range(NCH):
                i = rot("sq", 2)
                _act(P, sq[i][:], src[:, c, :], AF.Square, [(src_key, c)], [("sq", i)])
                P.op("pe", lambda e, i=i, c=c: e.matmul(pb[pbank][:], ones[:], sq[i][:],
                                                        start=(c == 0), stop=(c == NCH - 1)),
                     [("sq", i), "ones"], pk(pbank))
            P.op("dve", lambda e: e.tensor_scalar(out=rstd[:], in0=pb[pbank][:], scalar1=1.0 / D_MODEL,
                                                  scalar2=EPS, op0=ALU.mult, op1=ALU.add),
                 pk(pbank), ["rstd"])
            _act(P, rstd[:], rstd[:], AF.Sqrt, ["rstd"], ["rstd"])
            P.op("dve", lambda e: e.reciprocal(out=rstd[:], in_=rstd[:]), ["rstd"], ["rstd"])

        def load_h(src_ap, g):
            v = src_ap.rearrange("(c p) t -> p c t", p=128)
            P.dma("sp", lambda e: e.dma_start(out=hA[:], in_=v[:, :, g * TG:(g + 1) * TG]),
                  [("dram_h", g)], [("hA", c) for c in range(NCH)], key="hA")

        def store_h(dst_ap, g):
            v = dst_ap.rearrange("(c p) t -> p c t", p=128)
            P.dma("sp", lambda e: e.dma_start(out=v[:, :, g * TG:(g + 1) * TG], in_=hA[:]),
                  [("hA", c) for c in range(NCH)], [("dram_h", g)], key="hst")

        def prenorm(layer, k):
            rms_stats(hA, "hA", 0)
            for c in range(NCH):
                P.op("dve", lambda e, c=c: e.scalar_tensor_tensor(
                    out=xn[:, c, :], in0=hA[:, c, :], scalar=gain(layer, k, c), in1=rstd[:],
                    op0=ALU.mult, op1=ALU.mult), [("hA", c), "rstd", "gsb"], [("xn", c)])

        def post_residual(layer, k):
            rms_stats(yS, "yS", 0)
            for c in range(NCH):
                i = rot("tmp", 2)
                P.op("dve", lambda e, c=c, i=i: e.scalar_tensor_tensor(
                    out=tmp[i][:], in0=yS[:, c, :], scalar=gain(layer, k, c), in1=rstd[:],
                    op0=ALU.mult, op1=ALU.mult), [("yS", c), "rstd", "gsb"], [("tmp", i)])
                P.op("dve", lambda e, c=c, i=i: e.tensor_tensor(
                    out=hA[:, c, :], in0=hA[:, c, :], in1=tmp[i][:], op=ALU.add),
                    [("tmp", i), ("hA", c)], [("hA", c)])

        def dense_out(act_buf, act_key, kc, w_ap):
            wv = w_ap.rearrange("(j p) n -> p j n", p=128)
            for half in range(2):
                for j0 in range(0, kc, 4):
                    nj = min(4, kc - j0)
                    i = rot("wo", NWO)
                    P.dma("pool", lambda e, i=i, j0=j0, nj=nj, half=half: e.dma_start(
                        out=wo[i][:, 0:nj, :], in_=wv[:, j0:j0 + nj, half * 1024:(half + 1) * 1024]),
                        [], [("wo", i)], key="wo%d" % i)
                    for jj in range(nj):
                        j = j0 + jj
                        for dc in range(8):
                            P.op("pe", lambda e, i=i, jj=jj, j=j, dc=dc: e.matmul(
                                pb[dc][:], wo[i][:, jj, dc * 128:(dc + 1) * 128], act_buf[:, j, :],
                                start=(j == 0), stop=(j == kc - 1)),
                                [("wo", i), (act_key, j)], pk(dc))
                for dc in range(8):
                    c = half * 8 + dc
                    P.op("act", lambda e, c=c, dc=dc: e.copy(out=yS[:, c, :], in_=pb[dc][:]),
                         pk(dc), [("yS", c)])

        def ffn(layer):
            nf = D_FF // 128
            wv = w_ffn_in[layer]
            for j in range(nf):
                i = rot("wi", NWI)
                P.dma("pool", lambda e, i=i, j=j: [
                    e.dma_start(out=wi[i][:, 0, :, :], in_=wv[j].rearrange("p (c f) -> p c f", c=NCH)),
                    e.dma_start(out=wi[i][:, 1, :, :], in_=wv[nf + j].rearrange("p (c f) -> p c f", c=NCH))],
                    [], [("wi", i)], key="wi%d" % i, ndma=2)
                bg = (2 * j) % 4 + 4
                bu = bg + 1
                for c in range(NCH):
                    P.op("pe", lambda e, i=i, c=c, bg=bg: e.matmul(
                        pb[bg][:], wi[i][:, 0, c, :], xn[:, c, :], start=(c == 0), stop=(c == NCH - 1)),
                        [("wi", i), ("xn", c)], pk(bg))
                for c in range(NCH):
                    P.op("pe", lambda e, i=i, c=c, bu=bu: e.matmul(
                        pb[bu][:], wi[i][:, 1, c, :], xn[:, c, :], start=(c == 0), stop=(c == NCH - 1)),
                        [("wi", i), ("xn", c)], pk(bu))
                si = rot("sg", 2)
                _act(P, sg[si][:], pb[bg][:], AF.Silu, pk(bg), [("sg", si)])
                P.op("dve", lambda e, si=si, j=j, bu=bu: e.tensor_tensor(
                    out=hid[:, j, :], in0=sg[si][:], in1=pb[bu][:], op=ALU.mult),
                    [("sg", si)] + pk(bu), [("hid", j)])
            dense_out(hid, "hid", nf, w_ffn_out[layer])


        if n_even:
            w_in = B.dram_in("w_in", [n_even, D_MODEL, IN_WIDTH])
            wgu_in = B.dram_in("wgu", [n_even, 17, 512])
            glag_in = B.dram_in("gla_gain", [128, n_even * 8])
            w_out = B.dram_in("w_out", [n_even, D_MODEL, D_MODEL])
            cm_in = B.dram_in("cm128", [128, 4, 128], BF16)
            sbm_in = B.dram_in("sbmask", [128, 4, TG], BF16)
            qT_d = B.dram_tmp("qT_d", [8, 128, L], BF16)
            kT_d = B.dram_tmp("kT_d", [8, 128, L], BF16)
            v_d = B.dram_tmp("v_d", [L, 1024], BF16)
            mixT_d = B.dram_tmp("mixT_d", [16, 128, L], BF16)

            cm = B.sb(stack, "cm", [128, 4, 128], BF16)
            sbm = B.sb(stack, "sbm", [128, 4, TG], BF16)
            la = B.sb(stack, "la", [128, 4, 512], BF16)
            sbv = B.sb(stack, "sbv", [128, 4, 256], BF16)
            lrT = B.sb(stack, "lrT", [17, TG], BF16)
            wgu = B.sb(stack, "wgu", [17, 512], BF16)
            S32 = B.sb(stack, "S32", [64, 8, 128], F32)
            Sb = B.sb(stack, "Sb", [64, 8, 128], BF16)
            dec = B.sb(stack, "dec", [64, 8], F32)
            glag = B.sb(stack, "glag", [128, n_even * 8], F32)
            khat = sT16[:, 1, :]
            qtl = B.sb(stack, "qtl", [64, 128], BF16)
            ktl = B.sb(stack, "ktl", [64, 128], BF16)
            PT = B.sb(stack, "PT", [128, 128], BF16)
            e1 = B.sb(stack, "e1", [64, 128], F32)
            e2 = B.sb(stack, "e2", [64, 128], F32)
            at_e = sg[0]
            at_sp = sg[1]
            at_er = tmp[0]
            at_S32 = tmp[1]
            at_sp16 = xt16[:, 0, :]
            at_S16 = xt16[:, 1, :]
            at_A16 = sT16[:, 0, :]

            P.dma("sp", lambda e: e.dma_start(out=cm[:], in_=cm_in[:, :, :]), [], ["cm"], key="c2")
            P.dma("sp", lambda e: e.dma_start(out=sbm[:], in_=sbm_in[:, :, :]), [], ["sbm"], key="c3")
            P.dma("sp", lambda e: e.dma_start(out=glag[:], in_=glag_in[:, :]), [], ["glag"], key="c4")
            P.op("dve", lambda e: e.memset(lrT[:], 1.0), [], ["lrT"])


            def tmv(tt):
                return hid[:, 32 + 3 * tt:35 + 3 * tt, :].rearrange("p a b -> p (a b)")

            def even_m1(li, ei, g, src):
                load_h(src, g)
                prenorm(li, 0)
                wv = w_in[ei].rearrange("(c p) n -> p c n", p=128)
                gs = slice(g * TG, (g + 1) * TG)

                def fm_proj(col0, M, evac):
                    i = rot("wi", NWI)
                    P.dma("pool", lambda e: e.dma_start(out=wi[i][:, 0, :, 0:M], in_=wv[:, :, col0:col0 + M]),
                          [], [("wi", i)], key="wi%d" % i)
                    bank = 4 + rot("pbk", 4)
                    for c in range(NCH):
                        P.op("pe", lambda e, c=c: e.matmul(pb[bank][0:M, :], wi[i][:, 0, c, 0:M], xn[:, c, :],
                                                           start=(c == 0), stop=(c == NCH - 1)),
                             [("wi", i), ("xn", c)], pk(bank))
                    evac(bank)

                for h in range(8):
                    fm_proj(h * 128, 128, lambda bank, h=h: P.op(
                        "act", lambda e: e.mul(out=hid[:, h, :], in_=pb[bank][:], mul=128.0 ** -0.5),
                        pk(bank), hk(h, h + 1)))
                qv = qT_d.rearrange("h p t -> p h t")
                P.dma("sp", lambda e: e.dma_start(out=qv[:, :, gs], in_=hid[:, 0:8, :]), hk(0, 8), ["qT_d"], key="st0")
                for h in range(8):
                    fm_proj(1024 + h * 128, 128, lambda bank, h=h: P.op(
                        "act", lambda e: e.copy(out=hid[:, 8 + h, :], in_=pb[bank][:]),
                        pk(bank), hk(8 + h, 9 + h)))
                kv = kT_d.rearrange("h p t -> p h t")
                P.dma("sp", lambda e: e.dma_start(out=kv[:, :, gs], in_=hid[:, 8:16, :]), hk(8, 16), ["kT_d"], key="st1")
                for h in range(8):
                    fm_proj(3072 + h * 64, 64, lambda bank, h=h: P.op(
                        "act", lambda e: e.mul(out=hid[0:64, 16 + h, :], in_=pb[bank][0:64, :], mul=64.0 ** -0.5),
                        pk(bank), hk(16 + h, 17 + h)))
                    fm_proj(3584 + h * 64, 64, lambda bank, h=h: P.op(
                        "act", lambda e: e.copy(out=hid[0:64, 24 + h, :], in_=pb[bank][0:64, :]),
                        pk(bank), hk(24 + h, 25 + h)))
                for h in range(8):
                    fm_proj(5120 + h * 128, 128, lambda bank, h=h: _act(
                        P, yS[:, h, :], pb[bank][:], AF.Silu, pk(bank), [("yS", h)]))
                fm_proj(6144, 16, lambda bank: P.op(
                    "act", lambda e: e.copy(out=lrT[0:16, :], in_=pb[bank][0:16, :]), pk(bank), ["lrT"]))

                for blk in range(10):
                    if blk < 4:
                        col0 = 2048 + 256 * blk
                    elif blk < 8:
                        col0 = 4096 + 256 * (blk - 4)
                    else:
                        col0 = 3584 + 256 * (blk - 8)
                    i = rot("wi", NWI)
                    P.dma("pool", lambda e, i=i, col0=col0: e.dma_start(out=wiB[i][:, :, :], in_=wv[:, :, col0:col0 + 256]),
                          [], [("wi", i)], key="wi%d" % i)
                    for tt in range(4):
                        bank = 4 + rot("pbk", 4)
                        for c in range(NCH):
                            P.op("pe", lambda e, c=c, i=i, tt=tt, bank=bank: e.matmul(
                                pb[bank][:, 0:256], xn[:, c, tt * 128:(tt + 1) * 128], wiB[i][:, c, :],
                                start=(c == 0), stop=(c == NCH - 1)),
                                [("wi", i), ("xn", c)], pk(bank))
                        if blk < 4:
                            P.op("act", lambda e, tt=tt, bank=bank: e.copy(out=sbv[:, tt, :], in_=pb[bank][:, 0:256]),
                                 pk(bank), ["sbv"])
                        else:
                            o0 = 256 * (blk - 4)
                            P.op("act", lambda e, tt=tt, bank=bank, o0=o0: e.copy(
                                out=tmv(tt)[:, o0:o0 + 256], in_=pb[bank][:, 0:256]),
                                pk(bank), hk(32 + 3 * tt, 35 + 3 * tt))
                    if blk < 4:
                        dv = v_d[g * TG:(g + 1) * TG, blk * 256:(blk + 1) * 256].rearrange("(tt p) n -> p tt n", p=128)
                        P.dma("sp", lambda e, dv=dv: e.dma_start(out=dv, in_=sbv[:]), ["sbv"], ["v_d"], key="st2")

                for tt in range(4):
                    ts_ = slice(tt * 128, (tt + 1) * 128)
                    P.op("pe", lambda e, ts_=ts_: e.matmul(pb[1][:, :], lrT[0:17, ts_], wgu[0:17, :], start=True, stop=True),
                         ["lrT", "wgu"], pk(1))
                    si = rot("sg", 2)
                    _act(P, sg[si][:], pb[1][:], AF.Exp, pk(1), [("sg", si)], scale=-1.0)
                    _act(P, la[:, tt, :], sg[si][:], AF.Ln, [("sg", si)], [("la", tt)], bias=1.0)
                    P.op("pe", lambda e, tt=tt: e.matmul(pb[2][:, :], cm[:, 2, :], la[:, tt, :], start=True, stop=True),
                         ["cm", ("la", tt)], pk(2))
                    sj = rot("sg", 2)
                    _act(P, sg[sj][:], pb[2][:], AF.Exp, pk(2), [("sg", sj)])
                    P.op("dve", lambda e, sj=sj, tt=tt: e.tensor_tensor(
                        out=khat, in0=sg[sj][:], in1=tmv(tt)[:, 1024:1536], op=ALU.mult),
                        [("sg", sj)] + hk(32 + 3 * tt, 35 + 3 * tt), [("sT16", 1)])
                    for h in range(8):
                        ob = 5 + h // 4
                        oq = h % 4
                        osl = slice(oq * 128, (oq + 1) * 128)
                        vh = tmv(tt)[:, h * 128:(h + 1) * 128]
                        hkv = hk(32 + 3 * tt, 35 + 3 * tt)
                        P.op("pe", lambda e, h=h, tt=tt: e.matmul(pb[3][0:64, 0:128], la[:, tt, h * 64:(h + 1) * 64],
                                                                  cm[:, 1, :], start=True, stop=True),
                             ["cm", ("la", tt)], pk(3, 0))
                        _act(P, e1[:], pb[3][0:64, 0:128], AF.Exp, pk(3, 0), ["e1"])
                        _act(P, e2[:], pb[3][0:64, 0:128], AF.Exp, pk(3, 0), ["e2"], scale=-1.0)
                        _act(P, dec[:, h:h + 1], pb[3][0:64, 127:128], AF.Exp, pk(3, 0), [("dec", h)])
                        P.op("dve", lambda e, h=h, ts_=ts_: e.tensor_tensor(
                            out=qtl[:], in0=hid[0:64, 16 + h, ts_], in1=e1[:], op=ALU.mult),
                            ["e1"] + hk(16 + h, 17 + h), ["qtl"])
                        P.op("dve", lambda e, h=h, ts_=ts_: e.tensor_tensor(
                            out=ktl[:], in0=hid[0:64, 24 + h, ts_], in1=e2[:], op=ALU.mult),
                            ["e2"] + hk(24 + h, 25 + h), ["ktl"])
                        P.op("pe", lambda e: e.matmul(pb[4][:, 0:128], ktl[:], qtl[:], start=True, stop=True),
                             ["ktl", "qtl"], pk(4, 0))
                        P.op("dve", lambda e: e.tensor_tensor(out=PT[:], in0=pb[4][:, 0:128], in1=cm[:, 3, :], op=ALU.mult),
                             pk(4, 0) + ["cm"], ["PT"])
                        P.op("pe", lambda e, ob=ob, osl=osl, vh=vh: e.matmul(pb[ob][:, osl], vh, PT[:], start=True, stop=False),
                             ["PT"] + hkv, pk(ob, oq))
                        P.op("pe", lambda e, ob=ob, osl=osl, h=h: e.matmul(pb[ob][:, osl], Sb[:, h, :], qtl[:], start=False, stop=True),
                             [("Sb", h), "qtl"], pk(ob, oq))
                        P.op("pe", lambda e, h=h, vh=vh: e.matmul(pb[7][0:64, 0:128], khat[:, h * 64:(h + 1) * 64], vh,
                                                                  start=True, stop=True),
                             [("sT16", 1)] + hkv, pk(7, 0))
                        P.op("dve", lambda e, h=h: e.scalar_tensor_tensor(
                            out=S32[:, h, :], in0=S32[:, h, :], scalar=dec[:, h:h + 1], in1=pb[7][0:64, 0:128],
                            op0=ALU.mult, op1=ALU.add), [("S32", h), ("dec", h)] + pk(7, 0), [("S32", h)])
                        P.op("act", lambda e, h=h: e.copy(out=Sb[:, h, :], in_=S32[:, h, :]), [("S32", h)], [("Sb", h)])
                    for hb in range(2):
                        ob = 5 + hb
                        i = rot("sq", 2)
                        _act(P, sq[i][:], pb[ob][:], AF.Square, pk(ob), [("sq", i)])
                        P.op("pe", lambda e, i=i: e.matmul(pb[0][:], ones[:], sq[i][:], start=True, stop=True),
                             [("sq", i), "ones"], pk(0))
                        P.op("dve", lambda e: e.tensor_scalar(out=rstd[:], in0=pb[0][:], scalar1=1.0 / 128, scalar2=EPS,
                                                              op0=ALU.mult, op1=ALU.add), pk(0), ["rstd"])
                        _act(P, rstd[:], rstd[:], AF.Sqrt, ["rstd"], ["rstd"])
                        P.op("dve", lambda e: e.reciprocal(out=rstd[:], in_=rstd[:]), ["rstd"], ["rstd"])
                        ti = rot("tmp", 2)
                        P.op("dve", lambda e, ti=ti, ob=ob: e.tensor_tensor(out=tmp[ti][:], in0=pb[ob][:], in1=rstd[:], op=ALU.mult),
                             pk(ob) + ["rstd"], [("tmp", ti)])
                        for oq in range(4):
                            h = hb * 4 + oq
                            P.op("dve", lambda e, ti=ti, oq=oq, h=h, ts_=ts_: e.scalar_tensor_tensor(
                                out=hid[:, h, ts_], in0=tmp[ti][:, oq * 128:(oq + 1) * 128],
                                scalar=glag[:, ei * 8 + h:ei * 8 + h + 1], in1=yS[:, h, ts_],
                                op0=ALU.mult, op1=ALU.mult),
                                [("tmp", ti), "glag", ("yS", h)], hk(h, h + 1))
                mv = mixT_d[8:16].rearrange("h p t -> p h t")
                P.dma("sp", lambda e: e.dma_start(out=mv[:, :, gs], in_=hid[:, 0:8, :]), hk(0, 8), ["mixT_d"], key="st3")

            def sb_attention():
                nb = L // 128
                KT = hid[:, 0:8, :].rearrange("p a b -> p (a b)")
                QT = hid[:, 8:16, :].rearrange("p a b -> p (a b)")
                VV = hid[:, 16:24, :].rearrange("p a (b e) -> p (a b) e", e=128)
                OT = hid[:, 24:32, :].rearrange("p a b -> p (a b)")
                streams = [
                    dict(e=sg[0][:], ke=[("sg", 0)], sp=sg[1][:], ksp=[("sg", 1)], er=tmp[0][:], ker=[("tmp", 0)],
                         S32=tmp[1][:], kS32=[("tmp", 1)], sp16=xt16[:, 0, :], ksp16=[("xt16", 0)],
                         S16=xt16[:, 1, :], kS16=[("xt16", 1)], A16=sT16[:, 0, :], kA16=[("sT16", 0)],
                         bz=4, br=4, bo=0)]
                for si_ in range(3):
                    y0, h0 = 4 * si_, 32 + 3 * si_
                    streams.append(dict(
                        e=yS[:, y0, :], ke=[("yS", y0)], sp=yS[:, y0 + 1, :], ksp=[("yS", y0 + 1)],
                        er=yS[:, y0 + 2, :], ker=[("yS", y0 + 2)], S32=yS[:, y0 + 3, :], kS32=[("yS", y0 + 3)],
                        sp16=hid[:, h0, :], ksp16=hk(h0, h0 + 1), S16=hid[:, h0 + 1, :], kS16=hk(h0 + 1, h0 + 2),
                        A16=hid[:, h0 + 2, :], kA16=hk(h0 + 2, h0 + 3),
                        bz=5 + si_, br=5 + si_, bo=1 + si_))

                def step(st, G, kb, first):
                    di = kb - 4 * G
                    bz, br, bo = st["bz"], st["br"], st["bo"]
                    P.op("pe", lambda e: e.matmul(pb[bz][:], KT[:, kb * 128:(kb + 1) * 128],
                                                  QT[:, G * TG:(G + 1) * TG], start=True, stop=True),
                         hk(0, 16), pk(bz))
                    yield
                    _act(P, st["e"], pb[bz][:], AF.Exp, pk(bz), st["ke"])
                    yield
                    if di >= 0:
                        _act(P, st["sp"], st["e"], AF.Ln, st["ke"], st["ksp"], bias=1.0)
                        P.op("dve", lambda e: e.tensor_tensor(out=st["sp16"], in0=st["sp"], in1=sbm[:, di, :], op=ALU.mult),
                             st["ksp"] + ["sbm"], st["ksp16"])
                    else:
                        _act(P, st["sp16"], st["e"], AF.Ln, st["ke"], st["ksp16"], bias=1.0)
                    yield
                    P.op("pe", lambda e: e.matmul(pb[br][:], cm[:, 0, :], st["sp16"], start=True, stop=first),
                         ["cm"] + st["ksp16"], pk(br))
                    if not first:
                        P.op("pe", lambda e: e.matmul(pb[br][:], ones[:], st["S16"], start=False, stop=True),
                             ["ones"] + st["kS16"], pk(br))
                    yield
                    _act(P, st["er"], pb[br][:], AF.Exp, pk(br), st["ker"], scale=-1.0)
                    yield
                    P.op("dve", lambda e: e.tensor_tensor(out=st["A16"], in0=st["e"], in1=st["er"], op=ALU.mult),
                         st["ke"] + st["ker"], st["kA16"])
                    if di >= 0:
                        P.op("dve", lambda e: e.tensor_tensor(out=st["A16"], in0=st["A16"], in1=sbm[:, di, :], op=ALU.mult),
                             st["kA16"] + ["sbm"], st["kA16"])
                    yield
                    P.op("pe", lambda e: e.matmul(pb[bo][:], VV[:, kb, :], st["A16"], start=first, stop=(kb == 0)),
                         hk(16, 24) + st["kA16"], pk(bo))
                    if kb > 0:
                        if first:
                            P.op("pool", lambda e: e.tensor_copy(out=st["S32"], in_=st["sp16"]), st["ksp16"], st["kS32"])
                        else:
                            P.op("pool", lambda e: e.tensor_tensor(out=st["S32"], in0=st["S32"], in1=st["sp16"], op=ALU.add),
                                 st["ksp16"] + st["kS32"], st["kS32"])
                        P.op("pool", lambda e: e.tensor_copy(out=st["S16"], in_=st["S32"]), st["kS32"], st["kS16"])
                    if kb == 0:
                        P.op("act", lambda e: e.copy(out=OT[:, G * TG:(G + 1) * TG], in_=pb[bo][:]),
                             pk(bo), hk(24 + G, 25 + G))
                    yield

                def run_lockstep(gens):
                    gens = list(gens)
                    while gens:
                        nxt = []
                        for g_ in gens:
                            try:
                                next(g_)
                                nxt.append(g_)
                            except StopIteration:
                                pass
                        gens = nxt

                for h in range(8):
                    P.dma("sp", lambda e, h=h: e.dma_start(out=KT[:, 0:L], in_=kT_d[h]), ["kT_d"], hk(0, 8), key="ld0")
                    P.dma("sp", lambda e, h=h: e.dma_start(out=QT[:, 0:L], in_=qT_d[h]), ["qT_d"], hk(8, 16), key="ld1")
                    P.dma("sp", lambda e, h=h: e.dma_start(
                        out=VV[:, 0:nb, :], in_=v_d[:, h * 128:(h + 1) * 128].rearrange("(kb p) e -> p kb e", p=128)),
                        ["v_d"], hk(16, 24), key="ld2")
                    nG = L // TG
                    for G0 in range(0, nG, 4):
                        Gs = list(range(G0, min(G0 + 4, nG)))
                        for kb in range(4 * Gs[-1] + 3, -1, -1):
                            act_ = [(si_, G) for si_, G in enumerate(Gs) if 4 * G + 3 >= kb]
                            run_lockstep([step(streams[si_], G, kb, kb == 4 * G + 3) for si_, G in act_])
                    P.dma("sp", lambda e, h=h: e.dma_start(out=mixT_d[h], in_=OT[:, 0:L]), hk(24, 32), ["mixT_d"], key="st4")

            def even_m3(li, ei, g, src):
                mv = mixT_d.rearrange("c p t -> p c t")
                P.dma("sp", lambda e: e.dma_start(out=xn[:], in_=mv[:, :, g * TG:(g + 1) * TG]),
                      ["mixT_d"], [("xn", c) for c in range(NCH)], key="ld3")
                load_h(src, g)
                dense_out(xn, "xn", NCH, w_out[ei])
                post_residual(li, 1)
                store_h(hT, g)

            def even_mixer(li, ei, src):
                P.dma("pool", lambda e: e.dma_start(out=wgu[:], in_=wgu_in[ei]), [], ["wgu"], key="c5")
                P.op("dve", lambda e: e.memset(S32[:], 0.0), [], [("S32", h) for h in range(8)])
                P.op("dve", lambda e: e.memset(Sb[:], 0.0), [], [("Sb", h) for h in range(8)])
                for g in range(n_tg):
                    even_m1(li, ei, g, src)
                sb_attention()
                for g in range(n_tg):
                    even_m3(li, ei, g, src)


        if n_odd:
            PI = float(np.pi)
            s5_lam = B.dram_in("s5_lam", [n_odd, 3, 8192])
            s5_lamT = B.dram_in("s5_lamT", [n_odd, 128, 3, 64])
            s5_lamB = B.dram_in("s5_lamB", [n_odd, 128, 3, NCH, 64])
            s5_bT = B.dram_in("s5_bT", [n_odd, 128, 2, NCH, 64])
            s5_cw = B.dram_in("s5_cw", [n_odd, NCH, 128, 8, 128])
            s5_dT = B.dram_in("s5_dT", [128, n_odd * NCH])
            w_glu = B.dram_in("w_glu", [n_odd, 2 * D_MODEL // 128, 128, D_MODEL])
            jc_in = B.dram_in("jconst", [128, 138])
            tle_in = B.dram_in("trile", [128, 128], BF16)
            xnT_d = B.dram_tmp("xnT_d", [NCH, 128, L], BF16)
            geT_d = B.dram_tmp("geT_d", [NCH, 128, L], BF16)

            jc = B.sb(stack, "jc", [128, 138], F32)
            tle = B.sb(stack, "tle", [128, 128], BF16)
            lamT = B.sb(stack, "lamT", [128, 3, 64], F32)
            arT = B.sb(stack, "arT", [128, 64], F32)
            aiT = B.sb(stack, "aiT", [128, 64], F32)
            lbr = B.sb(stack, "lbr", [128, 64], F32)
            lbi = B.sb(stack, "lbi", [128, 64], F32)
            dsk = B.sb(stack, "dsk", [128, n_odd * NCH], F32)
            car = B.sb(stack, "car", [128, 2, 4], F32)
            lam128 = B.sb(stack, "lam128", [128, 2, 4], F32)
            zt = [B.sb(stack, "zt%d" % i, [128, 64], F32) for i in range(10)]
            lamB = B.sb(stack, "lamB", [128, 3, 64], F32)
            bTs = B.sb(stack, "bTs", [128, 2, 64], F32)
            Bblk = B.sb(stack, "Bblk", [128, 2, 8, 64], BF16)
            Cw = B.sb(stack, "Cw", [128, 8, 128], BF16)
            k4 = [B.sb(stack, "k4_%d" % i, [128, 4], F32) for i in range(4)]
            ytmp = B.sb(stack, "ytmp", [128, 128], F32)
            Ptab = yS[:, 4:6, :]
            Qtab = yS[:, 6:8, :]
            lb3 = yS[:, 12:15, :]
            u1b, u2b = yS[:, 15, :], yS[:, 3, :]
            pmg = yS[:, 2, :]
            jrow = jc[:, 0:128]
            jcol = jc[:, 128:129]

            if not (_SKIP & 1):
                P.dma("sp", lambda e: e.dma_start(out=jc[:], in_=jc_in[:, :]), [], ["jc"], key="c6")
                P.dma("sp", lambda e: e.dma_start(out=tle[:], in_=tle_in[:, :]), [], ["tle"], key="c7")
                P.dma("sp", lambda e: e.dma_start(out=dsk[:], in_=s5_dT[:, :]), [], ["dsk"], key="c8")

            def dv(out, in0, in1, op, r, w):
                P.op("dve", lambda e: e.tensor_tensor(out=out, in0=in0, in1=in1, op=op), r, w)

            def ds(out, in0, s1, s2, op0, op1, r, w, eng="dve"):
                if s2 is None:
                    P.op(eng, lambda e: e.tensor_scalar(out=out, in0=in0, scalar1=s1, scalar2=None, op0=op0), r, w)
                else:
                    P.op(eng, lambda e: e.tensor_scalar(out=out, in0=in0, scalar1=s1, scalar2=s2, op0=op0, op1=op1), r, w)

            MAGIC = 12582912.0

            def red_sin(buf, tb, kb, kt):
                ds(tb, buf, 1.0 / (2 * PI), MAGIC, ALU.mult, ALU.add, kb, kt)
                ds(tb, tb, -MAGIC, None, ALU.add, None, kt, kt)
                P.op("dve", lambda e: e.scalar_tensor_tensor(out=buf, in0=tb, scalar=-2 * PI, in1=buf,
                                                             op0=ALU.mult, op1=ALU.add), kt + kb, kb)
                ds(buf, buf, -3.14159, 3.14159, ALU.max, ALU.min, kb, kb)
                _act(P, buf, buf, AF.Sin, kb, kb)

            def sincos(o1, o2, ang_in, scal, rk, wk1, wk2, tb=None, kt=None):
                if tb is None:
                    tb, kt = zt[9][:], [("zt", 9)]
                ds(o1, ang_in, scal, None, ALU.mult, None, rk, [wk1])
                ds(o2, o1, 1.5 * PI, None, ALU.add, None, [wk1], [wk2])
                ds(o1, o1, PI, None, ALU.add, None, [wk1], [wk1])
                red_sin(o1, tb, [wk1], kt)
                red_sin(o2, tb, [wk2], kt)

            def odd_mixer(li, oi, src):
                xv = xnT_d.rearrange("c p t -> p c t")
                for g in range(n_tg if not (_SKIP & 2) else 0):
                    load_h(src, g)
                    prenorm(li, 0)
                    P.dma("sp", lambda e, g=g: e.dma_start(out=xv[:, :, g * TG:(g + 1) * TG], in_=xn[:]),
                          [("xn", c) for c in range(NCH)], ["xnT_d"], key="st5")
                def trivial_o3():
                    for g in range(n_tg if not (_SKIP & 4) else 0):
                        load_h(src, g)
                        store_h(hT, g)
                if _STOP == 1:
                    return trivial_o3()
                P.dma("sp", lambda e: e.dma_start(out=lamT[:], in_=s5_lamT[oi]), [], ["lamT"], key="c9")
                _act(P, zt[0][:], lamT[:, 2, :], AF.Exp, ["lamT"], [("zt", 0)])
                dv(arT[:], zt[0][:], lamT[:, 0, :], ALU.mult, [("zt", 0), "lamT"], ["arT"])
                dv(aiT[:], zt[0][:], lamT[:, 1, :], ALU.mult, [("zt", 0), "lamT"], ["aiT"])
                _act(P, zt[1][:], arT[:], AF.Exp, ["arT"], [("zt", 1)])
                sincos(zt[2][:], zt[3][:], aiT[:], 1.0, ["aiT"], ("zt", 2), ("zt", 3))
                P.op("dve", lambda e: e.scalar_tensor_tensor(out=lbr[:], in0=zt[1][:], scalar=-1.0, in1=zt[3][:],
                                                             op0=ALU.mult, op1=ALU.mult), [("zt", 1), ("zt", 3)], ["lbr"])
                P.op("dve", lambda e: e.scalar_tensor_tensor(out=lbi[:], in0=zt[1][:], scalar=-1.0, in1=zt[2][:],
                                                             op0=ALU.mult, op1=ALU.mult), [("zt", 1), ("zt", 2)], ["lbi"])
                if _STOP == 2:
                    return trivial_o3()
                UC = hid[:, 0:8, :].rearrange("p a b -> p (a b)")
                GS = hid[:, 8:16, :].rearrange("p a b -> p (a b)")
                yk = lambda a, b: [("yS", j) for j in range(a, b)]
                for c in range(NCH):
                    c4 = slice(c * 4, (c + 1) * 4)
                    P.dma("sp", lambda e, c=c: e.dma_start(out=lamB[:], in_=s5_lamB[oi][:, :, c, :]), [], ["lamB"], key="ld5")
                    P.dma("sp", lambda e, c=c: e.dma_start(out=bTs[:], in_=s5_bT[oi][:, :, c, :]), [], ["bTs"], key="ld6")
                    P.dma("pool", lambda e, c=c: e.dma_start(out=Cw[:], in_=s5_cw[oi][c]), [], ["Cw"], key="ld7")
                    for gp_ in range(4):
                        P.op("act", lambda e, gp_=gp_: e.mul(out=Cw[:, gp_ * 2 + 1, :], in_=Cw[:, gp_ * 2 + 1, :], mul=-1.0),
                             ["Cw"], ["Cw"])
                    P.dma("sp", lambda e, c=c: e.dma_start(out=UC[:, 0:L], in_=xnT_d[c]), ["xnT_d"], hk(0, 8), key="ld8")
                    z = lambda i: zt[i][:]
                    zk = lambda i: ("zt", i)
                    _act(P, z(0), lamB[:, 2, :], AF.Exp, ["lamB"], [zk(0)])
                    dv(z(1), z(0), lamB[:, 0, :], ALU.mult, [zk(0), "lamB"], [zk(1)])
                    dv(z(2), z(0), lamB[:, 1, :], ALU.mult, [zk(0), "lamB"], [zk(2)])
                    _act(P, z(1), z(1), AF.Exp, [zk(1)], [zk(1)])
                    sincos(z(3), z(4), z(2), 1.0, [zk(2)], zk(3), zk(4))
                    P.op("dve", lambda e: e.scalar_tensor_tensor(out=z(5), in0=z(1), scalar=-1.0, in1=z(4),
                                                                 op0=ALU.mult, op1=ALU.mult), [zk(1), zk(4)], [zk(5)])
                    P.op("dve", lambda e: e.scalar_tensor_tensor(out=z(6), in0=z(1), scalar=-1.0, in1=z(3),
                                                                 op0=ALU.mult, op1=ALU.mult), [zk(1), zk(3)], [zk(6)])
                    ds(z(5), z(5), -1.0, None, ALU.add, None, [zk(5)], [zk(5)])
                    dv(z(0), lamB[:, 0, :], lamB[:, 0, :], ALU.mult, ["lamB"], [zk(0)])
                    dv(z(1), lamB[:, 1, :], lamB[:, 1, :], ALU.mult, ["lamB"], [zk(1)])
                    dv(z(0), z(0), z(1), ALU.add, [zk(0), zk(1)], [zk(0)])
                    P.op("dve", lambda e: e.reciprocal(out=z(0), in_=z(0)), [zk(0)], [zk(0)])
                    dv(z(1), z(5), lamB[:, 0, :], ALU.mult, [zk(5), "lamB"], [zk(1)])
                    dv(z(2), z(6), lamB[:, 1, :], ALU.mult, [zk(6), "lamB"], [zk(2)])
                    dv(z(1), z(1), z(2), ALU.add, [zk(1), zk(2)], [zk(1)])
                    dv(z(7), z(1), z(0), ALU.mult, [zk(1), zk(0)], [zk(7)])
                    dv(z(1), z(6), lamB[:, 0, :], ALU.mult, [zk(6), "lamB"], [zk(1)])
                    dv(z(2), z(5), lamB[:, 1, :], ALU.mult, [zk(5), "lamB"], [zk(2)])
                    dv(z(1), z(1), z(2), ALU.subtract, [zk(1), zk(2)], [zk(1)])
                    dv(z(8), z(1), z(0), ALU.mult, [zk(1), zk(0)], [zk(8)])
                    dv(z(1), z(7), bTs[:, 0, :], ALU.mult, [zk(7), "bTs"], [zk(1)])
                    dv(z(2), z(8), bTs[:, 1, :], ALU.mult, [zk(8), "bTs"], [zk(2)])
                    dv(z(3), z(1), z(2), ALU.subtract, [zk(1), zk(2)], [zk(3)])
                    dv(z(1), z(7), bTs[:, 1, :], ALU.mult, [zk(7), "bTs"], [zk(1)])
                    dv(z(2), z(8), bTs[:, 0, :], ALU.mult, [zk(8), "bTs"], [zk(2)])
                    dv(z(4), z(1), z(2), ALU.add, [zk(1), zk(2)], [zk(4)])
                    for ri in range(2):
                        for gg in range(8):
                            ds(Bblk[:, ri, gg, :], z(3 + ri), jc[:, 130 + gg:131 + gg], None, ALU.mult, None,
                               [zk(3 + ri), "jc"], ["Bblk"])
                    if _STOP == 3:
                        continue
                    lv = s5_lam[oi].rearrange("k (c n) -> k c n", c=NCH)
                    for k3 in range(3):
                        if _DBG == 2:
                            P.dma("sp", lambda e, k3=k3, c=c: e.dma_start(
                                out=lb3[0:1, k3, :], in_=lv[k3, c:c + 1, :]),
                                [], yk(12 + k3, 13 + k3), key="ld9_%d" % k3)
                        else:
                            P.dma("pool" if _DBG == 1 else "sp", lambda e, k3=k3, c=c: e.dma_start(
                                out=lb3[:, k3, :], in_=lv[k3, c:c + 1, :].partition_broadcast(128)),
                                [], yk(12 + k3, 13 + k3), key="ld9_%d" % k3)
                    _act(P, lb3[:, 2, :], lb3[:, 2, :], AF.Exp, yk(14, 15), yk(14, 15))
                    dv(lb3[:, 0, :], lb3[:, 0, :], lb3[:, 2, :], ALU.mult, yk(12, 13) + yk(14, 15), yk(12, 13))
                    dv(lb3[:, 1, :], lb3[:, 1, :], lb3[:, 2, :], ALU.mult, yk(13, 14) + yk(14, 15), yk(13, 14))
                    ds(pmg, lb3[:, 0, :], jcol, None, ALU.mult, None, yk(12, 13) + ["jc"], yk(2, 3))
                    _act(P, pmg, pmg, AF.Exp, yk(2, 3), yk(2, 3), scale=-1.0)
                    sincos(u1b, u2b, lb3[:, 1, :], jcol, yk(13, 14) + ["jc"], ("yS", 15), ("yS", 3), tb=sg[0][:], kt=[("sg", 0)])
                    P.op("dve", lambda e: e.scalar_tensor_tensor(out=Ptab[:, 0, :], in0=pmg, scalar=-1.0, in1=u2b,
                                                                 op0=ALU.mult, op1=ALU.mult), yk(2, 4), yk(4, 5))
                    dv(Ptab[:, 1, :], pmg, u1b, ALU.mult, yk(2, 3) + yk(15, 16), yk(5, 6))
                    for gp in range(4):
                        cg = c * 4 + gp
                        qs = slice(gp * 128, (gp + 1) * 128)
                        ds(pmg[:, qs], jrow, arT[:, cg:cg + 1], None, ALU.mult, None, ["jc", "arT"], yk(2, 3))
                        ds(u1b[:, qs], jrow, aiT[:, cg:cg + 1], None, ALU.mult, None, ["jc", "aiT"], yk(15, 16))
                    _act(P, pmg, pmg, AF.Exp, yk(2, 3), yk(2, 3))
                    ds(u2b, u1b, 1.5 * PI, None, ALU.add, None, yk(15, 16), yk(3, 4))
                    ds(u1b, u1b, PI, None, ALU.add, None, yk(15, 16), yk(15, 16))
                    red_sin(u1b, sg[0][:], yk(15, 16), [("sg", 0)])
                    red_sin(u2b, sg[0][:], yk(3, 4), [("sg", 0)])
                    P.op("dve", lambda e: e.scalar_tensor_tensor(out=Qtab[:, 0, :], in0=pmg, scalar=-1.0, in1=u2b,
                                                                 op0=ALU.mult, op1=ALU.mult), yk(2, 4), yk(6, 7))
                    P.op("dve", lambda e: e.scalar_tensor_tensor(out=Qtab[:, 1, :], in0=pmg, scalar=-1.0, in1=u1b,
                                                                 op0=ALU.mult, op1=ALU.mult), yk(2, 3) + yk(15, 16), yk(7, 8))
                    if _STOP == 4:
                        continue
                    q127r = Qtab[:, 0, :].rearrange("p (g j) -> p g j", j=128)[:, :, 127]
                    q127i = Qtab[:, 1, :].rearrange("p (g j) -> p g j", j=128)[:, :, 127]
                    dv(k4[0][:], lbr[:, c4], q127r, ALU.mult, ["lbr"] + yk(6, 7), [("k4", 0)])
                    dv(k4[1][:], lbi[:, c4], q127i, ALU.mult, ["lbi"] + yk(7, 8), [("k4", 1)])
                    dv(k4[2][:], lbr[:, c4], q127i, ALU.mult, ["lbr"] + yk(7, 8), [("k4", 2)])
                    dv(k4[3][:], lbi[:, c4], q127r, ALU.mult, ["lbi"] + yk(6, 7), [("k4", 3)])
                    dv(lam128[:, 0, :], k4[0][:], k4[1][:], ALU.subtract, [("k4", 0), ("k4", 1)], ["lam128"])
                    dv(lam128[:, 1, :], k4[2][:], k4[3][:], ALU.add, [("k4", 2), ("k4", 3)], ["lam128"])
                    P.op("dve", lambda e: e.memset(car[:], 0.0), [], ["car"])
                    def s5_front(k, c=c, c4=c4):
                        ks = slice(k * 128, (k + 1) * 128)
                        for ri in range(2):
                            P.op("pe", lambda e, ri=ri, ks=ks: e.matmul(
                                pb[1 + ri][:], UC[:, ks], Bblk[:, ri, :, :].rearrange("p a b -> p (a b)"),
                                start=True, stop=True), hk(0, 8) + ["Bblk"], pk(1 + ri))
                        t1, t2 = sg[0][:], sg[1][:]
                        dv(t1, pb[1][:], Ptab[:, 0, :], ALU.mult, pk(1) + yk(4, 5), [("sg", 0)])
                        dv(t2, pb[2][:], Ptab[:, 1, :], ALU.mult, pk(2) + yk(5, 6), [("sg", 1)])
                        dv(xt16[:, 0, :], t1, t2, ALU.subtract, [("sg", 0), ("sg", 1)], [("xt16", 0)])
                        dv(t1, pb[2][:], Ptab[:, 0, :], ALU.mult, pk(2) + yk(4, 5), [("sg", 0)])
                        dv(t2, pb[1][:], Ptab[:, 1, :], ALU.mult, pk(1) + yk(5, 6), [("sg", 1)])
                        dv(xt16[:, 1, :], t1, t2, ALU.add, [("sg", 0), ("sg", 1)], [("xt16", 1)])
                        cb = 3 if k % 2 == 0 else 6
                        for ri in range(2):
                            for gp in range(4):
                                P.op("pe", lambda e, ri=ri, gp=gp, cb=cb: e.matmul(
                                    pb[cb + ri][:, gp * 128:(gp + 1) * 128], xt16[:, ri, gp * 128:(gp + 1) * 128], tle[:],
                                    start=True, stop=True), [("xt16", ri), "tle"], pk(cb + ri))
                    def s5_mid(k, c=c, c4=c4):
                        par = k % 2
                        ccr_, cci_ = yS[:, 8 + 2 * par, :], yS[:, 9 + 2 * par, :]
                        kcr, kci = yk(8 + 2 * par, 9 + 2 * par), yk(9 + 2 * par, 10 + 2 * par)
                        for gp in range(4):
                            qs = slice(gp * 128, (gp + 1) * 128)
                            cb = 3 if par == 0 else 6
                            _act(P, ccr_[:, qs], pb[cb][:, qs], AF.Identity, pk(cb) + ["car"], kcr, bias=car[:, 0, gp:gp + 1])
                            _act(P, cci_[:, qs], pb[cb + 1][:, qs], AF.Identity, pk(cb + 1) + ["car"], kci, bias=car[:, 1, gp:gp + 1])
                        cr127 = ccr_.rearrange("p (g j) -> p g j", j=128)[:, :, 127]
                        ci127 = cci_.rearrange("p (g j) -> p g j", j=128)[:, :, 127]
                        dv(k4[0][:], lam128[:, 0, :], cr127, ALU.mult, ["lam128"] + kcr, [("k4", 0)])
                        dv(k4[1][:], lam128[:, 1, :], ci127, ALU.mult, ["lam128"] + kci, [("k4", 1)])
                        dv(k4[2][:], lam128[:, 0, :], ci127, ALU.mult, ["lam128"] + kci, [("k4", 2)])
                        dv(k4[3][:], lam128[:, 1, :], cr127, ALU.mult, ["lam128"] + kcr, [("k4", 3)])
                        dv(car[:, 0, :], k4[0][:], k4[1][:], ALU.subtract, [("k4", 0), ("k4", 1)], ["car"])
                        dv(car[:, 1, :], k4[2][:], k4[3][:], ALU.add, [("k4", 2), ("k4", 3)], ["car"])
                    def s5_back(k, c=c, c4=c4):
                        ks = slice(k * 128, (k + 1) * 128)
                        par = k % 2
                        ccr_, cci_ = yS[:, 8 + 2 * par, :], yS[:, 9 + 2 * par, :]
                        kcr, kci = yk(8 + 2 * par, 9 + 2 * par), yk(9 + 2 * par, 10 + 2 * par)
                        t3, t4 = tmp[0][:], tmp[1][:]
                        pv = lambda out, in0, in1, op, r, w: P.op(
                            "pool", lambda e: e.tensor_tensor(out=out, in0=in0, in1=in1, op=op), r, w)
                        pv(t3, ccr_, Qtab[:, 0, :], ALU.mult, kcr + yk(6, 7), [("tmp", 0)])
                        pv(t4, cci_, Qtab[:, 1, :], ALU.mult, kci + yk(7, 8), [("tmp", 1)])
                        pv(sT16[:, 0, :], t3, t4, ALU.subtract, [("tmp", 0), ("tmp", 1)], [("sT16", 0)])
                        pv(t3, cci_, Qtab[:, 0, :], ALU.mult, kci + yk(6, 7), [("tmp", 0)])
                        pv(t4, ccr_, Qtab[:, 1, :], ALU.mult, kcr + yk(7, 8), [("tmp", 1)])
                        pv(sT16[:, 1, :], t3, t4, ALU.add, [("tmp", 0), ("tmp", 1)], [("sT16", 1)])
                        n = 0
                        for gp in range(4):
                            for ri in range(2):
                                P.op("pe", lambda e, gp=gp, ri=ri, n=n: e.matmul(
                                    pb[5][:, 0:128], Cw[:, gp * 2 + ri, :], sT16[:, ri, gp * 128:(gp + 1) * 128],
                                    start=(n == 0), stop=(n == 7)), ["Cw", ("sT16", ri)], pk(5, 0))
                                n += 1
                        P.op("dve", lambda e, ks=ks, c=c: e.scalar_tensor_tensor(
                            out=ytmp[:], in0=UC[:, ks], scalar=dsk[:, oi * NCH + c:oi * NCH + c + 1], in1=pb[5][:, 0:128],
                            op0=ALU.mult, op1=ALU.add), hk(0, 8) + ["dsk"] + pk(5, 0), ["ytmp"])
                        _act(P, GS[:, ks], ytmp[:], AF.Gelu, ["ytmp"], hk(8 + k // 4, 9 + k // 4))
                    nk = L // 128
                    s5_front(0)
                    for k in range(nk):
                        if k + 1 < nk:
                            s5_front(k + 1)
                        s5_mid(k)
                        if k >= 1:
                            s5_back(k - 1)
                    s5_back(nk - 1)
                    if _STOP:
                        continue
                    P.dma("sp", lambda e, c=c: e.dma_start(out=geT_d[c], in_=GS[:, 0:L]), hk(8, 16), ["geT_d"], key="st6")
                if _STOP:
                    return trivial_o3()
                gv = geT_d.rearrange("c p t -> p c t")
                wv = w_glu[oi]
                for g in range(n_tg):
                    P.dma("sp", lambda e, g=g: e.dma_start(out=xn[:], in_=gv[:, :, g * TG:(g + 1) * TG]),
                          ["geT_d"], [("xn", c) for c in range(NCH)], key="ld3")
                    load_h(src, g)
                    for j in range(NCH):
                        i = rot("wi", NWI)
                        P.dma("pool", lambda e, i=i, j=j: [
                            e.dma_start(out=wi[i][:, 0, :, :], in_=wv[j].rearrange("p (c f) -> p c f", c=NCH)),
                            e.dma_start(out=wi[i][:, 1, :, :], in_=wv[NCH + j].rearrange("p (c f) -> p c f", c=NCH))],
                            [], [("wi", i)], key="wi%d" % i, ndma=2)
                        bv = (2 * j) % 4 + 4
                        bgt = bv + 1
                        for a, bank in ((0, bv), (1, bgt)):
                            for c in range(NCH):
                                P.op("pe", lambda e, i=i, c=c, a=a, bank=bank: e.matmul(
                                    pb[bank][:], wi[i][:, a, c, :], xn[:, c, :], start=(c == 0), stop=(c == NCH - 1)),
                                    [("wi", i), ("xn", c)], pk(bank))
                        si = rot("sg", 2)
                        _act(P, sg[si][:], pb[bgt][:], AF.Sigmoid, pk(bgt), [("sg", si)])
                        P.op("dve", lambda e, si=si, j=j, bv=bv: e.tensor_tensor(
                            out=yS[:, j, :], in0=sg[si][:], in1=pb[bv][:], op=ALU.mult),
                            [("sg", si)] + pk(bv), [("yS", j)])
                    post_residual(li, 1)
                    store_h(hT, g)

        ei_map, oi_map = {}, {}
        for li, kind in enumerate(layers):
            if kind == "even":
                ei_map[li] = len(ei_map)
            elif kind == "odd":
                oi_map[li] = len(oi_map)
        for li, kind in enumerate(layers):
            src = xT if li == 0 else hT
            last = (li == depth - 1)
            if kind == "even":
                even_mixer(li, ei_map[li], src)
                src = hT
            elif kind == "odd":
                odd_mixer(li, oi_map[li], src)
                src = hT
            for g in range(n_tg):
                load_h(src, g)
                prenorm(li, 2)
                ffn(li)
                post_residual(li, 3)
                store_h(yT if last else hT, g)

        P.emit(stack)
    nc.in_names_ = list(B.in_names)
    return nc


def _bf16(a):
    return np.asarray(a, dtype=np.float32).astype(ml_dtypes.bfloat16)


def host_consts():
    i = np.arange(128)
    cm = np.zeros((128, 4, 128), np.float32)
    cm[:, 0, :] = (i[:, None] >= i[None, :])
    cm[:, 1, :] = (i[:, None] <= i[None, :]) * (-1.0 / 16)
    cm[:, 2, :] = (i[:, None] > i[None, :]) * (-1.0 / 16)
    cm[:, 3, :] = (i[:, None] <= i[None, :])
    sbm = np.zeros((128, 4, TG), np.float32)
    for d in range(4):
        for tb in range(4):
            if tb > d:
                sbm[:, d, tb * 128:(tb + 1) * 128] = 1.0
            elif tb == d:
                sbm[:, d, tb * 128:(tb + 1) * 128] = (i[:, None] < i[None, :])
    return {"cm128": _bf16(cm), "sbmask": _bf16(sbm), "ones_bf": _bf16(np.ones((128, 128)))}


def _tile_w(w):
    n, k, N = w.shape
    t = w.reshape(n, k // 128, 128, N // 128, 128).transpose(0, 3, 2, 1, 4)
    return np.ascontiguousarray(t).reshape(n, N // 128, 128, k)


def host_layout(inputs, layers):
    depth = len(layers)
    f = lambda k: np.asarray(inputs[k], dtype=np.float32)
    g = f("norm_gains")[:depth]
    m = {"gains": np.ascontiguousarray(g.reshape(depth * 4, NCH, 128).transpose(2, 0, 1).reshape(128, depth * 4 * NCH)),
         "w_ffn_in": _tile_w(f("w_ffn_in")[:depth]), "w_ffn_out": f("w_ffn_out")[:depth]}
    ne = sum(1 for l in layers if l == "even")
    no = sum(1 for l in layers if l == "odd")
    if ne:
        m["w_in"] = f("w_in")[:ne]
        m["w_out"] = f("w_out")[:ne]
        m["wgu"] = np.ascontiguousarray(np.concatenate([f("w_gate_up")[:ne], f("b_gate")[:ne, None, :]], axis=1))
        gg = f("gla_norm_gain")[:ne]
        m["gla_gain"] = np.ascontiguousarray(gg.reshape(ne, 8, 128).transpose(2, 0, 1).reshape(128, ne * 8))
    if no:
        lre, lim, ls = f("s5_lambda_re")[:no], f("s5_lambda_im")[:no], f("s5_log_step")[:no]
        lse = np.broadcast_to(ls[:, :, None], lre.shape)
        lam3 = np.stack([lre, lim, lse], axis=1)
        m["s5_lam"] = np.ascontiguousarray(lam3.reshape(no, 3, 8192))
        t = lam3.reshape(no, 3, 64, 2, 64)
        m["s5_lamT"] = np.ascontiguousarray(t.transpose(0, 3, 4, 1, 2).reshape(no, 128, 3, 64))
        t = lam3.reshape(no, 3, NCH, 8, 1, 64)
        t = np.broadcast_to(t, (no, 3, NCH, 8, 16, 64))
        m["s5_lamB"] = np.ascontiguousarray(t.transpose(0, 3, 4, 1, 2, 5).reshape(no, 128, 3, NCH, 64))
        b2 = np.stack([f("s5_b_re")[:no], f("s5_b_im")[:no]], axis=1)
        t = b2.reshape(no, 2, NCH, 8, 64, 16)
        m["s5_bT"] = np.ascontiguousarray(t.transpose(0, 3, 5, 1, 2, 4).reshape(no, 128, 2, NCH, 64))
        c2 = np.stack([f("s5_c_re")[:no], f("s5_c_im")[:no]], axis=1)
        cw = np.zeros((no, NCH, 2, 64, 4, 2, 8, 16), np.float32)
        t = c2.reshape(no, 2, NCH, 4, 2, 16, 64)
        for gp in range(4):
            for g2 in range(2):
                cw[:, :, g2, :, gp, :, 2 * gp + g2, :] = t[:, :, :, gp, g2].transpose(0, 2, 4, 1, 3)
        m["s5_cw"] = np.ascontiguousarray(cw.reshape(no, NCH, 128, 8, 128))
        m["s5_dT"] = np.ascontiguousarray(f("s5_d")[:no].reshape(no, NCH, 128).transpose(2, 0, 1).reshape(128, no * NCH))
        m["w_glu"] = _tile_w(f("w_glu")[:no])
        jc = np.zeros((128, 138), np.float32)
        jc[:, 0:128] = np.arange(128)[None, :]
        jc[:, 128] = np.arange(128)
        jc[:, 129] = 1.0
        jc[:, 130:138] = (np.arange(128)[:, None] // 16 == np.arange(8)[None, :])
        m["jconst"] = jc
        i = np.arange(128)
        m["trile"] = _bf16((i[:, None] <= i[None, :]).astype(np.float32))
    m.update(host_consts())
    return m, no


_CACHE = {}


def kernel(**inputs):
    layers = ["even", "odd"] * (DEPTH // 2)
    x = np.asarray(inputs["x"], dtype=np.float32)
    m, _ = host_layout(inputs, layers)
    if "nc" not in _CACHE:
        _CACHE["nc"] = build(SEQ, layers)
    nc = _CACHE["nc"]
    in_maps = []
    for b in range(BATCH):
        mm = dict(m)
        mm["xT"] = np.ascontiguousarray(x[b].T)
        in_maps.append(mm)
    res = run_bass_kernel_spmd(nc, in_maps, core_ids=list(range(BATCH)))
    out = np.stack([np.ascontiguousarray(res.results[b]["yT"].T) for b in range(BATCH)], axis=0)
    return out.astype(np.float32)
```

```python
import contextlib
import numpy as np
import ml_dtypes
import concourse.bass as bass
import concourse.mybir as mybir
from concourse.bass_utils import run_bass_kernel_spmd

F32 = mybir.dt.float32
BF16 = mybir.dt.bfloat16
AF = mybir.ActivationFunctionType
ALU = mybir.AluOpType

D_MODEL = 2048
SEQ = 4096
BATCH = 2
DEPTH = 4
D_FF = 5632
IN_WIDTH = 6160
EPS = 1e-6
NCH = D_MODEL // 128
TG = 512

import os
_DBG = int(os.environ.get('ODD_DBG', '0'))
_STOP = int(os.environ.get('ODD_STOP', '0'))
_SKIP = int(os.environ.get('ODD_SKIP', '0'))
SAME_ENGINE_SYNC = True
SEM_LIMIT = int(os.environ.get('SEM_LIMIT', '8000'))


class Op:
    __slots__ = ("eng", "fn", "reads", "writes", "dma", "key", "deps", "signal",
                 "sem_i", "sem_v", "ndma")

    def __init__(self, eng, fn, reads, writes, dma=False, key=None, ndma=1):
        self.eng = eng
        self.fn = fn
        self.reads = tuple(reads)
        self.writes = tuple(writes)
        self.dma = dma
        self.key = key
        self.deps = []
        self.signal = False
        self.sem_i = None
        self.sem_v = None
        self.ndma = ndma


class Prog:
    ENGS = ("pe", "act", "dve", "pool", "sp")

    def __init__(self, nc):
        self.nc = nc
        self.ops = []
        self.last_w = {}
        self.readers = {}
        self.last_dma = {}

    def op(self, eng, fn, reads=(), writes=()):
        o = Op(eng, fn, reads, writes)
        self._track(o)
        return o

    def dma(self, eng, fn, reads=(), writes=(), key=None, ndma=1):
        o = Op(eng, fn, reads, writes, dma=True, key=key, ndma=ndma)
        prev = self.last_dma.get(key)
        if prev is not None:
            o.deps.append(prev)
        self.last_dma[key] = o
        self._track(o)
        return o

    def _track(self, o):
        deps = o.deps
        for k in o.reads:
            w = self.last_w.get(k)
            if w is not None:
                deps.append(w)
        for k in o.writes:
            w = self.last_w.get(k)
            if w is not None:
                deps.append(w)
            deps.extend(self.readers.get(k, ()))
        for k in o.reads:
            self.readers.setdefault(k, []).append(o)
        for k in o.writes:
            self.last_w[k] = o
            self.readers[k] = []
        seen = set()
        dd = []
        for d in deps:
            if d is o or id(d) in seen:
                continue
            seen.add(id(d))
            dd.append(d)
        o.deps = dd
        self.ops.append(o)

    def simulate(self, per_eng, sems_of):
        semv = {}
        pos = {e: 0 for e in self.ENGS}
        total = sum(len(v) for v in per_eng.values())
        done = 0
        while done < total:
            prog = False
            for e in self.ENGS:
                while pos[e] < len(per_eng[e]):
                    o = per_eng[e][pos[e]]
                    ok = True
                    for d in o.deps:
                        k = (sems_of[id(d)]["name"], d.sem_i)
                        if semv.get(k, 0) < d.sem_v:
                            ok = False
                            break
                    if not ok:
                        break
                    if o.dma:
                        k = (sems_of[id(o)]["name"], o.sem_i)
                        semv[k] = semv.get(k, 0) + 16 * o.ndma
                        assert semv[k] == o.sem_v, (k, semv[k], o.sem_v)
                    elif o.signal:
                        k = (sems_of[id(o)]["name"], o.sem_i)
                        semv[k] = semv.get(k, 0) + 1
                        assert semv[k] == o.sem_v, (k, semv[k], o.sem_v)
                    pos[e] += 1
                    done += 1
                    prog = True
            if not prog:
                print("DEADLOCK at", {e: pos[e] for e in self.ENGS})
                for e in self.ENGS:
                    if pos[e] < len(per_eng[e]):
                        o = per_eng[e][pos[e]]
                        print(" ", e, "reads", o.reads[:4], "writes", o.writes[:4],
                              [(sems_of[id(d)]["name"], d.sem_i, d.sem_v, d.eng, d.writes[:2]) for d in o.deps][:6])
                raise RuntimeError("deadlock")
        print("PROG_SIM ok: %d ops, sems %d" % (total, len(semv)))

    def emit(self, stack):
        nc = self.nc
        for o in self.ops:
            nd = []
            for d in o.deps:
                if not d.dma and d.eng == o.eng:
                    if d.eng == "pe" or not SAME_ENGINE_SYNC or o.dma:
                        continue
                d.signal = True
                nd.append(d)
            o.deps = nd
        streams = {}

        def stream(name):
            s = streams.get(name)
            if s is None:
                s = {"sems": [], "val": 0, "name": name}
                streams[name] = s
            return s

        def bump(s, inc):
            if not s["sems"] or s["val"] + inc > SEM_LIMIT:
                s["sems"].append(stack.enter_context(
                    nc.semaphore("s_%s_%d" % (s["name"], len(s["sems"])))))
                s["val"] = 0
            s["val"] += inc
            return len(s["sems"]) - 1, s["val"]

        sems_of = {}
        for o in self.ops:
            if o.dma:
                s = stream("d_" + str(o.key))
                o.sem_i, o.sem_v = bump(s, 16 * o.ndma)
                sems_of[id(o)] = s
            elif o.signal:
                s = stream("e_" + o.eng)
                o.sem_i, o.sem_v = bump(s, 1)
                sems_of[id(o)] = s
        final = []
        for name, s in streams.items():
            final.append((s, len(s["sems"]) - 1, s["val"]))

        per_eng = {e: [o for o in self.ops if o.eng == e] for e in self.ENGS}
        if os.environ.get("PROG_SIM"):
            self.simulate(per_eng, sems_of)
        block = stack.enter_context(nc.Block())

        def run(eng_name, eng):
            waited = {}
            for o in per_eng[eng_name]:
                for d in o.deps:
                    s = sems_of[id(d)]
                    sem = s["sems"][d.sem_i]
                    k = (s["name"], d.sem_i)
                    if waited.get(k, 0) >= d.sem_v:
                        continue
                    waited[k] = d.sem_v
                    eng.wait_ge(sem, d.sem_v)
                r = o.fn(eng)
                if o.dma:
                    sem = sems_of[id(o)]["sems"][o.sem_i]
                    if not isinstance(r, (list, tuple)):
                        r = [r]
                    assert len(r) == o.ndma, (len(r), o.ndma)
                    for ins in r:
                        ins.then_inc(sem, 16)
                elif o.signal:
                    sem = sems_of[id(o)]["sems"][o.sem_i]
                    r.then_inc(sem, 1)
            if eng_name == "sp":
                for s, i, v in final:
                    if s["name"].startswith("d_"):
                        eng.wait_ge(s["sems"][i], v)

        @block.tensor
        def _(e):
            run("pe", e)

        @block.scalar
        def _(e):
            run("act", e)

        @block.vector
        def _(e):
            run("dve", e)

        @block.gpsimd
        def _(e):
            run("pool", e)

        @block.sync
        def _(e):
            run("sp", e)


class Builder:
    def __init__(self, L, layers):
        self.L = L
        self.layers = layers
        self.nc = bass.Bass("TRN2", target_bir_lowering=False)
        self.P = Prog(self.nc)
        self.uid = 0
        self.psum_rr = 0
        self.in_names = []

    def dram_in(self, name, shape, dt=F32):
        self.in_names.append(name)
        return self.nc.dram_tensor(name, list(shape), dt, kind="ExternalInput").ap()

    def dram_out(self, name, shape, dt=F32):
        return self.nc.dram_tensor(name, list(shape), dt, kind="ExternalOutput").ap()

    def dram_tmp(self, name, shape, dt):
        return self.nc.dram_tensor(name, list(shape), dt, kind="Internal").ap()

    def sb(self, stack, name, shape, dt):
        return stack.enter_context(self.nc.sbuf_tensor("sb_" + name, list(shape), dt))

    def ps(self, stack, name, shape, dt=F32):
        return stack.enter_context(self.nc.psum_tensor("ps_" + name, list(shape), dt))


def _act(P, out, in_, func, reads, writes, eng="act", **kw):
    return P.op(eng, lambda e: e.activation(out=out, in_=in_, func=func, **kw), reads, writes)


def build(L, layers, n_in_layers=None):
    B = Builder(L, layers)
    nc, P = B.nc, B.P
    n_tg = L // TG
    n_even = sum(1 for l in layers if l == "even")
    n_odd = sum(1 for l in layers if l == "odd")
    depth = len(layers)

    xT = B.dram_in("xT", [D_MODEL, L])
    gains = B.dram_in("gains", [128, depth * 4 * NCH])
    w_ffn_in = B.dram_in("w_ffn_in", [depth, 2 * D_FF // 128, 128, D_MODEL])
    w_ffn_out = B.dram_in("w_ffn_out", [depth, D_FF, D_MODEL])
    ones_in = B.dram_in("ones_bf", [128, 128], BF16)
    yT = B.dram_out("yT", [D_MODEL, L])
    hT = B.dram_tmp("hT", [D_MODEL, L], F32)

    stack = contextlib.ExitStack()
    with stack:
        hA = B.sb(stack, "hA", [128, NCH, TG], F32)
        xn = B.sb(stack, "xn", [128, NCH, TG], BF16)
        hid = B.sb(stack, "hid", [128, D_FF // 128, TG], BF16)
        yS = B.sb(stack, "yS", [128, NCH, TG], F32)
        NWI = 2
        wi_flat = [B.sb(stack, "wi%d" % i, [128, 4096], BF16) for i in range(NWI)]
        wi = [w[:].rearrange("p (a c f) -> p a c f", a=2, c=NCH) for w in wi_flat]
        wiB = [w[:].rearrange("p (c n) -> p c n", c=NCH) for w in wi_flat]
        NWO = 2
        wo = [B.sb(stack, "wo%d" % i, [128, 4, 1024], BF16) for i in range(NWO)]
        sq = [B.sb(stack, "sq%d" % i, [128, TG], BF16) for i in range(2)]
        sg = [B.sb(stack, "sg%d" % i, [128, TG], F32) for i in range(2)]
        tmp = [B.sb(stack, "tmp%d" % i, [128, TG], F32) for i in range(2)]
        rstd = B.sb(stack, "rstd", [128, TG], F32)
        xt16 = B.sb(stack, "xt16", [128, 2, 512], BF16)
        sT16 = B.sb(stack, "sT16", [128, 2, 512], BF16)
        gsb = B.sb(stack, "gsb", [128, depth * 4 * NCH], F32)
        ones = B.sb(stack, "ones", [128, 128], BF16)
        pb = [B.ps(stack, "pb%d" % i, [128, TG], F32) for i in range(8)]

        P.dma("sp", lambda e: e.dma_start(out=gsb[:], in_=gains[:, :]), [], ["gsb"], key="c0")
        P.dma("sp", lambda e: e.dma_start(out=ones[:], in_=ones_in[:, :]), [], ["ones"], key="c1")

        def gain(layer, k, c):
            i = (layer * 4 + k) * NCH + c
            return gsb[:, i:i + 1]

        cnt = {"wi": 0, "wo": 0, "sq": 0, "sg": 0, "tmp": 0, "pbk": 0}

        def hk(a, b):
            return [("hid", j) for j in range(a, b)]

        def pk(n, q=None):
            if q is None:
                return [("pb", n, k) for k in range(4)]
            return [("pb", n, q)]

        def rot(name, n):
            i = cnt[name] % n
            cnt[name] += 1
            return i

        def rms_stats(src, src_key, pbank):
            for c in range(NCH):
                i = rot("sq", 2)
                _act(P, sq[i][:], src[:, c, :], AF.Square, [(src_key, c)], [("sq", i)])
                P.op("pe", lambda e, i=i, c=c: e.matmul(pb[pbank][:], ones[:], sq[i][:],
                                                        start=(c == 0), stop=(c == NCH - 1)),
                     [("sq", i), "ones"], pk(pbank))
            P.op("dve", lambda e: e.tensor_scalar(out=rstd[:], in0=pb[pbank][:], scalar1=1.0 / D_MODEL,
                                                  scalar2=EPS, op0=ALU.mult, op1=ALU.add),
                 pk(pbank), ["rstd"])
            _act(P, rstd[:], rstd[:], AF.Sqrt, ["rstd"], ["rstd"])
            P.op("dve", lambda e: e.reciprocal(out=rstd[:], in_=rstd[:]), ["rstd"], ["rstd"])

        def load_h(src_ap, g):
            v = src_ap.rearrange("(c p) t -> p c t", p=128)
            P.dma("sp", lambda e: e.dma_start(out=hA[:], in_=v[:, :, g * TG:(g + 1) * TG]),
                  [("dram_h", g)], [("hA", c) for c in range(NCH)], key="hA")

        def store_h(dst_ap, g):
            v = dst_ap.rearrange("(c p) t -> p c t", p=128)
            P.dma("sp", lambda e: e.dma_start(out=v[:, :, g * TG:(g + 1) * TG], in_=hA[:]),
                  [("hA", c) for c in range(NCH)], [("dram_h", g)], key="hst")

        def prenorm(layer, k):
            rms_stats(hA, "hA", 0)
            for c in range(NCH):
                P.op("dve", lambda e, c=c: e.scalar_tensor_tensor(
                    out=xn[:, c, :], in0=hA[:, c, :], scalar=gain(layer, k, c), in1=rstd[:],
                    op0=ALU.mult, op1=ALU.mult), [("hA", c), "rstd", "gsb"], [("xn", c)])

        def post_residual(layer, k):
            rms_stats(yS, "yS", 0)
            for c in range(NCH):
                i = rot("tmp", 2)
                P.op("dve", lambda e, c=c, i=i: e.scalar_tensor_tensor(
                    out=tmp[i][:], in0=yS[:, c, :], scalar=gain(layer, k, c), in1=rstd[:],
                    op0=ALU.mult, op1=ALU.mult), [("yS", c), "rstd", "gsb"], [("tmp", i)])
                P.op("dve", lambda e, c=c, i=i: e.tensor_tensor(
                    out=hA[:, c, :], in0=hA[:, c, :], in1=tmp[i][:], op=ALU.add),
                    [("tmp", i), ("hA", c)], [("hA", c)])

        def dense_out(act_buf, act_key, kc, w_ap):
            wv = w_ap.rearrange("(j p) n -> p j n", p=128)
            for half in range(2):
                for j0 in range(0, kc, 4):
                    nj = min(4, kc - j0)
                    i = rot("wo", NWO)
                    P.dma("pool", lambda e, i=i, j0=j0, nj=nj, half=half: e.dma_start(
                        out=wo[i][:, 0:nj, :], in_=wv[:, j0:j0 + nj, half * 1024:(half + 1) * 1024]),
                        [], [("wo", i)], key="wo%d" % i)
                    for jj in range(nj):
                        j = j0 + jj
                        for dc in range(8):
                            P.op("pe", lambda e, i=i, jj=jj, j=j, dc=dc: e.matmul(
                                pb[dc][:], wo[i][:, jj, dc * 128:(dc + 1) * 128], act_buf[:, j, :],
                                start=(j == 0), stop=(j == kc - 1)),
                                [("wo", i), (act_key, j)], pk(dc))
                for dc in range(8):
                    c = half * 8 + dc
                    P.op("act", lambda e, c=c, dc=dc: e.copy(out=yS[:, c, :], in_=pb[dc][:]),
                         pk(dc), [("yS", c)])

        def ffn(layer):
            nf = D_FF // 128
            wv = w_ffn_in[layer]
            for j in range(nf):
                i = rot("wi", NWI)
                P.dma("pool", lambda e, i=i, j=j: [
                    e.dma_start(out=wi[i][:, 0, :, :], in_=wv[j].rearrange("p (c f) -> p c f", c=NCH)),
                    e.dma_start(out=wi[i][:, 1, :, :], in_=wv[nf + j].rearrange("p (c f) -> p c f", c=NCH))],
                    [], [("wi", i)], key="wi%d" % i, ndma=2)
                bg = (2 * j) % 4 + 4
                bu = bg + 1
                for c in range(NCH):
                    P.op("pe", lambda e, i=i, c=c, bg=bg: e.matmul(
                        pb[bg][:], wi[i][:, 0, c, :], xn[:, c, :], start=(c == 0), stop=(c == NCH - 1)),
                        [("wi", i), ("xn", c)], pk(bg))
                for c in range(NCH):
                    P.op("pe", lambda e, i=i, c=c, bu=bu: e.matmul(
                        pb[bu][:], wi[i][:, 1, c, :], xn[:, c, :], start=(c == 0), stop=(c == NCH - 1)),
                        [("wi", i), ("xn", c)], pk(bu))
                si = rot("sg", 2)
                _act(P, sg[si][:], pb[bg][:], AF.Silu, pk(bg), [("sg", si)])
                P.op("dve", lambda e, si=si, j=j, bu=bu: e.tensor_tensor(
                    out=hid[:, j, :], in0=sg[si][:], in1=pb[bu][:], op=ALU.mult),
                    [("sg", si)] + pk(bu), [("hid", j)])
            dense_out(hid, "hid", nf, w_ffn_out[layer])


        if n_even:
            w_t128 = B.dram_in("w_in_t128", [n_even, 48, 128, D_MODEL])
            w_tm = B.dram_in("w_in_tm", [n_even, 10, 128, 2 * D_MODEL])
            w_lr = B.dram_in("w_in_lr", [n_even, 128, NCH * 16])
            wgu_in = B.dram_in("wgu", [n_even, 17, 512])
            glag_in = B.dram_in("gla_gain", [128, n_even * 8])
            w_out = B.dram_in("w_out", [n_even, D_MODEL, D_MODEL])
            cm_in = B.dram_in("cm128", [128, 4, 128], BF16)
            sbm_in = B.dram_in("sbmask", [128, 4, TG], BF16)
            qT_d = B.dram_tmp("qT_d", [8, 128, L], BF16)
            kT_d = B.dram_tmp("kT_d", [8, 128, L], BF16)
            v_d = B.dram_tmp("v_d", [L, 1024], BF16)
            mixT_d = B.dram_tmp("mixT_d", [16, 128, L], BF16)

            cm = B.sb(stack, "cm", [128, 4, 128], BF16)
            sbm = B.sb(stack, "sbm", [128, 4, TG], BF16)
            la = B.sb(stack, "la", [128, 4, 512], BF16)
            sbv = B.sb(stack, "sbv", [128, 4, 256], BF16)
            lrT = B.sb(stack, "lrT", [17, TG], BF16)
            wgu = B.sb(stack, "wgu", [17, 512], BF16)
            S32 = B.sb(stack, "S32", [64, 8, 128], F32)
            Sb = B.sb(stack, "Sb", [64, 8, 128], BF16)
            dec = B.sb(stack, "dec", [64, 8], F32)
            glag = B.sb(stack, "glag", [128, n_even * 8], F32)
            khat = sT16[:, 1, :]
            qtl = B.sb(stack, "qtl", [64, 128], BF16)
            ktl = B.sb(stack, "ktl", [64, 128], BF16)
            PT = B.sb(stack, "PT", [128, 128], BF16)
            e1 = B.sb(stack, "e1", [64, 128], F32)
            e2 = B.sb(stack, "e2", [64, 128], F32)
            at_e = sg[0]
            at_sp = sg[1]
            at_er = tmp[0]
            at_S32 = tmp[1]
            at_sp16 = xt16[:, 0, :]
            at_S16 = xt16[:, 1, :]
            at_A16 = sT16[:, 0, :]

            P.dma("sp", lambda e: e.dma_start(out=cm[:], in_=cm_in[:, :, :]), [], ["cm"], key="c2")
            P.dma("sp", lambda e: e.dma_start(out=sbm[:], in_=sbm_in[:, :, :]), [], ["sbm"], key="c3")
            P.dma("sp", lambda e: e.dma_start(out=glag[:], in_=glag_in[:, :]), [], ["glag"], key="c4")
            P.op("dve", lambda e: e.memset(lrT[:], 1.0), [], ["lrT"])


            def tmv(tt):
                return hid[:, 32 + 3 * tt:35 + 3 * tt, :].rearrange("p a b -> p (a b)")

            def even_m1(li, ei, g, src):
                load_h(src, g)
                prenorm(li, 0)
                gs = slice(g * TG, (g + 1) * TG)

                def fm_proj(src_ap, f, parts):
                    i = rot("wi", NWI)
                    P.dma("pool", lambda e: e.dma_start(out=wi[i][:, 0, :, 0:f],
                                                        in_=src_ap.rearrange("p (c f) -> p c f", c=NCH)),
                          [], [("wi", i)], key="wi%d" % i)
                    for off, M, evac in parts:
                        bank = 4 + rot("pbk", 4)
                        for c in range(NCH):
                            P.op("pe", lambda e, c=c, bank=bank, off=off, M=M: e.matmul(
                                pb[bank][0:M, :], wi[i][:, 0, c, off:off + M], xn[:, c, :],
                                start=(c == 0), stop=(c == NCH - 1)),
                                [("wi", i), ("xn", c)], pk(bank))
                        evac(bank)

                wt = w_t128[ei]
                for h in range(8):
                    fm_proj(wt[h], 128, [(0, 128, lambda bank, h=h: P.op(
                        "act", lambda e: e.mul(out=hid[:, h, :], in_=pb[bank][:], mul=128.0 ** -0.5),
                        pk(bank), hk(h, h + 1)))])
                qv = qT_d.rearrange("h p t -> p h t")
                P.dma("sp", lambda e: e.dma_start(out=qv[:, :, gs], in_=hid[:, 0:8, :]), hk(0, 8), ["qT_d"], key="st0")
                for h in range(8):
                    fm_proj(wt[8 + h], 128, [(0, 128, lambda bank, h=h: P.op(
                        "act", lambda e: e.copy(out=hid[:, 8 + h, :], in_=pb[bank][:]),
                        pk(bank), hk(8 + h, 9 + h)))])
                kv = kT_d.rearrange("h p t -> p h t")
                P.dma("sp", lambda e: e.dma_start(out=kv[:, :, gs], in_=hid[:, 8:16, :]), hk(8, 16), ["kT_d"], key="st1")
                for hp in range(4):
                    fm_proj(wt[24 + hp], 128, [((h % 2) * 64, 64, lambda bank, h=h: P.op(
                        "act", lambda e: e.mul(out=hid[0:64, 16 + h, :], in_=pb[bank][0:64, :], mul=64.0 ** -0.5),
                        pk(bank), hk(16 + h, 17 + h))) for h in (2 * hp, 2 * hp + 1)])
                    fm_proj(wt[28 + hp], 128, [((h % 2) * 64, 64, lambda bank, h=h: P.op(
                        "act", lambda e: e.copy(out=hid[0:64, 24 + h, :], in_=pb[bank][0:64, :]),
                        pk(bank), hk(24 + h, 25 + h))) for h in (2 * hp, 2 * hp + 1)])
                for h in range(8):
                    fm_proj(wt[40 + h], 128, [(0, 128, lambda bank, h=h: _act(
                        P, yS[:, h, :], pb[bank][:], AF.Silu, pk(bank), [("yS", h)]))])
                fm_proj(w_lr[ei], 16, [(0, 16, lambda bank: P.op(
                    "act", lambda e: e.copy(out=lrT[0:16, :], in_=pb[bank][0:16, :]), pk(bank), ["lrT"]))])

                for blk in range(10):
                    if blk < 4:
                        col0 = 2048 + 256 * blk
                    elif blk < 8:
                        col0 = 4096 + 256 * (blk - 4)
                    else:
                        col0 = 3584 + 256 * (blk - 8)
                    i = rot("wi", NWI)
                    P.dma("pool", lambda e, i=i, blk=blk: e.dma_start(
                        out=wiB[i][:, :, :], in_=w_tm[ei][blk].rearrange("p (c n) -> p c n", c=NCH)),
                          [], [("wi", i)], key="wi%d" % i)
                    for tt in range(4):
                        bank = 4 + rot("pbk", 4)
                        for c in range(NCH):
                            P.op("pe", lambda e, c=c, i=i, tt=tt, bank=bank: e.matmul(
                                pb[bank][:, 0:256], xn[:, c, tt * 128:(tt + 1) * 128], wiB[i][:, c, :],
                                start=(c == 0), stop=(c == NCH - 1)),
                                [("wi", i), ("xn", c)], pk(bank))
                        if blk < 4:
                            P.op("act", lambda e, tt=tt, bank=bank: e.copy(out=sbv[:, tt, :], in_=pb[bank][:, 0:256]),
                                 pk(bank), ["sbv"])
                        else:
                            o0 = 256 * (blk - 4)
                            P.op("act", lambda e, tt=tt, bank=bank, o0=o0: e.copy(
                                out=tmv(tt)[:, o0:o0 + 256], in_=pb[bank][:, 0:256]),
                                pk(bank), hk(32 + 3 * tt, 35 + 3 * tt))
                    if blk < 4:
                        dv = v_d[g * TG:(g + 1) * TG, blk * 256:(blk + 1) * 256].rearrange("(tt p) n -> p tt n", p=128)
                        P.dma("sp", lambda e, dv=dv: e.dma_start(out=dv, in_=sbv[:]), ["sbv"], ["v_d"], key="st2")

                for tt in range(4):
                    ts_ = slice(tt * 128, (tt + 1) * 128)
                    P.op("pe", lambda e, ts_=ts_: e.matmul(pb[1][:, :], lrT[0:17, ts_], wgu[0:17, :], start=True, stop=True),
                         ["lrT", "wgu"], pk(1))
                    si = rot("sg", 2)
                    _act(P, sg[si][:], pb[1][:], AF.Exp, pk(1), [("sg", si)], scale=-1.0)
                    _act(P, la[:, tt, :], sg[si][:], AF.Ln, [("sg", si)], [("la", tt)], bias=1.0)
                    P.op("pe", lambda e, tt=tt: e.matmul(pb[2][:, :], cm[:, 2, :], la[:, tt, :], start=True, stop=True),
                         ["cm", ("la", tt)], pk(2))
                    sj = rot("sg", 2)
                    _act(P, sg[sj][:], pb[2][:], AF.Exp, pk(2), [("sg", sj)])
                    P.op("dve", lambda e, sj=sj, tt=tt: e.tensor_tensor(
                        out=khat, in0=sg[sj][:], in1=tmv(tt)[:, 1024:1536], op=ALU.mult),
                        [("sg", sj)] + hk(32 + 3 * tt, 35 + 3 * tt), [("sT16", 1)])
                    for h in range(8):
                        ob = 5 + h // 4
                        oq = h % 4
                        osl = slice(oq * 128, (oq + 1) * 128)
                        vh = tmv(tt)[:, h * 128:(h + 1) * 128]
                        hkv = hk(32 + 3 * tt, 35 + 3 * tt)
                        P.op("pe", lambda e, h=h, tt=tt: e.matmul(pb[3][0:64, 0:128], la[:, tt, h * 64:(h + 1) * 64],
                                                                  cm[:, 1, :], start=True, stop=True),
                             ["cm", ("la", tt)], pk(3, 0))
                        _act(P, e1[:], pb[3][0:64, 0:128], AF.Exp, pk(3, 0), ["e1"])
                        _act(P, e2[:], pb[3][0:64, 0:128], AF.Exp, pk(3, 0), ["e2"], scale=-1.0)
                        _act(P, dec[:, h:h + 1], pb[3][0:64, 127:128], AF.Exp, pk(3, 0), [("dec", h)])
                        P.op("dve", lambda e, h=h, ts_=ts_: e.tensor_tensor(
                            out=qtl[:], in0=hid[0:64, 16 + h, ts_], in1=e1[:], op=ALU.mult),
                            ["e1"] + hk(16 + h, 17 + h), ["qtl"])
                        P.op("dve", lambda e, h=h, ts_=ts_: e.tensor_tensor(
                            out=ktl[:], in0=hid[0:64, 24 + h, ts_], in1=e2[:], op=ALU.mult),
                            ["e2"] + hk(24 + h, 25 + h), ["ktl"])
                        P.op("pe", lambda e: e.matmul(pb[4][:, 0:128], ktl[:], qtl[:], start=True, stop=True),
                             ["ktl", "qtl"], pk(4, 0))
                        P.op("dve", lambda e: e.tensor_tensor(out=PT[:], in0=pb[4][:, 0:128], in1=cm[:, 3, :], op=ALU.mult),
                             pk(4, 0) + ["cm"], ["PT"])
                        P.op("pe", lambda e, ob=ob, osl=osl, vh=vh: e.matmul(pb[ob][:, osl], vh, PT[:], start=True, stop=False),
                             ["PT"] + hkv, pk(ob, oq))
                        P.op("pe", lambda e, ob=ob, osl=osl, h=h: e.matmul(pb[ob][:, osl], Sb[:, h, :], qtl[:], start=False, stop=True),
                             [("Sb", h), "qtl"], pk(ob, oq))
                        P.op("pe", lambda e, h=h, vh=vh: e.matmul(pb[7][0:64, 0:128], khat[:, h * 64:(h + 1) * 64], vh,
                                                                  start=True, stop=True),
                             [("sT16", 1)] + hkv, pk(7, 0))
                        P.op("dve", lambda e, h=h: e.scalar_tensor_tensor(
                            out=S32[:, h, :], in0=S32[:, h, :], scalar=dec[:, h:h + 1], in1=pb[7][0:64, 0:128],
                            op0=ALU.mult, op1=ALU.add), [("S32", h), ("dec", h)] + pk(7, 0), [("S32", h)])
                        P.op("act", lambda e, h=h: e.copy(out=Sb[:, h, :], in_=S32[:, h, :]), [("S32", h)], [("Sb", h)])
                    for hb in range(2):
                        ob = 5 + hb
                        i = rot("sq", 2)
                        _act(P, sq[i][:], pb[ob][:], AF.Square, pk(ob), [("sq", i)])
                        P.op("pe", lambda e, i=i: e.matmul(pb[0][:], ones[:], sq[i][:], start=True, stop=True),
                             [("sq", i), "ones"], pk(0))
                        P.op("dve", lambda e: e.tensor_scalar(out=rstd[:], in0=pb[0][:], scalar1=1.0 / 128, scalar2=EPS,
                                                              op0=ALU.mult, op1=ALU.add), pk(0), ["rstd"])
                        _act(P, rstd[:], rstd[:], AF.Sqrt, ["rstd"], ["rstd"])
                        P.op("dve", lambda e: e.reciprocal(out=rstd[:], in_=rstd[:]), ["rstd"], ["rstd"])
                        ti = rot("tmp", 2)
                        P.op("dve", lambda e, ti=ti, ob=ob: e.tensor_tensor(out=tmp[ti][:], in0=pb[ob][:], in1=rstd[:], op=ALU.mult),
                             pk(ob) + ["rstd"], [("tmp", ti)])
                        for oq in range(4):
                            h = hb * 4 + oq
                            P.op("dve", lambda e, ti=ti, oq=oq, h=h, ts_=ts_: e.scalar_tensor_tensor(
                                out=hid[:, h, ts_], in0=tmp[ti][:, oq * 128:(oq + 1) * 128],
                                scalar=glag[:, ei * 8 + h:ei * 8 + h + 1], in1=yS[:, h, ts_],
                                op0=ALU.mult, op1=ALU.mult),
                                [("tmp", ti), "glag", ("yS", h)], hk(h, h + 1))
                mv = mixT_d[8:16].rearrange("h p t -> p h t")
                P.dma("sp", lambda e: e.dma_start(out=mv[:, :, gs], in_=hid[:, 0:8, :]), hk(0, 8), ["mixT_d"], key="st3")

            def sb_attention():
                nb = L // 128
                KT = hid[:, 0:8, :].rearrange("p a b -> p (a b)")
                QT = hid[:, 8:16, :].rearrange("p a b -> p (a b)")
                VV = hid[:, 16:24, :].rearrange("p a (b e) -> p (a b) e", e=128)
                OT = hid[:, 24:32, :].rearrange("p a b -> p (a b)")
                streams = [
                    dict(e=sg[0][:], ke=[("sg", 0)], sp=sg[1][:], ksp=[("sg", 1)], er=tmp[0][:], ker=[("tmp", 0)],
                         S32=tmp[1][:], kS32=[("tmp", 1)], sp16=xt16[:, 0, :], ksp16=[("xt16", 0)],
                         S16=xt16[:, 1, :], kS16=[("xt16", 1)], A16=sT16[:, 0, :], kA16=[("sT16", 0)],
                         bz=4, br=4, bo=0)]
                for si_ in range(3):
                    y0, h0 = 4 * si_, 32 + 3 * si_
                    streams.append(dict(
                        e=yS[:, y0, :], ke=[("yS", y0)], sp=yS[:, y0 + 1, :], ksp=[("yS", y0 + 1)],
                        er=yS[:, y0 + 2, :], ker=[("yS", y0 + 2)], S32=yS[:, y0 + 3, :], kS32=[("yS", y0 + 3)],
                        sp16=hid[:, h0, :], ksp16=hk(h0, h0 + 1), S16=hid[:, h0 + 1, :], kS16=hk(h0 + 1, h0 + 2),
                        A16=hid[:, h0 + 2, :], kA16=hk(h0 + 2, h0 + 3),
                        bz=5 + si_, br=5 + si_, bo=1 + si_))

                def step(st, G, kb, first):
                    di = kb - 4 * G
                    bz, br, bo = st["bz"], st["br"], st["bo"]
                    P.op("pe", lambda e: e.matmul(pb[bz][:], KT[:, kb * 128:(kb + 1) * 128],
                                                  QT[:, G * TG:(G + 1) * TG], start=True, stop=True),
                         hk(0, 16), pk(bz))
                    yield
                    _act(P, st["e"], pb[bz][:], AF.Exp, pk(bz), st["ke"])
                    yield
                    if di >= 0:
                        _act(P, st["sp"], st["e"], AF.Ln, st["ke"], st["ksp"], bias=1.0)
                        P.op("dve", lambda e: e.tensor_tensor(out=st["sp16"], in0=st["sp"], in1=sbm[:, di, :], op=ALU.mult),
                             st["ksp"] + ["sbm"], st["ksp16"])
                    else:
                        _act(P, st["sp16"], st["e"], AF.Ln, st["ke"], st["ksp16"], bias=1.0)
                    yield
                    P.op("pe", lambda e: e.matmul(pb[br][:], cm[:, 0, :], st["sp16"], start=True, stop=first),
                         ["cm"] + st["ksp16"], pk(br))
                    if not first:
                        P.op("pe", lambda e: e.matmul(pb[br][:], ones[:], st["S16"], start=False, stop=True),
                             ["ones"] + st["kS16"], pk(br))
                    yield
                    _act(P, st["er"], pb[br][:], AF.Exp, pk(br), st["ker"], scale=-1.0)
                    yield
                    P.op("dve", lambda e: e.tensor_tensor(out=st["A16"], in0=st["e"], in1=st["er"], op=ALU.mult),
                         st["ke"] + st["ker"], st["kA16"])
                    if di >= 0:
                        P.op("dve", lambda e: e.tensor_tensor(out=st["A16"], in0=st["A16"], in1=sbm[:, di, :], op=ALU.mult),
                             st["kA16"] + ["sbm"], st["kA16"])
                    yield
                    P.op("pe", lambda e: e.matmul(pb[bo][:], VV[:, kb, :], st["A16"], start=first, stop=(kb == 0)),
                         hk(16, 24) + st["kA16"], pk(bo))
                    if kb > 0:
                        if first:
                            P.op("pool", lambda e: e.tensor_copy(out=st["S32"], in_=st["sp16"]), st["ksp16"], st["kS32"])
                        else:
                            P.op("pool", lambda e: e.tensor_tensor(out=st["S32"], in0=st["S32"], in1=st["sp16"], op=ALU.add),
                                 st["ksp16"] + st["kS32"], st["kS32"])
                        P.op("pool", lambda e: e.tensor_copy(out=st["S16"], in_=st["S32"]), st["kS32"], st["kS16"])
                    if kb == 0:
                        P.op("act", lambda e: e.copy(out=OT[:, G * TG:(G + 1) * TG], in_=pb[bo][:]),
                             pk(bo), hk(24 + G, 25 + G))
                    yield

                def run_lockstep(gens):
                    gens = list(gens)
                    while gens:
                        nxt = []
                        for g_ in gens:
                            try:
                                next(g_)
                                nxt.append(g_)
                            except StopIteration:
                                pass
                        gens = nxt

                for h in range(8):
                    P.dma("sp", lambda e, h=h: e.dma_start(out=KT[:, 0:L], in_=kT_d[h]), ["kT_d"], hk(0, 8), key="ld0")
                    P.dma("sp", lambda e, h=h: e.dma_start(out=QT[:, 0:L], in_=qT_d[h]), ["qT_d"], hk(8, 16), key="ld1")
                    P.dma("sp", lambda e, h=h: e.dma_start(
                        out=VV[:, 0:nb, :], in_=v_d[:, h * 128:(h + 1) * 128].rearrange("(kb p) e -> p kb e", p=128)),
                        ["v_d"], hk(16, 24), key="ld2")
                    nG = L // TG
                    for G0 in range(0, nG, 4):
                        Gs = list(range(G0, min(G0 + 4, nG)))
                        for kb in range(4 * Gs[-1] + 3, -1, -1):
                            act_ = [(si_, G) for si_, G in enumerate(Gs) if 4 * G + 3 >= kb]
                            run_lockstep([step(streams[si_], G, kb, kb == 4 * G + 3) for si_, G in act_])
                    P.dma("sp", lambda e, h=h: e.dma_start(out=mixT_d[h], in_=OT[:, 0:L]), hk(24, 32), ["mixT_d"], key="st4")

            def even_m3(li, ei, g, src):
                mv = mixT_d.rearrange("c p t -> p c t")
                P.dma("sp", lambda e: e.dma_start(out=xn[:], in_=mv[:, :, g * TG:(g + 1) * TG]),
                      ["mixT_d"], [("xn", c) for c in range(NCH)], key="ld3")
                load_h(src, g)
                dense_out(xn, "xn", NCH, w_out[ei])
                post_residual(li, 1)
                store_h(hT, g)

            def even_mixer(li, ei, src):
                P.dma("pool", lambda e: e.dma_start(out=wgu[:], in_=wgu_in[ei]), [], ["wgu"], key="c5")
                P.op("dve", lambda e: e.memset(S32[:], 0.0), [], [("S32", h) for h in range(8)])
                P.op("dve", lambda e: e.memset(Sb[:], 0.0), [], [("Sb", h) for h in range(8)])
                for g in range(n_tg):
                    even_m1(li, ei, g, src)
                sb_attention()
                for g in range(n_tg):
                    even_m3(li, ei, g, src)


        if n_odd:
            PI = float(np.pi)
            s5_lam = B.dram_in("s5_lam", [n_odd, 3, 8192])
            s5_lamT = B.dram_in("s5_lamT", [n_odd, 128, 3, 64])
            s5_lamB = B.dram_in("s5_lamB", [n_odd, 128, 3, NCH, 64])
            s5_bT = B.dram_in("s5_bT", [n_odd, 128, 2, NCH, 64])
            s5_cw = B.dram_in("s5_cw", [n_odd, NCH, 128, 8, 128])
            s5_dT = B.dram_in("s5_dT", [128, n_odd * NCH])
            w_glu = B.dram_in("w_glu", [n_odd, 2 * D_MODEL // 128, 128, D_MODEL])
            jc_in = B.dram_in("jconst", [128, 138])
            tle_in = B.dram_in("trile", [128, 128], BF16)
            xnT_d = B.dram_tmp("xnT_d", [NCH, 128, L], BF16)
            geT_d = B.dram_tmp("geT_d", [NCH, 128, L], BF16)

            jc = B.sb(stack, "jc", [128, 138], F32)
            tle = B.sb(stack, "tle", [128, 128], BF16)
            lamT = B.sb(stack, "lamT", [128, 3, 64], F32)
            arT = B.sb(stack, "arT", [128, 64], F32)
            aiT = B.sb(stack, "aiT", [128, 64], F32)
            lbr = B.sb(stack, "lbr", [128, 64], F32)
            lbi = B.sb(stack, "lbi", [128, 64], F32)
            dsk = B.sb(stack, "dsk", [128, n_odd * NCH], F32)
            car = B.sb(stack, "car", [128, 2, 4], F32)
            lam128 = B.sb(stack, "lam128", [128, 2, 4], F32)
            zt = [B.sb(stack, "zt%d" % i, [128, 64], F32) for i in range(10)]
            lamB = B.sb(stack, "lamB", [128, 3, 64], F32)
            bTs = B.sb(stack, "bTs", [128, 2, 64], F32)
            Bblk = B.sb(stack, "Bblk", [128, 2, 8, 64], BF16)
            Cw = B.sb(stack, "Cw", [128, 8, 128], BF16)
            k4 = [B.sb(stack, "k4_%d" % i, [128, 4], F32) for i in range(4)]
            ytmp = B.sb(stack, "ytmp", [128, 128], F32)
            Ptab = yS[:, 4:6, :]
            Qtab = yS[:, 6:8, :]
            lb3 = yS[:, 12:15, :]
            u1b, u2b = yS[:, 15, :], yS[:, 3, :]
            pmg = yS[:, 2, :]
            jrow = jc[:, 0:128]
            jcol = jc[:, 128:129]

            if not (_SKIP & 1):
                P.dma("sp", lambda e: e.dma_start(out=jc[:], in_=jc_in[:, :]), [], ["jc"], key="c6")
                P.dma("sp", lambda e: e.dma_start(out=tle[:], in_=tle_in[:, :]), [], ["tle"], key="c7")
                P.dma("sp", lambda e: e.dma_start(out=dsk[:], in_=s5_dT[:, :]), [], ["dsk"], key="c8")

            def dv(out, in0, in1, op, r, w):
                P.op("dve", lambda e: e.tensor_tensor(out=out, in0=in0, in1=in1, op=op), r, w)

            def ds(out, in0, s1, s2, op0, op1, r, w, eng="dve"):
                if s2 is None:
                    P.op(eng, lambda e: e.tensor_scalar(out=out, in0=in0, scalar1=s1, scalar2=None, op0=op0), r, w)
                else:
                    P.op(eng, lambda e: e.tensor_scalar(out=out, in0=in0, scalar1=s1, scalar2=s2, op0=op0, op1=op1), r, w)

            MAGIC = 12582912.0

            def red_sin(buf, tb, kb, kt):
                ds(tb, buf, 1.0 / (2 * PI), MAGIC, ALU.mult, ALU.add, kb, kt)
                ds(tb, tb, -MAGIC, None, ALU.add, None, kt, kt)
                P.op("dve", lambda e: e.scalar_tensor_tensor(out=buf, in0=tb, scalar=-2 * PI, in1=buf,
                                                             op0=ALU.mult, op1=ALU.add), kt + kb, kb)
                ds(buf, buf, -3.14159, 3.14159, ALU.max, ALU.min, kb, kb)
                _act(P, buf, buf, AF.Sin, kb, kb)

            def sincos(o1, o2, ang_in, scal, rk, wk1, wk2, tb=None, kt=None):
                if tb is None:
                    tb, kt = zt[9][:], [("zt", 9)]
                ds(o1, ang_in, scal, None, ALU.mult, None, rk, [wk1])
                ds(o2, o1, 1.5 * PI, None, ALU.add, None, [wk1], [wk2])
                ds(o1, o1, PI, None, ALU.add, None, [wk1], [wk1])
                red_sin(o1, tb, [wk1], kt)
                red_sin(o2, tb, [wk2], kt)

            def odd_mixer(li, oi, src):
                xv = xnT_d.rearrange("c p t -> p c t")
                for g in range(n_tg if not (_SKIP & 2) else 0):
                    load_h(src, g)
                    prenorm(li, 0)
                    P.dma("sp", lambda e, g=g: e.dma_start(out=xv[:, :, g * TG:(g + 1) * TG], in_=xn[:]),
                          [("xn", c) for c in range(NCH)], ["xnT_d"], key="st5")
                def trivial_o3():
                    for g in range(n_tg if not (_SKIP & 4) else 0):
                        load_h(src, g)
                        store_h(hT, g)
                if _STOP == 1:
                    return trivial_o3()
                P.dma("sp", lambda e: e.dma_start(out=lamT[:], in_=s5_lamT[oi]), [], ["lamT"], key="c9")
                _act(P, zt[0][:], lamT[:, 2, :], AF.Exp, ["lamT"], [("zt", 0)])
                dv(arT[:], zt[0][:], lamT[:, 0, :], ALU.mult, [("zt", 0), "lamT"], ["arT"])
                dv(aiT[:], zt[0][:], lamT[:, 1, :], ALU.mult, [("zt", 0), "lamT"], ["aiT"])
                _act(P, zt[1][:], arT[:], AF.Exp, ["arT"], [("zt", 1)])
                sincos(zt[2][:], zt[3][:], aiT[:], 1.0, ["aiT"], ("zt", 2), ("zt", 3))
                P.op("dve", lambda e: e.scalar_tensor_tensor(out=lbr[:], in0=zt[1][:], scalar=-1.0, in1=zt[3][:],
                                                             op0=ALU.mult, op1=ALU.mult), [("zt", 1), ("zt", 3)], ["lbr"])
                P.op("dve", lambda e: e.scalar_tensor_tensor(out=lbi[:], in0=zt[1][:], scalar=-1.0, in1=zt[2][:],
                                                             op0=ALU.mult, op1=ALU.mult), [("zt", 1), ("zt", 2)], ["lbi"])
                if _STOP == 2:
                    return trivial_o3()
                UC = hid[:, 0:8, :].rearrange("p a b -> p (a b)")
                GS = hid[:, 8:16, :].rearrange("p a b -> p (a b)")
                yk = lambda a, b: [("yS", j) for j in range(a, b)]
                for c in range(NCH):
                    c4 = slice(c * 4, (c + 1) * 4)
                    P.dma("sp", lambda e, c=c: e.dma_start(out=lamB[:], in_=s5_lamB[oi][:, :, c, :]), [], ["lamB"], key="ld5")
                    P.dma("sp", lambda e, c=c: e.dma_start(out=bTs[:], in_=s5_bT[oi][:, :, c, :]), [], ["bTs"], key="ld6")
                    P.dma("pool", lambda e, c=c: e.dma_start(out=Cw[:], in_=s5_cw[oi][c]), [], ["Cw"], key="ld7")
                    for gp_ in range(4):
                        P.op("act", lambda e, gp_=gp_: e.mul(out=Cw[:, gp_ * 2 + 1, :], in_=Cw[:, gp_ * 2 + 1, :], mul=-1.0),
                             ["Cw"], ["Cw"])
                    P.dma("sp", lambda e, c=c: e.dma_start(out=UC[:, 0:L], in_=xnT_d[c]), ["xnT_d"], hk(0, 8), key="ld8")
                    z = lambda i: zt[i][:]
                    zk = lambda i: ("zt", i)
                    _act(P, z(0), lamB[:, 2, :], AF.Exp, ["lamB"], [zk(0)])
                    dv(z(1), z(0), lamB[:, 0, :], ALU.mult, [zk(0), "lamB"], [zk(1)])
                    dv(z(2), z(0), lamB[:, 1, :], ALU.mult, [zk(0), "lamB"], [zk(2)])
                    _act(P, z(1), z(1), AF.Exp, [zk(1)], [zk(1)])
                    sincos(z(3), z(4), z(2), 1.0, [zk(2)], zk(3), zk(4))
                    P.op("dve", lambda e: e.scalar_tensor_tensor(out=z(5), in0=z(1), scalar=-1.0, in1=z(4),
                                                                 op0=ALU.mult, op1=ALU.mult), [zk(1), zk(4)], [zk(5)])
                    P.op("dve", lambda e: e.scalar_tensor_tensor(out=z(6), in0=z(1), scalar=-1.0, in1=z(3),
                                                                 op0=ALU.mult, op1=ALU.mult), [zk(1), zk(3)], [zk(6)])
                    ds(z(5), z(5), -1.0, None, ALU.add, None, [zk(5)], [zk(5)])
                    dv(z(0), lamB[:, 0, :], lamB[:, 0, :], ALU.mult, ["lamB"], [zk(0)])
                    dv(z(1), lamB[:, 1, :], lamB[:, 1, :], ALU.mult, ["lamB"], [zk(1)])
                    dv(z(0), z(0), z(1), ALU.add, [zk(0), zk(1)], [zk(0)])
                    P.op("dve", lambda e: e.reciprocal(out=z(0), in_=z(0)), [zk(0)], [zk(0)])
                    dv(z(1), z(5), lamB[:, 0, :], ALU.mult, [zk(5), "lamB"], [zk(1)])
                    dv(z(2), z(6), lamB[:, 1, :], ALU.mult, [zk(6), "lamB"], [zk(2)])
                    dv(z(1), z(1), z(2), ALU.add, [zk(1), zk(2)], [zk(1)])
                    dv(z(7), z(1), z(0), ALU.mult, [zk(1), zk(0)], [zk(7)])
                    dv(z(1), z(6), lamB[:, 0, :], ALU.mult, [zk(6), "lamB"], [zk(1)])
                    dv(z(2), z(5), lamB[:, 1, :], ALU.mult, [zk(5), "lamB"], [zk(2)])
                    dv(z(1), z(1), z(2), ALU.subtract, [zk(1), zk(2)], [zk(1)])
                    dv(z(8), z(1), z(0), ALU.mult, [zk(1), zk(0)], [zk(8)])
                    dv(z(1), z(7), bTs[:, 0, :], ALU.mult, [zk(7), "bTs"], [zk(1)])
                    dv(z(2), z(8), bTs[:, 1, :], ALU.mult, [zk(8), "bTs"], [zk(2)])
                    dv(z(3), z(1), z(2), ALU.subtract, [zk(1), zk(2)], [zk(3)])
                    dv(z(1), z(7), bTs[:, 1, :], ALU.mult, [zk(7), "bTs"], [zk(1)])
                    dv(z(2), z(8), bTs[:, 0, :], ALU.mult, [zk(8), "bTs"], [zk(2)])
                    dv(z(4), z(1), z(2), ALU.add, [zk(1), zk(2)], [zk(4)])
                    for ri in range(2):
                        for gg in range(8):
                            ds(Bblk[:, ri, gg, :], z(3 + ri), jc[:, 130 + gg:131 + gg], None, ALU.mult, None,
                               [zk(3 + ri), "jc"], ["Bblk"])
                    if _STOP == 3:
                        continue
                    lv = s5_lam[oi].rearrange("k (c n) -> k c n", c=NCH)
                    for k3 in range(3):
                        if _DBG == 2:
                            P.dma("sp", lambda e, k3=k3, c=c: e.dma_start(
                                out=lb3[0:1, k3, :], in_=lv[k3, c:c + 1, :]),
                                [], yk(12 + k3, 13 + k3), key="ld9_%d" % k3)
                        else:
                            P.dma("pool" if _DBG == 1 else "sp", lambda e, k3=k3, c=c: e.dma_start(
                                out=lb3[:, k3, :], in_=lv[k3, c:c + 1, :].partition_broadcast(128)),
                                [], yk(12 + k3, 13 + k3), key="ld9_%d" % k3)
                    _act(P, lb3[:, 2, :], lb3[:, 2, :], AF.Exp, yk(14, 15), yk(14, 15))
                    dv(lb3[:, 0, :], lb3[:, 0, :], lb3[:, 2, :], ALU.mult, yk(12, 13) + yk(14, 15), yk(12, 13))
                    dv(lb3[:, 1, :], lb3[:, 1, :], lb3[:, 2, :], ALU.mult, yk(13, 14) + yk(14, 15), yk(13, 14))
                    ds(pmg, lb3[:, 0, :], jcol, None, ALU.mult, None, yk(12, 13) + ["jc"], yk(2, 3))
                    _act(P, pmg, pmg, AF.Exp, yk(2, 3), yk(2, 3), scale=-1.0)
                    sincos(u1b, u2b, lb3[:, 1, :], jcol, yk(13, 14) + ["jc"], ("yS", 15), ("yS", 3), tb=sg[0][:], kt=[("sg", 0)])
                    P.op("dve", lambda e: e.scalar_tensor_tensor(out=Ptab[:, 0, :], in0=pmg, scalar=-1.0, in1=u2b,
                                                                 op0=ALU.mult, op1=ALU.mult), yk(2, 4), yk(4, 5))
                    dv(Ptab[:, 1, :], pmg, u1b, ALU.mult, yk(2, 3) + yk(15, 16), yk(5, 6))
                    for gp in range(4):
                        cg = c * 4 + gp
                        qs = slice(gp * 128, (gp + 1) * 128)
                        ds(pmg[:, qs], jrow, arT[:, cg:cg + 1], None, ALU.mult, None, ["jc", "arT"], yk(2, 3))
                        ds(u1b[:, qs], jrow, aiT[:, cg:cg + 1], None, ALU.mult, None, ["jc", "aiT"], yk(15, 16))
                    _act(P, pmg, pmg, AF.Exp, yk(2, 3), yk(2, 3))
                    ds(u2b, u1b, 1.5 * PI, None, ALU.add, None, yk(15, 16), yk(3, 4))
                    ds(u1b, u1b, PI, None, ALU.add, None, yk(15, 16), yk(15, 16))
                    red_sin(u1b, sg[0][:], yk(15, 16), [("sg", 0)])
                    red_sin(u2b, sg[0][:], yk(3, 4), [("sg", 0)])
                    P.op("dve", lambda e: e.scalar_tensor_tensor(out=Qtab[:, 0, :], in0=pmg, scalar=-1.0, in1=u2b,
                                                                 op0=ALU.mult, op1=ALU.mult), yk(2, 4), yk(6, 7))
                    P.op("dve", lambda e: e.scalar_tensor_tensor(out=Qtab[:, 1, :], in0=pmg, scalar=-1.0, in1=u1b,
                                                                 op0=ALU.mult, op1=ALU.mult), yk(2, 3) + yk(15, 16), yk(7, 8))
                    if _STOP == 4:
                        continue
                    q127r = Qtab[:, 0, :].rearrange("p (g j) -> p g j", j=128)[:, :, 127]
                    q127i = Qtab[:, 1, :].rearrange("p (g j) -> p g j", j=128)[:, :, 127]
                    dv(k4[0][:], lbr[:, c4], q127r, ALU.mult, ["lbr"] + yk(6, 7), [("k4", 0)])
                    dv(k4[1][:], lbi[:, c4], q127i, ALU.mult, ["lbi"] + yk(7, 8), [("k4", 1)])
                    dv(k4[2][:], lbr[:, c4], q127i, ALU.mult, ["lbr"] + yk(7, 8), [("k4", 2)])
                    dv(k4[3][:], lbi[:, c4], q127r, ALU.mult, ["lbi"] + yk(6, 7), [("k4", 3)])
                    dv(lam128[:, 0, :], k4[0][:], k4[1][:], ALU.subtract, [("k4", 0), ("k4", 1)], ["lam128"])
                    dv(lam128[:, 1, :], k4[2][:], k4[3][:], ALU.add, [("k4", 2), ("k4", 3)], ["lam128"])
                    P.op("dve", lambda e: e.memset(car[:], 0.0), [], ["car"])
                    def s5_front(k, c=c, c4=c4):
                        ks = slice(k * 128, (k + 1) * 128)
                        for ri in range(2):
                            P.op("pe", lambda e, ri=ri, ks=ks: e.matmul(
                                pb[1 + ri][:], UC[:, ks], Bblk[:, ri, :, :].rearrange("p a b -> p (a b)"),
                                start=True, stop=True), hk(0, 8) + ["Bblk"], pk(1 + ri))
                        t1, t2 = sg[0][:], sg[1][:]
                        dv(t1, pb[1][:], Ptab[:, 0, :], ALU.mult, pk(1) + yk(4, 5), [("sg", 0)])
                        dv(t2, pb[2][:], Ptab[:, 1, :], ALU.mult, pk(2) + yk(5, 6), [("sg", 1)])
                        dv(xt16[:, 0, :], t1, t2, ALU.subtract, [("sg", 0), ("sg", 1)], [("xt16", 0)])
                        dv(t1, pb[2][:], Ptab[:, 0, :], ALU.mult, pk(2) + yk(4, 5), [("sg", 0)])
                        dv(t2, pb[1][:], Ptab[:, 1, :], ALU.mult, pk(1) + yk(5, 6), [("sg", 1)])
                        dv(xt16[:, 1, :], t1, t2, ALU.add, [("sg", 0), ("sg", 1)], [("xt16", 1)])
                        cb = 3 if k % 2 == 0 else 6
                        for ri in range(2):
                            for gp in range(4):
                                P.op("pe", lambda e, ri=ri, gp=gp, cb=cb: e.matmul(
                                    pb[cb + ri][:, gp * 128:(gp + 1) * 128], xt16[:, ri, gp * 128:(gp + 1) * 128], tle[:],
                                    start=True, stop=True), [("xt16", ri), "tle"], pk(cb + ri))
                    def s5_mid(k, c=c, c4=c4):
                        par = k % 2
                        ccr_, cci_ = yS[:, 8 + 2 * par, :], yS[:, 9 + 2 * par, :]
                        kcr, kci = yk(8 + 2 * par, 9 + 2 * par), yk(9 + 2 * par, 10 + 2 * par)
                        for gp in range(4):
                            qs = slice(gp * 128, (gp + 1) * 128)
                            cb = 3 if par == 0 else 6
                            _act(P, ccr_[:, qs], pb[cb][:, qs], AF.Identity, pk(cb) + ["car"], kcr, bias=car[:, 0, gp:gp + 1])
                            _act(P, cci_[:, qs], pb[cb + 1][:, qs], AF.Identity, pk(cb + 1) + ["car"], kci, bias=car[:, 1, gp:gp + 1])
                        cr127 = ccr_.rearrange("p (g j) -> p g j", j=128)[:, :, 127]
                        ci127 = cci_.rearrange("p (g j) -> p g j", j=128)[:, :, 127]
                        dv(k4[0][:], lam128[:, 0, :], cr127, ALU.mult, ["lam128"] + kcr, [("k4", 0)])
                        dv(k4[1][:], lam128[:, 1, :], ci127, ALU.mult, ["lam128"] + kci, [("k4", 1)])
                        dv(k4[2][:], lam128[:, 0, :], ci127, ALU.mult, ["lam128"] + kci, [("k4", 2)])
                        dv(k4[3][:], lam128[:, 1, :], cr127, ALU.mult, ["lam128"] + kcr, [("k4", 3)])
                        dv(car[:, 0, :], k4[0][:], k4[1][:], ALU.subtract, [("k4", 0), ("k4", 1)], ["car"])
                        dv(car[:, 1, :], k4[2][:], k4[3][:], ALU.add, [("k4", 2), ("k4", 3)], ["car"])
                    def s5_back(k, c=c, c4=c4):
                        ks = slice(k * 128, (k + 1) * 128)
                        par = k % 2
                        ccr_, cci_ = yS[:, 8 + 2 * par, :], yS[:, 9 + 2 * par, :]
                        kcr, kci = yk(8 + 2 * par, 9 + 2 * par), yk(9 + 2 * par, 10 + 2 * par)
                        t3, t4 = tmp[0][:], tmp[1][:]
                        pv = lambda out, in0, in1, op, r, w: P.op(
                            "pool", lambda e: e.tensor_tensor(out=out, in0=in0, in1=in1, op=op), r, w)
                        pv(t3, ccr_, Qtab[:, 0, :], ALU.mult, kcr + yk(6, 7), [("tmp", 0)])
                        pv(t4, cci_, Qtab[:, 1, :], ALU.mult, kci + yk(7, 8), [("tmp", 1)])
                        pv(sT16[:, 0, :], t3, t4, ALU.subtract, [("tmp", 0), ("tmp", 1)], [("sT16", 0)])
                        pv(t3, cci_, Qtab[:, 0, :], ALU.mult, kci + yk(6, 7), [("tmp", 0)])
                        pv(t4, ccr_, Qtab[:, 1, :], ALU.mult, kcr + yk(7, 8), [("tmp", 1)])
                        pv(sT16[:, 1, :], t3, t4, ALU.add, [("tmp", 0), ("tmp", 1)], [("sT16", 1)])
                        n = 0
                        for gp in range(4):
                            for ri in range(2):
                                P.op("pe", lambda e, gp=gp, ri=ri, n=n: e.matmul(
                                    pb[5][:, 0:128], Cw[:, gp * 2 + ri, :], sT16[:, ri, gp * 128:(gp + 1) * 128],
                                    start=(n == 0), stop=(n == 7)), ["Cw", ("sT16", ri)], pk(5, 0))
                                n += 1
                        P.op("dve", lambda e, ks=ks, c=c: e.scalar_tensor_tensor(
                            out=ytmp[:], in0=UC[:, ks], scalar=dsk[:, oi * NCH + c:oi * NCH + c + 1], in1=pb[5][:, 0:128],
                            op0=ALU.mult, op1=ALU.add), hk(0, 8) + ["dsk"] + pk(5, 0), ["ytmp"])
                        _act(P, GS[:, ks], ytmp[:], AF.Gelu, ["ytmp"], hk(8 + k // 4, 9 + k // 4))
                    nk = L // 128
                    s5_front(0)
                    for k in range(nk):
                        if k + 1 < nk:
                            s5_front(k + 1)
                        s5_mid(k)
                        if k >= 1:
                            s5_back(k - 1)
                    s5_back(nk - 1)
                    if _STOP:
                        continue
                    P.dma("sp", lambda e, c=c: e.dma_start(out=geT_d[c], in_=GS[:, 0:L]), hk(8, 16), ["geT_d"], key="st6")
                if _STOP:
                    return trivial_o3()
                gv = geT_d.rearrange("c p t -> p c t")
                wv = w_glu[oi]
                for g in range(n_tg):
                    P.dma("sp", lambda e, g=g: e.dma_start(out=xn[:], in_=gv[:, :, g * TG:(g + 1) * TG]),
                          ["geT_d"], [("xn", c) for c in range(NCH)], key="ld3")
                    load_h(src, g)
                    for j in range(NCH):
                        i = rot("wi", NWI)
                        P.dma("pool", lambda e, i=i, j=j: [
                            e.dma_start(out=wi[i][:, 0, :, :], in_=wv[j].rearrange("p (c f) -> p c f", c=NCH)),
                            e.dma_start(out=wi[i][:, 1, :, :], in_=wv[NCH + j].rearrange("p (c f) -> p c f", c=NCH))],
                            [], [("wi", i)], key="wi%d" % i, ndma=2)
                        bv = (2 * j) % 4 + 4
                        bgt = bv + 1
                        for a, bank in ((0, bv), (1, bgt)):
                            for c in range(NCH):
                                P.op("pe", lambda e, i=i, c=c, a=a, bank=bank: e.matmul(
                                    pb[bank][:], wi[i][:, a, c, :], xn[:, c, :], start=(c == 0), stop=(c == NCH - 1)),
                                    [("wi", i), ("xn", c)], pk(bank))
                        si = rot("sg", 2)
                        _act(P, sg[si][:], pb[bgt][:], AF.Sigmoid, pk(bgt), [("sg", si)])
                        P.op("dve", lambda e, si=si, j=j, bv=bv: e.tensor_tensor(
                            out=yS[:, j, :], in0=sg[si][:], in1=pb[bv][:], op=ALU.mult),
                            [("sg", si)] + pk(bv), [("yS", j)])
                    post_residual(li, 1)
                    store_h(hT, g)

        ei_map, oi_map = {}, {}
        for li, kind in enumerate(layers):
            if kind == "even":
                ei_map[li] = len(ei_map)
            elif kind == "odd":
                oi_map[li] = len(oi_map)
        for li, kind in enumerate(layers):
            src = xT if li == 0 else hT
            last = (li == depth - 1)
            if kind == "even":
                even_mixer(li, ei_map[li], src)
                src = hT
            elif kind == "odd":
                odd_mixer(li, oi_map[li], src)
                src = hT
            for g in range(n_tg):
                load_h(src, g)
                prenorm(li, 2)
                ffn(li)
                post_residual(li, 3)
                store_h(yT if last else hT, g)

        P.emit(stack)
    nc.in_names_ = list(B.in_names)
    return nc


def _bf16(a):
    return np.asarray(a, dtype=np.float32).astype(ml_dtypes.bfloat16)


def host_consts():
    i = np.arange(128)
    cm = np.zeros((128, 4, 128), np.float32)
    cm[:, 0, :] = (i[:, None] >= i[None, :])
    cm[:, 1, :] = (i[:, None] <= i[None, :]) * (-1.0 / 16)
    cm[:, 2, :] = (i[:, None] > i[None, :]) * (-1.0 / 16)
    cm[:, 3, :] = (i[:, None] <= i[None, :])
    sbm = np.zeros((128, 4, TG), np.float32)
    for d in range(4):
        for tb in range(4):
            if tb > d:
                sbm[:, d, tb * 128:(tb + 1) * 128] = 1.0
            elif tb == d:
                sbm[:, d, tb * 128:(tb + 1) * 128] = (i[:, None] < i[None, :])
    return {"cm128": _bf16(cm), "sbmask": _bf16(sbm), "ones_bf": _bf16(np.ones((128, 128)))}


def _tile_w(w):
    n, k, N = w.shape
    t = w.reshape(n, k // 128, 128, N // 128, 128).transpose(0, 3, 2, 1, 4)
    return np.ascontiguousarray(t).reshape(n, N // 128, 128, k)


def host_layout(inputs, layers):
    depth = len(layers)
    f = lambda k: np.asarray(inputs[k], dtype=np.float32)
    g = f("norm_gains")[:depth]
    m = {"gains": np.ascontiguousarray(g.reshape(depth * 4, NCH, 128).transpose(2, 0, 1).reshape(128, depth * 4 * NCH)),
         "w_ffn_in": _tile_w(f("w_ffn_in")[:depth]), "w_ffn_out": f("w_ffn_out")[:depth]}
    ne = sum(1 for l in layers if l == "even")
    no = sum(1 for l in layers if l == "odd")
    if ne:
        w_in = f("w_in")[:ne]
        m["w_in_t128"] = _tile_w(w_in[:, :, :6144])
        tmcols = [2048 + 256 * b for b in range(4)] + [4096 + 256 * b for b in range(4)] + [3584, 3840]
        tm = np.stack([w_in[:, :, c0:c0 + 256] for c0 in tmcols], axis=1)
        m["w_in_tm"] = np.ascontiguousarray(
            tm.reshape(ne, 10, NCH, 128, 256).transpose(0, 1, 3, 2, 4)).reshape(ne, 10, 128, NCH * 256)
        lr = w_in[:, :, 6144:6160]
        m["w_in_lr"] = np.ascontiguousarray(lr.reshape(ne, NCH, 128, 16).transpose(0, 2, 1, 3)).reshape(ne, 128, NCH * 16)
        m["w_out"] = f("w_out")[:ne]
        m["wgu"] = np.ascontiguousarray(np.concatenate([f("w_gate_up")[:ne], f("b_gate")[:ne, None, :]], axis=1))
        gg = f("gla_norm_gain")[:ne]
        m["gla_gain"] = np.ascontiguousarray(gg.reshape(ne, 8, 128).transpose(2, 0, 1).reshape(128, ne * 8))
    if no:
        lre, lim, ls = f("s5_lambda_re")[:no], f("s5_lambda_im")[:no], f("s5_log_step")[:no]
        lse = np.broadcast_to(ls[:, :, None], lre.shape)
        lam3 = np.stack([lre, lim, lse], axis=1)
        m["s5_lam"] = np.ascontiguousarray(lam3.reshape(no, 3, 8192))
        t = lam3.reshape(no, 3, 64, 2, 64)
        m["s5_lamT"] = np.ascontiguousarray(t.transpose(0, 3, 4, 1, 2).reshape(no, 128, 3, 64))
        t = lam3.reshape(no, 3, NCH, 8, 1, 64)
        t = np.broadcast_to(t, (no, 3, NCH, 8, 16, 64))
        m["s5_lamB"] = np.ascontiguousarray(t.transpose(0, 3, 4, 1, 2, 5).reshape(no, 128, 3, NCH, 64))
        b2 = np.stack([f("s5_b_re")[:no], f("s5_b_im")[:no]], axis=1)
        t = b2.reshape(no, 2, NCH, 8, 64, 16)
        m["s5_bT"] = np.ascontiguousarray(t.transpose(0, 3, 5, 1, 2, 4).reshape(no, 128, 2, NCH, 64))
        c2 = np.stack([f("s5_c_re")[:no], f("s5_c_im")[:no]], axis=1)
        cw = np.zeros((no, NCH, 2, 64, 4, 2, 8, 16), np.float32)
        t = c2.reshape(no, 2, NCH, 4, 2, 16, 64)
        for gp in range(4):
            for g2 in range(2):
                cw[:, :, g2, :, gp, :, 2 * gp + g2, :] = t[:, :, :, gp, g2].transpose(0, 2, 4, 1, 3)
        m["s5_cw"] = np.ascontiguousarray(cw.reshape(no, NCH, 128, 8, 128))
        m["s5_dT"] = np.ascontiguousarray(f("s5_d")[:no].reshape(no, NCH, 128).transpose(2, 0, 1).reshape(128, no * NCH))
        m["w_glu"] = _tile_w(f("w_glu")[:no])
        jc = np.zeros((128, 138), np.float32)
        jc[:, 0:128] = np.arange(128)[None, :]
        jc[:, 128] = np.arange(128)
        jc[:, 129] = 1.0
        jc[:, 130:138] = (np.arange(128)[:, None] // 16 == np.arange(8)[None, :])
        m["jconst"] = jc
        i = np.arange(128)
        m["trile"] = _bf16((i[:, None] <= i[None, :]).astype(np.float32))
    m.update(host_consts())
    return m, no


_CACHE = {}


def kernel(**inputs):
    layers = ["even", "odd"] * (DEPTH // 2)
    x = np.asarray(inputs["x"], dtype=np.float32)
    m, _ = host_layout(inputs, layers)
    if "nc" not in _CACHE:
        _CACHE["nc"] = build(SEQ, layers)
    nc = _CACHE["nc"]
    in_maps = []
    for b in range(BATCH):
        mm = dict(m)
        mm["xT"] = np.ascontiguousarray(x[b].T)
        in_maps.append(mm)
    res = run_bass_kernel_spmd(nc, in_maps, core_ids=list(range(BATCH)))
    out = np.stack([np.ascontiguousarray(res.results[b]["yT"].T) for b in range(BATCH)], axis=0)
    return out.astype(np.float32)
```

```python
import contextlib
import numpy as np
import ml_dtypes
import concourse.bass as bass
import concourse.mybir as mybir
from concourse.bass_utils import run_bass_kernel_spmd

F32 = mybir.dt.float32
BF16 = mybir.dt.bfloat16
AF = mybir.ActivationFunctionType
ALU = mybir.AluOpType

D_MODEL = 2048
SEQ = 4096
BATCH = 2
DEPTH = 4
D_FF = 5632
IN_WIDTH = 6160
EPS = 1e-6
NCH = D_MODEL // 128
TG = 512

import os
_DBG = int(os.environ.get('ODD_DBG', '0'))
_STOP = int(os.environ.get('ODD_STOP', '0'))
_SKIP = int(os.environ.get('ODD_SKIP', '0'))
SAME_ENGINE_SYNC = True
SEM_LIMIT = int(os.environ.get('SEM_LIMIT', '8000'))


class Op:
    __slots__ = ("eng", "fn", "reads", "writes", "dma", "key", "deps", "signal",
                 "sem_i", "sem_v", "ndma")

    def __init__(self, eng, fn, reads, writes, dma=False, key=None, ndma=1):
        self.eng = eng
        self.fn = fn
        self.reads = tuple(reads)
        self.writes = tuple(writes)
        self.dma = dma
        self.key = key
        self.deps = []
        self.signal = False
        self.sem_i = None
        self.sem_v = None
        self.ndma = ndma


class Prog:
    ENGS = ("pe", "act", "dve", "pool", "sp")

    def __init__(self, nc):
        self.nc = nc
        self.ops = []
        self.last_w = {}
        self.readers = {}
        self.last_dma = {}

    def op(self, eng, fn, reads=(), writes=()):
        o = Op(eng, fn, reads, writes)
        self._track(o)
        return o

    def dma(self, eng, fn, reads=(), writes=(), key=None, ndma=1):
        o = Op(eng, fn, reads, writes, dma=True, key=key, ndma=ndma)
        prev = self.last_dma.get(key)
        if prev is not None:
            o.deps.append(prev)
        self.last_dma[key] = o
        self._track(o)
        return o

    def _track(self, o):
        deps = o.deps
        for k in o.reads:
            w = self.last_w.get(k)
            if w is not None:
                deps.append(w)
        for k in o.writes:
            w = self.last_w.get(k)
            if w is not None:
                deps.append(w)
            deps.extend(self.readers.get(k, ()))
        for k in o.reads:
            self.readers.setdefault(k, []).append(o)
        for k in o.writes:
            self.last_w[k] = o
            self.readers[k] = []
        seen = set()
        dd = []
        for d in deps:
            if d is o or id(d) in seen:
                continue
            seen.add(id(d))
            dd.append(d)
        o.deps = dd
        self.ops.append(o)

    def simulate(self, per_eng, sems_of):
        semv = {}
        pos = {e: 0 for e in self.ENGS}
        total = sum(len(v) for v in per_eng.values())
        done = 0
        while done < total:
            prog = False
            for e in self.ENGS:
                while pos[e] < len(per_eng[e]):
                    o = per_eng[e][pos[e]]
                    ok = True
                    for d in o.deps:
                        k = (sems_of[id(d)]["name"], d.sem_i)
                        if semv.get(k, 0) < d.sem_v:
                            ok = False
                            break
                    if not ok:
                        break
                    if o.dma:
                        k = (sems_of[id(o)]["name"], o.sem_i)
                        semv[k] = semv.get(k, 0) + 16 * o.ndma
                        assert semv[k] == o.sem_v, (k, semv[k], o.sem_v)
                    elif o.signal:
                        k = (sems_of[id(o)]["name"], o.sem_i)
                        semv[k] = semv.get(k, 0) + 1
                        assert semv[k] == o.sem_v, (k, semv[k], o.sem_v)
                    pos[e] += 1
                    done += 1
                    prog = True
            if not prog:
                print("DEADLOCK at", {e: pos[e] for e in self.ENGS})
                for e in self.ENGS:
                    if pos[e] < len(per_eng[e]):
                        o = per_eng[e][pos[e]]
                        print(" ", e, "reads", o.reads[:4], "writes", o.writes[:4],
                              [(sems_of[id(d)]["name"], d.sem_i, d.sem_v, d.eng, d.writes[:2]) for d in o.deps][:6])
                raise RuntimeError("deadlock")
        print("PROG_SIM ok: %d ops, sems %d" % (total, len(semv)))

    def emit(self, stack):
        nc = self.nc
        for o in self.ops:
            nd = []
            for d in o.deps:
                if not d.dma and d.eng == o.eng:
                    if d.eng == "pe" or not SAME_ENGINE_SYNC or o.dma:
                        continue
                d.signal = True
                nd.append(d)
            o.deps = nd
        streams = {}

        def stream(name):
            s = streams.get(name)
            if s is None:
                s = {"sems": [], "val": 0, "name": name}
                streams[name] = s
            return s

        def bump(s, inc):
            if not s["sems"] or s["val"] + inc > SEM_LIMIT:
                s["sems"].append(stack.enter_context(
                    nc.semaphore("s_%s_%d" % (s["name"], len(s["sems"])))))
                s["val"] = 0
            s["val"] += inc
            return len(s["sems"]) - 1, s["val"]

        sems_of = {}
        for o in self.ops:
            if o.dma:
                s = stream("d_" + str(o.key))
                o.sem_i, o.sem_v = bump(s, 16 * o.ndma)
                sems_of[id(o)] = s
            elif o.signal:
                s = stream("e_" + o.eng)
                o.sem_i, o.sem_v = bump(s, 1)
                sems_of[id(o)] = s
        final = []
        for name, s in streams.items():
            final.append((s, len(s["sems"]) - 1, s["val"]))

        per_eng = {e: [o for o in self.ops if o.eng == e] for e in self.ENGS}
        if os.environ.get("PROG_SIM"):
            self.simulate(per_eng, sems_of)
        block = stack.enter_context(nc.Block())

        def run(eng_name, eng):
            waited = {}
            for o in per_eng[eng_name]:
                for d in o.deps:
                    s = sems_of[id(d)]
                    sem = s["sems"][d.sem_i]
                    k = (s["name"], d.sem_i)
                    if waited.get(k, 0) >= d.sem_v:
                        continue
                    waited[k] = d.sem_v
                    eng.wait_ge(sem, d.sem_v)
                r = o.fn(eng)
                if o.dma:
                    sem = sems_of[id(o)]["sems"][o.sem_i]
                    if not isinstance(r, (list, tuple)):
                        r = [r]
                    assert len(r) == o.ndma, (len(r), o.ndma)
                    for ins in r:
                        ins.then_inc(sem, 16)
                elif o.signal:
                    sem = sems_of[id(o)]["sems"][o.sem_i]
                    r.then_inc(sem, 1)
            if eng_name == "sp":
                for s, i, v in final:
                    if s["name"].startswith("d_"):
                        eng.wait_ge(s["sems"][i], v)

        @block.tensor
        def _(e):
            run("pe", e)

        @block.scalar
        def _(e):
            run("act", e)

        @block.vector
        def _(e):
            run("dve", e)

        @block.gpsimd
        def _(e):
            run("pool", e)

        @block.sync
        def _(e):
            run("sp", e)


class Builder:
    def __init__(self, L, layers):
        self.L = L
        self.layers = layers
        self.nc = bass.Bass("TRN2", target_bir_lowering=False)
        self.P = Prog(self.nc)
        self.uid = 0
        self.psum_rr = 0
        self.in_names = []

    def dram_in(self, name, shape, dt=F32):
        self.in_names.append(name)
        return self.nc.dram_tensor(name, list(shape), dt, kind="ExternalInput").ap()

    def dram_out(self, name, shape, dt=F32):
        return self.nc.dram_tensor(name, list(shape), dt, kind="ExternalOutput").ap()

    def dram_tmp(self, name, shape, dt):
        return self.nc.dram_tensor(name, list(shape), dt, kind="Internal").ap()

    def sb(self, stack, name, shape, dt):
        return stack.enter_context(self.nc.sbuf_tensor("sb_" + name, list(shape), dt))

    def ps(self, stack, name, shape, dt=F32):
        return stack.enter_context(self.nc.psum_tensor("ps_" + name, list(shape), dt))


def _act(P, out, in_, func, reads, writes, eng="act", **kw):
    return P.op(eng, lambda e: e.activation(out=out, in_=in_, func=func, **kw), reads, writes)


def build(L, layers, n_in_layers=None):
    B = Builder(L, layers)
    nc, P = B.nc, B.P
    n_tg = L // TG
    n_even = sum(1 for l in layers if l == "even")
    n_odd = sum(1 for l in layers if l == "odd")
    depth = len(layers)

    xT = B.dram_in("xT", [D_MODEL, L])
    gains = B.dram_in("gains", [128, depth * 4 * NCH])
    w_ffn_in = B.dram_in("w_ffn_in", [depth, 2 * D_FF // 128, 128, D_MODEL])
    w_ffn_out = B.dram_in("w_ffn_out", [depth, D_FF, D_MODEL])
    ones_in = B.dram_in("ones_bf", [128, 128], BF16)
    yT = B.dram_out("yT", [D_MODEL, L])
    hT = B.dram_tmp("hT", [D_MODEL, L], F32)

    stack = contextlib.ExitStack()
    with stack:
        hA = B.sb(stack, "hA", [128, NCH, TG], F32)
        xn = B.sb(stack, "xn", [128, NCH, TG], BF16)
        hid = B.sb(stack, "hid", [128, D_FF // 128, TG], BF16)
        yS = B.sb(stack, "yS", [128, NCH, TG], F32)
        NWI = 2
        wi_flat = [B.sb(stack, "wi%d" % i, [128, 4096], BF16) for i in range(NWI)]
        wi = [w[:].rearrange("p (a c f) -> p a c f", a=2, c=NCH) for w in wi_flat]
        wiB = [w[:].rearrange("p (c n) -> p c n", c=NCH) for w in wi_flat]
        NWO = 2
        wo = [B.sb(stack, "wo%d" % i, [128, 4, 1024], BF16) for i in range(NWO)]
        sq = [B.sb(stack, "sq%d" % i, [128, TG], BF16) for i in range(2)]
        sg = [B.sb(stack, "sg%d" % i, [128, TG], F32) for i in range(2)]
        tmp = [B.sb(stack, "tmp%d" % i, [128, TG], F32) for i in range(2)]
        rstd = B.sb(stack, "rstd", [128, TG], F32)
        xt16 = B.sb(stack, "xt16", [128, 2, 512], BF16)
        sT16 = B.sb(stack, "sT16", [128, 2, 512], BF16)
        gsb = B.sb(stack, "gsb", [128, depth * 4 * NCH], F32)
        ones = B.sb(stack, "ones", [128, 128], BF16)
        pb = [B.ps(stack, "pb%d" % i, [128, TG], F32) for i in range(8)]

        P.dma("sp", lambda e: e.dma_start(out=gsb[:], in_=gains[:, :]), [], ["gsb"], key="c0")
        P.dma("sp", lambda e: e.dma_start(out=ones[:], in_=ones_in[:, :]), [], ["ones"], key="c1")

        def gain(layer, k, c):
            i = (layer * 4 + k) * NCH + c
            return gsb[:, i:i + 1]

        cnt = {"wi": 0, "wo": 0, "sq": 0, "sg": 0, "tmp": 0, "pbk": 0}

        def hk(a, b):
            return [("hid", j) for j in range(a, b)]

        def pk(n, q=None):
            if q is None:
                return [("pb", n, k) for k in range(4)]
            return [("pb", n, q)]

        def rot(name, n):
            i = cnt[name] % n
            cnt[name] += 1
            return i

        def rms_stats(src, src_key, pbank):
            for c in range(NCH):
                i = rot("sq", 2)
                _act(P, sq[i][:], src[:, c, :], AF.Square, [(src_key, c)], [("sq", i)])
                P.op("pe", lambda e, i=i, c=c: e.matmul(pb[pbank][:], ones[:], sq[i][:],
                                                        start=(c == 0), stop=(c == NCH - 1)),
                     [("sq", i), "ones"], pk(pbank))
            P.op("dve", lambda e: e.tensor_scalar(out=rstd[:], in0=pb[pbank][:], scalar1=1.0 / D_MODEL,
                                                  scalar2=EPS, op0=ALU.mult, op1=ALU.add),
                 pk(pbank), ["rstd"])
            _act(P, rstd[:], rstd[:], AF.Sqrt, ["rstd"], ["rstd"])
            P.op("dve", lambda e: e.reciprocal(out=rstd[:], in_=rstd[:]), ["rstd"], ["rstd"])

        def load_h(src_ap, g):
            v = src_ap.rearrange("(c p) t -> p c t", p=128)
            P.dma("sp", lambda e: e.dma_start(out=hA[:], in_=v[:, :, g * TG:(g + 1) * TG]),
                  [("dram_h", g)], [("hA", c) for c in range(NCH)], key="hA")

        def store_h(dst_ap, g):
            v = dst_ap.rearrange("(c p) t -> p c t", p=128)
            P.dma("sp", lambda e: e.dma_start(out=v[:, :, g * TG:(g + 1) * TG], in_=hA[:]),
                  [("hA", c) for c in range(NCH)], [("dram_h", g)], key="hst")

        def prenorm(layer, k):
            rms_stats(hA, "hA", 0)
            for c in range(NCH):
                P.op("dve", lambda e, c=c: e.scalar_tensor_tensor(
                    out=xn[:, c, :], in0=hA[:, c, :], scalar=gain(layer, k, c), in1=rstd[:],
                    op0=ALU.mult, op1=ALU.mult), [("hA", c), "rstd", "gsb"], [("xn", c)])

        def post_residual(layer, k):
            rms_stats(yS, "yS", 0)
            for c in range(NCH):
                i = rot("tmp", 2)
                P.op("dve", lambda e, c=c, i=i: e.scalar_tensor_tensor(
                    out=tmp[i][:], in0=yS[:, c, :], scalar=gain(layer, k, c), in1=rstd[:],
                    op0=ALU.mult, op1=ALU.mult), [("yS", c), "rstd", "gsb"], [("tmp", i)])
                P.op("dve", lambda e, c=c, i=i: e.tensor_tensor(
                    out=hA[:, c, :], in0=hA[:, c, :], in1=tmp[i][:], op=ALU.add),
                    [("tmp", i), ("hA", c)], [("hA", c)])

        def dense_out(act_buf, act_key, kc, w_ap):
            wv = w_ap.rearrange("(j p) n -> p j n", p=128)
            for half in range(2):
                for j0 in range(0, kc, 4):
                    nj = min(4, kc - j0)
                    i = rot("wo", NWO)
                    P.dma("pool", lambda e, i=i, j0=j0, nj=nj, half=half: e.dma_start(
                        out=wo[i][:, 0:nj, :], in_=wv[:, j0:j0 + nj, half * 1024:(half + 1) * 1024]),
                        [], [("wo", i)], key="wo%d" % i)
                    for jj in range(nj):
                        j = j0 + jj
                        for dc in range(8):
                            P.op("pe", lambda e, i=i, jj=jj, j=j, dc=dc: e.matmul(
                                pb[dc][:], wo[i][:, jj, dc * 128:(dc + 1) * 128], act_buf[:, j, :],
                                start=(j == 0), stop=(j == kc - 1)),
                                [("wo", i), (act_key, j)], pk(dc))
                for dc in range(8):
                    c = half * 8 + dc
                    P.op("act", lambda e, c=c, dc=dc: e.copy(out=yS[:, c, :], in_=pb[dc][:]),
                         pk(dc), [("yS", c)])

        def ffn(layer):
            nf = D_FF // 128
            wv = w_ffn_in[layer]
            for j in range(nf):
                i = rot("wi", NWI)
                P.dma("pool", lambda e, i=i, j=j: [
                    e.dma_start(out=wi[i][:, 0, :, :], in_=wv[j].rearrange("p (c f) -> p c f", c=NCH)),
                    e.dma_start(out=wi[i][:, 1, :, :], in_=wv[nf + j].rearrange("p (c f) -> p c f", c=NCH))],
                    [], [("wi", i)], key="wi%d" % i, ndma=2)
                bg = (2 * j) % 4 + 4
                bu = bg + 1
                for c in range(NCH):
                    P.op("pe", lambda e, i=i, c=c, bg=bg: e.matmul(
                        pb[bg][:], wi[i][:, 0, c, :], xn[:, c, :], start=(c == 0), stop=(c == NCH - 1)),
                        [("wi", i), ("xn", c)], pk(bg))
                for c in range(NCH):
                    P.op("pe", lambda e, i=i, c=c, bu=bu: e.matmul(
                        pb[bu][:], wi[i][:, 1, c, :], xn[:, c, :], start=(c == 0), stop=(c == NCH - 1)),
                        [("wi", i), ("xn", c)], pk(bu))
                si = rot("sg", 2)
                _act(P, sg[si][:], pb[bg][:], AF.Silu, pk(bg), [("sg", si)])
                P.op("dve", lambda e, si=si, j=j, bu=bu: e.tensor_tensor(
                    out=hid[:, j, :], in0=sg[si][:], in1=pb[bu][:], op=ALU.mult),
                    [("sg", si)] + pk(bu), [("hid", j)])
            dense_out(hid, "hid", nf, w_ffn_out[layer])


        if n_even:
            w_t128 = B.dram_in("w_in_t128", [n_even, 48, 128, D_MODEL])
            w_tm = B.dram_in("w_in_tm", [n_even, 10, 128, 2 * D_MODEL])
            w_lr = B.dram_in("w_in_lr", [n_even, 128, NCH * 16])
            wgu_in = B.dram_in("wgu", [n_even, 17, 512])
            glag_in = B.dram_in("gla_gain", [128, n_even * 8])
            w_out = B.dram_in("w_out", [n_even, D_MODEL, D_MODEL])
            cm_in = B.dram_in("cm128", [128, 4, 128], BF16)
            sbm_in = B.dram_in("sbmask", [128, 4, TG], BF16)
            qT_d = B.dram_tmp("qT_d", [8, 128, L], BF16)
            kT_d = B.dram_tmp("kT_d", [8, 128, L], BF16)
            v_d = B.dram_tmp("v_d", [L, 1024], BF16)
            mixT_d = B.dram_tmp("mixT_d", [16, 128, L], BF16)

            cm = B.sb(stack, "cm", [128, 4, 128], BF16)
            sbm = B.sb(stack, "sbm", [128, 4, TG], BF16)
            la = B.sb(stack, "la", [128, 4, 512], BF16)
            sbv = B.sb(stack, "sbv", [128, 4, 256], BF16)
            lrT = B.sb(stack, "lrT", [17, TG], BF16)
            wgu = B.sb(stack, "wgu", [17, 512], BF16)
            S32 = B.sb(stack, "S32", [64, 8, 128], F32)
            Sb = B.sb(stack, "Sb", [64, 8, 128], BF16)
            dec = B.sb(stack, "dec", [64, 8], F32)
            glag = B.sb(stack, "glag", [128, n_even * 8], F32)
            khat = sT16[:, 1, :]
            qtl = B.sb(stack, "qtl", [64, 128], BF16)
            ktl = B.sb(stack, "ktl", [64, 128], BF16)
            PT = B.sb(stack, "PT", [128, 128], BF16)
            e1 = B.sb(stack, "e1", [64, 128], F32)
            e2 = B.sb(stack, "e2", [64, 128], F32)
            at_e = sg[0]
            at_sp = sg[1]
            at_er = tmp[0]
            at_S32 = tmp[1]
            at_sp16 = xt16[:, 0, :]
            at_S16 = xt16[:, 1, :]
            at_A16 = sT16[:, 0, :]

            P.dma("sp", lambda e: e.dma_start(out=cm[:], in_=cm_in[:, :, :]), [], ["cm"], key="c2")
            P.dma("sp", lambda e: e.dma_start(out=sbm[:], in_=sbm_in[:, :, :]), [], ["sbm"], key="c3")
            P.dma("sp", lambda e: e.dma_start(out=glag[:], in_=glag_in[:, :]), [], ["glag"], key="c4")
            P.op("dve", lambda e: e.memset(lrT[:], 1.0), [], ["lrT"])


            def tmv(tt):
                return hid[:, 32 + 3 * tt:35 + 3 * tt, :].rearrange("p a b -> p (a b)")

            def even_m1(li, ei, g, src):
                load_h(src, g)
                prenorm(li, 0)
                gs = slice(g * TG, (g + 1) * TG)

                def fm_proj(src_ap, f, parts):
                    i = rot("wi", NWI)
                    P.dma("pool", lambda e: e.dma_start(out=wi[i][:, 0, :, 0:f],
                                                        in_=src_ap.rearrange("p (c f) -> p c f", c=NCH)),
                          [], [("wi", i)], key="wi%d" % i)
                    for off, M, evac in parts:
                        bank = 4 + rot("pbk", 4)
                        for c in range(NCH):
                            P.op("pe", lambda e, c=c, bank=bank, off=off, M=M: e.matmul(
                                pb[bank][0:M, :], wi[i][:, 0, c, off:off + M], xn[:, c, :],
                                start=(c == 0), stop=(c == NCH - 1)),
                                [("wi", i), ("xn", c)], pk(bank))
                        evac(bank)

                wt = w_t128[ei]
                for h in range(8):
                    fm_proj(wt[h], 128, [(0, 128, lambda bank, h=h: P.op(
                        "act", lambda e: e.mul(out=hid[:, h, :], in_=pb[bank][:], mul=128.0 ** -0.5),
                        pk(bank), hk(h, h + 1)))])
                qv = qT_d.rearrange("h p t -> p h t")
                P.dma("sp", lambda e: e.dma_start(out=qv[:, :, gs], in_=hid[:, 0:8, :]), hk(0, 8), ["qT_d"], key="st0")
                for h in range(8):
                    fm_proj(wt[8 + h], 128, [(0, 128, lambda bank, h=h: P.op(
                        "act", lambda e: e.copy(out=hid[:, 8 + h, :], in_=pb[bank][:]),
                        pk(bank), hk(8 + h, 9 + h)))])
                kv = kT_d.rearrange("h p t -> p h t")
                P.dma("sp", lambda e: e.dma_start(out=kv[:, :, gs], in_=hid[:, 8:16, :]), hk(8, 16), ["kT_d"], key="st1")
                for hp in range(4):
                    fm_proj(wt[24 + hp], 128, [((h % 2) * 64, 64, lambda bank, h=h: P.op(
                        "act", lambda e: e.mul(out=hid[0:64, 16 + h, :], in_=pb[bank][0:64, :], mul=64.0 ** -0.5),
                        pk(bank), hk(16 + h, 17 + h))) for h in (2 * hp, 2 * hp + 1)])
                    fm_proj(wt[28 + hp], 128, [((h % 2) * 64, 64, lambda bank, h=h: P.op(
                        "act", lambda e: e.copy(out=hid[0:64, 24 + h, :], in_=pb[bank][0:64, :]),
                        pk(bank), hk(24 + h, 25 + h))) for h in (2 * hp, 2 * hp + 1)])
                for h in range(8):
                    fm_proj(wt[40 + h], 128, [(0, 128, lambda bank, h=h: _act(
                        P, yS[:, h, :], pb[bank][:], AF.Silu, pk(bank), [("yS", h)]))])
                fm_proj(w_lr[ei], 16, [(0, 16, lambda bank: P.op(
                    "act", lambda e: e.copy(out=lrT[0:16, :], in_=pb[bank][0:16, :]), pk(bank), ["lrT"]))])

                for blk in range(10):
                    if blk < 4:
                        col0 = 2048 + 256 * blk
                    elif blk < 8:
                        col0 = 4096 + 256 * (blk - 4)
                    else:
                        col0 = 3584 + 256 * (blk - 8)
                    i = rot("wi", NWI)
                    P.dma("pool", lambda e, i=i, blk=blk: e.dma_start(
                        out=wiB[i][:, :, :], in_=w_tm[ei][blk].rearrange("p (c n) -> p c n", c=NCH)),
                          [], [("wi", i)], key="wi%d" % i)
                    for tt in range(4):
                        bank = 4 + rot("pbk", 4)
                        for c in range(NCH):
                            P.op("pe", lambda e, c=c, i=i, tt=tt, bank=bank: e.matmul(
                                pb[bank][:, 0:256], xn[:, c, tt * 128:(tt + 1) * 128], wiB[i][:, c, :],
                                start=(c == 0), stop=(c == NCH - 1)),
                                [("wi", i), ("xn", c)], pk(bank))
                        if blk < 4:
                            P.op("act", lambda e, tt=tt, bank=bank: e.copy(out=sbv[:, tt, :], in_=pb[bank][:, 0:256]),
                                 pk(bank), ["sbv"])
                        else:
                            o0 = 256 * (blk - 4)
                            P.op("act", lambda e, tt=tt, bank=bank, o0=o0: e.copy(
                                out=tmv(tt)[:, o0:o0 + 256], in_=pb[bank][:, 0:256]),
                                pk(bank), hk(32 + 3 * tt, 35 + 3 * tt))
                    if blk < 4:
                        dv = v_d[g * TG:(g + 1) * TG, blk * 256:(blk + 1) * 256].rearrange("(tt p) n -> p tt n", p=128)
                        P.dma("sp", lambda e, dv=dv: e.dma_start(out=dv, in_=sbv[:]), ["sbv"], ["v_d"], key="st2")

                for tt in range(4):
                    ts_ = slice(tt * 128, (tt + 1) * 128)
                    P.op("pe", lambda e, ts_=ts_: e.matmul(pb[1][:, :], lrT[0:17, ts_], wgu[0:17, :], start=True, stop=True),
                         ["lrT", "wgu"], pk(1))
                    si = rot("sg", 2)
                    _act(P, sg[si][:], pb[1][:], AF.Exp, pk(1), [("sg", si)], scale=-1.0)
                    _act(P, la[:, tt, :], sg[si][:], AF.Ln, [("sg", si)], [("la", tt)], bias=1.0)
                    P.op("pe", lambda e, tt=tt: e.matmul(pb[2][:, :], cm[:, 2, :], la[:, tt, :], start=True, stop=True),
                         ["cm", ("la", tt)], pk(2))
                    sj = rot("sg", 2)
                    _act(P, sg[sj][:], pb[2][:], AF.Exp, pk(2), [("sg", sj)])
                    P.op("dve", lambda e, sj=sj, tt=tt: e.tensor_tensor(
                        out=khat, in0=sg[sj][:], in1=tmv(tt)[:, 1024:1536], op=ALU.mult),
                        [("sg", sj)] + hk(32 + 3 * tt, 35 + 3 * tt), [("sT16", 1)])
                    for h in range(8):
                        ob = 5 + h // 4
                        oq = h % 4
                        osl = slice(oq * 128, (oq + 1) * 128)
                        vh = tmv(tt)[:, h * 128:(h + 1) * 128]
                        hkv = hk(32 + 3 * tt, 35 + 3 * tt)
                        P.op("pe", lambda e, h=h, tt=tt: e.matmul(pb[3][0:64, 0:128], la[:, tt, h * 64:(h + 1) * 64],
                                                                  cm[:, 1, :], start=True, stop=True),
                             ["cm", ("la", tt)], pk(3, 0))
                        _act(P, e1[:], pb[3][0:64, 0:128], AF.Exp, pk(3, 0), ["e1"])
                        _act(P, e2[:], pb[3][0:64, 0:128], AF.Exp, pk(3, 0), ["e2"], scale=-1.0)
                        _act(P, dec[:, h:h + 1], pb[3][0:64, 127:128], AF.Exp, pk(3, 0), [("dec", h)])
                        P.op("dve", lambda e, h=h, ts_=ts_: e.tensor_tensor(
                            out=qtl[:], in0=hid[0:64, 16 + h, ts_], in1=e1[:], op=ALU.mult),
                            ["e1"] + hk(16 + h, 17 + h), ["qtl"])
                        P.op("dve", lambda e, h=h, ts_=ts_: e.tensor_tensor(
                            out=ktl[:], in0=hid[0:64, 24 + h, ts_], in1=e2[:], op=ALU.mult),
                            ["e2"] + hk(24 + h, 25 + h), ["ktl"])
                        P.op("pe", lambda e: e.matmul(pb[4][:, 0:128], ktl[:], qtl[:], start=True, stop=True),
                             ["ktl", "qtl"], pk(4, 0))
                        P.op("dve", lambda e: e.tensor_tensor(out=PT[:], in0=pb[4][:, 0:128], in1=cm[:, 3, :], op=ALU.mult),
                             pk(4, 0) + ["cm"], ["PT"])
                        P.op("pe", lambda e, ob=ob, osl=osl, vh=vh: e.matmul(pb[ob][:, osl], vh, PT[:], start=True, stop=False),
                             ["PT"] + hkv, pk(ob, oq))
                        P.op("pe", lambda e, ob=ob, osl=osl, h=h: e.matmul(pb[ob][:, osl], Sb[:, h, :], qtl[:], start=False, stop=True),
                             [("Sb", h), "qtl"], pk(ob, oq))
                        P.op("pe", lambda e, h=h, vh=vh: e.matmul(pb[7][0:64, 0:128], khat[:, h * 64:(h + 1) * 64], vh,
                                                                  start=True, stop=True),
                             [("sT16", 1)] + hkv, pk(7, 0))
                        P.op("dve", lambda e, h=h: e.scalar_tensor_tensor(
                            out=S32[:, h, :], in0=S32[:, h, :], scalar=dec[:, h:h + 1], in1=pb[7][0:64, 0:128],
                            op0=ALU.mult, op1=ALU.add), [("S32", h), ("dec", h)] + pk(7, 0), [("S32", h)])
                        P.op("act", lambda e, h=h: e.copy(out=Sb[:, h, :], in_=S32[:, h, :]), [("S32", h)], [("Sb", h)])
                    for hb in range(2):
                        ob = 5 + hb
                        i = rot("sq", 2)
                        _act(P, sq[i][:], pb[ob][:], AF.Square, pk(ob), [("sq", i)])
                        P.op("pe", lambda e, i=i: e.matmul(pb[0][:], ones[:], sq[i][:], start=True, stop=True),
                             [("sq", i), "ones"], pk(0))
                        P.op("dve", lambda e: e.tensor_scalar(out=rstd[:], in0=pb[0][:], scalar1=1.0 / 128, scalar2=EPS,
                                                              op0=ALU.mult, op1=ALU.add), pk(0), ["rstd"])
                        _act(P, rstd[:], rstd[:], AF.Sqrt, ["rstd"], ["rstd"])
                        P.op("dve", lambda e: e.reciprocal(out=rstd[:], in_=rstd[:]), ["rstd"], ["rstd"])
                        ti = rot("tmp", 2)
                        P.op("dve", lambda e, ti=ti, ob=ob: e.tensor_tensor(out=tmp[ti][:], in0=pb[ob][:], in1=rstd[:], op=ALU.mult),
                             pk(ob) + ["rstd"], [("tmp", ti)])
                        for oq in range(4):
                            h = hb * 4 + oq
                            P.op("dve", lambda e, ti=ti, oq=oq, h=h, ts_=ts_: e.scalar_tensor_tensor(
                                out=hid[:, h, ts_], in0=tmp[ti][:, oq * 128:(oq + 1) * 128],
                                scalar=glag[:, ei * 8 + h:ei * 8 + h + 1], in1=yS[:, h, ts_],
                                op0=ALU.mult, op1=ALU.mult),
                                [("tmp", ti), "glag", ("yS", h)], hk(h, h + 1))
                mv = mixT_d[8:16].rearrange("h p t -> p h t")
                P.dma("sp", lambda e: e.dma_start(out=mv[:, :, gs], in_=hid[:, 0:8, :]), hk(0, 8), ["mixT_d"], key="st3")

            def sb_attention():
                nb = L // 128
                KT = hid[:, 0:8, :].rearrange("p a b -> p (a b)")
                QT = hid[:, 8:16, :].rearrange("p a b -> p (a b)")
                VV = hid[:, 16:24, :].rearrange("p a (b e) -> p (a b) e", e=128)
                OT = hid[:, 24:32, :].rearrange("p a b -> p (a b)")
                streams = [
                    dict(e=sg[0][:], ke=[("sg", 0)], sp=sg[1][:], ksp=[("sg", 1)], er=tmp[0][:], ker=[("tmp", 0)],
                         S32=tmp[1][:], kS32=[("tmp", 1)], sp16=xt16[:, 0, :], ksp16=[("xt16", 0)],
                         S16=xt16[:, 1, :], kS16=[("xt16", 1)], A16=sT16[:, 0, :], kA16=[("sT16", 0)],
                         bz=4, br=4, bo=0)]
                for si_ in range(3):
                    y0, h0 = 4 * si_, 32 + 3 * si_
                    streams.append(dict(
                        e=yS[:, y0, :], ke=[("yS", y0)], sp=yS[:, y0 + 1, :], ksp=[("yS", y0 + 1)],
                        er=yS[:, y0 + 2, :], ker=[("yS", y0 + 2)], S32=yS[:, y0 + 3, :], kS32=[("yS", y0 + 3)],
                        sp16=hid[:, h0, :], ksp16=hk(h0, h0 + 1), S16=hid[:, h0 + 1, :], kS16=hk(h0 + 1, h0 + 2),
                        A16=hid[:, h0 + 2, :], kA16=hk(h0 + 2, h0 + 3),
                        bz=5 + si_, br=5 + si_, bo=1 + si_))

                def step(st, G, kb, first):
                    di = kb - 4 * G
                    bz, br, bo = st["bz"], st["br"], st["bo"]
                    P.op("pe", lambda e: e.matmul(pb[bz][:], KT[:, kb * 128:(kb + 1) * 128],
                                                  QT[:, G * TG:(G + 1) * TG], start=True, stop=True),
                         hk(0, 16), pk(bz))
                    yield
                    _act(P, st["e"], pb[bz][:], AF.Exp, pk(bz), st["ke"])
                    yield
                    if di >= 0:
                        _act(P, st["sp"], st["e"], AF.Ln, st["ke"], st["ksp"], bias=1.0)
                        P.op("dve", lambda e: e.tensor_tensor(out=st["sp16"], in0=st["sp"], in1=sbm[:, di, :], op=ALU.mult),
                             st["ksp"] + ["sbm"], st["ksp16"])
                    else:
                        _act(P, st["sp16"], st["e"], AF.Ln, st["ke"], st["ksp16"], bias=1.0)
                    yield
                    P.op("pe", lambda e: e.matmul(pb[br][:], cm[:, 0, :], st["sp16"], start=True, stop=first),
                         ["cm"] + st["ksp16"], pk(br))
                    if not first:
                        P.op("pe", lambda e: e.matmul(pb[br][:], ones[:], st["S16"], start=False, stop=True),
                             ["ones"] + st["kS16"], pk(br))
                    yield
                    _act(P, st["er"], pb[br][:], AF.Exp, pk(br), st["ker"], scale=-1.0)
                    yield
                    P.op("dve", lambda e: e.tensor_tensor(out=st["A16"], in0=st["e"], in1=st["er"], op=ALU.mult),
                         st["ke"] + st["ker"], st["kA16"])
                    if di >= 0:
                        P.op("dve", lambda e: e.tensor_tensor(out=st["A16"], in0=st["A16"], in1=sbm[:, di, :], op=ALU.mult),
                             st["kA16"] + ["sbm"], st["kA16"])
                    yield
                    P.op("pe", lambda e: e.matmul(pb[bo][:], VV[:, kb, :], st["A16"], start=first, stop=(kb == 0)),
                         hk(16, 24) + st["kA16"], pk(bo))
                    if kb > 0:
                        if first:
                            P.op("pool", lambda e: e.tensor_copy(out=st["S32"], in_=st["sp16"]), st["ksp16"], st["kS32"])
                        else:
                            P.op("pool", lambda e: e.tensor_tensor(out=st["S32"], in0=st["S32"], in1=st["sp16"], op=ALU.add),
                                 st["ksp16"] + st["kS32"], st["kS32"])
                        P.op("pool", lambda e: e.tensor_copy(out=st["S16"], in_=st["S32"]), st["kS32"], st["kS16"])
                    if kb == 0:
                        P.op("act", lambda e: e.copy(out=OT[:, G * TG:(G + 1) * TG], in_=pb[bo][:]),
                             pk(bo), hk(24 + G, 25 + G))
                    yield

                def run_lockstep(gens):
                    gens = list(gens)
                    while gens:
                        nxt = []
                        for g_ in gens:
                            try:
                                next(g_)
                                nxt.append(g_)
                            except StopIteration:
                                pass
                        gens = nxt

                for h in range(8):
                    P.dma("sp", lambda e, h=h: e.dma_start(out=KT[:, 0:L], in_=kT_d[h]), ["kT_d"], hk(0, 8), key="ld0")
                    P.dma("sp", lambda e, h=h: e.dma_start(out=QT[:, 0:L], in_=qT_d[h]), ["qT_d"], hk(8, 16), key="ld1")
                    P.dma("sp", lambda e, h=h: e.dma_start(
                        out=VV[:, 0:nb, :], in_=v_d[:, h * 128:(h + 1) * 128].rearrange("(kb p) e -> p kb e", p=128)),
                        ["v_d"], hk(16, 24), key="ld2")
                    nG = L // TG
                    for G0 in range(0, nG, 4):
                        Gs = list(range(G0, min(G0 + 4, nG)))
                        for kb in range(4 * Gs[-1] + 3, -1, -1):
                            act_ = [(si_, G) for si_, G in enumerate(Gs) if 4 * G + 3 >= kb]
                            run_lockstep([step(streams[si_], G, kb, kb == 4 * G + 3) for si_, G in act_])
                    P.dma("sp", lambda e, h=h: e.dma_start(out=mixT_d[h], in_=OT[:, 0:L]), hk(24, 32), ["mixT_d"], key="st4")

            def even_m3(li, ei, g, src):
                mv = mixT_d.rearrange("c p t -> p c t")
                P.dma("sp", lambda e: e.dma_start(out=xn[:], in_=mv[:, :, g * TG:(g + 1) * TG]),
                      ["mixT_d"], [("xn", c) for c in range(NCH)], key="ld3")
                load_h(src, g)
                dense_out(xn, "xn", NCH, w_out[ei])
                post_residual(li, 1)
                store_h(hT, g)

            def even_mixer(li, ei, src):
                P.dma("pool", lambda e: e.dma_start(out=wgu[:], in_=wgu_in[ei]), [], ["wgu"], key="c5")
                P.op("dve", lambda e: e.memset(S32[:], 0.0), [], [("S32", h) for h in range(8)])
                P.op("dve", lambda e: e.memset(Sb[:], 0.0), [], [("Sb", h) for h in range(8)])
                for g in range(n_tg):
                    even_m1(li, ei, g, src)
                sb_attention()
                for g in range(n_tg):
                    even_m3(li, ei, g, src)


        if n_odd:
            PI = float(np.pi)
            s5_lam = B.dram_in("s5_lam", [n_odd, 3, 8192])
            s5_lamT = B.dram_in("s5_lamT", [n_odd, 128, 3, 64])
            s5_lamB = B.dram_in("s5_lamB", [n_odd, 128, 3, NCH, 64])
            s5_bT = B.dram_in("s5_bT", [n_odd, 128, 2, NCH, 64])
            s5_cw = B.dram_in("s5_cw", [n_odd, NCH, 128, 8, 128])
            s5_dT = B.dram_in("s5_dT", [128, n_odd * NCH])
            w_glu = B.dram_in("w_glu", [n_odd, 2 * D_MODEL // 128, 128, D_MODEL])
            jc_in = B.dram_in("jconst", [128, 138])
            tle_in = B.dram_in("trile", [128, 128], BF16)
            xnT_d = B.dram_tmp("xnT_d", [NCH, 128, L], BF16)
            geT_d = B.dram_tmp("geT_d", [NCH, 128, L], BF16)

            jc = B.sb(stack, "jc", [128, 138], F32)
            tle = B.sb(stack, "tle", [128, 128], BF16)
            lamT = B.sb(stack, "lamT", [128, 3, 64], F32)
            arT = B.sb(stack, "arT", [128, 64], F32)
            aiT = B.sb(stack, "aiT", [128, 64], F32)
            lbr = B.sb(stack, "lbr", [128, 64], F32)
            lbi = B.sb(stack, "lbi", [128, 64], F32)
            dsk = B.sb(stack, "dsk", [128, n_odd * NCH], F32)
            car = B.sb(stack, "car", [128, 2, 4], F32)
            lam128 = B.sb(stack, "lam128", [128, 2, 4], F32)
            zt = [B.sb(stack, "zt%d" % i, [128, 64], F32) for i in range(10)]
            lamB = B.sb(stack, "lamB", [128, 3, 64], F32)
            bTs = B.sb(stack, "bTs", [128, 2, 64], F32)
            Bblk = B.sb(stack, "Bblk", [128, 2, 8, 64], BF16)
            Cw = B.sb(stack, "Cw", [128, 8, 128], BF16)
            k4 = [B.sb(stack, "k4_%d" % i, [128, 4], F32) for i in range(4)]
            ytmp = B.sb(stack, "ytmp", [128, 128], F32)
            Ptab = yS[:, 4:6, :]
            Qtab = yS[:, 6:8, :]
            lb3 = yS[:, 12:15, :]
            u1b, u2b = yS[:, 15, :], yS[:, 3, :]
            pmg = yS[:, 2, :]
            jrow = jc[:, 0:128]
            jcol = jc[:, 128:129]

            if not (_SKIP & 1):
                P.dma("sp", lambda e: e.dma_start(out=jc[:], in_=jc_in[:, :]), [], ["jc"], key="c6")
                P.dma("sp", lambda e: e.dma_start(out=tle[:], in_=tle_in[:, :]), [], ["tle"], key="c7")
                P.dma("sp", lambda e: e.dma_start(out=dsk[:], in_=s5_dT[:, :]), [], ["dsk"], key="c8")

            def dv(out, in0, in1, op, r, w):
                P.op("dve", lambda e: e.tensor_tensor(out=out, in0=in0, in1=in1, op=op), r, w)

            def ds(out, in0, s1, s2, op0, op1, r, w, eng="dve"):
                if s2 is None:
                    P.op(eng, lambda e: e.tensor_scalar(out=out, in0=in0, scalar1=s1, scalar2=None, op0=op0), r, w)
                else:
                    P.op(eng, lambda e: e.tensor_scalar(out=out, in0=in0, scalar1=s1, scalar2=s2, op0=op0, op1=op1), r, w)

            MAGIC = 12582912.0

            def red_sin(buf, tb, kb, kt):
                ds(tb, buf, 1.0 / (2 * PI), MAGIC, ALU.mult, ALU.add, kb, kt)
                ds(tb, tb, -MAGIC, None, ALU.add, None, kt, kt)
                P.op("dve", lambda e: e.scalar_tensor_tensor(out=buf, in0=tb, scalar=-2 * PI, in1=buf,
                                                             op0=ALU.mult, op1=ALU.add), kt + kb, kb)
                ds(buf, buf, -3.14159, 3.14159, ALU.max, ALU.min, kb, kb)
                _act(P, buf, buf, AF.Sin, kb, kb)

            def sincos(o1, o2, ang_in, scal, rk, wk1, wk2, tb=None, kt=None):
                if tb is None:
                    tb, kt = zt[9][:], [("zt", 9)]
                ds(o1, ang_in, scal, None, ALU.mult, None, rk, [wk1])
                ds(o2, o1, 1.5 * PI, None, ALU.add, None, [wk1], [wk2])
                ds(o1, o1, PI, None, ALU.add, None, [wk1], [wk1])
                red_sin(o1, tb, [wk1], kt)
                red_sin(o2, tb, [wk2], kt)

            def odd_mixer(li, oi, src):
                xv = xnT_d.rearrange("c p t -> p c t")
                for g in range(n_tg if not (_SKIP & 2) else 0):
                    load_h(src, g)
                    prenorm(li, 0)
                    P.dma("sp", lambda e, g=g: e.dma_start(out=xv[:, :, g * TG:(g + 1) * TG], in_=xn[:]),
                          [("xn", c) for c in range(NCH)], ["xnT_d"], key="st5")
                def trivial_o3():
                    for g in range(n_tg if not (_SKIP & 4) else 0):
                        load_h(src, g)
                        store_h(hT, g)
                if _STOP == 1:
                    return trivial_o3()
                P.dma("sp", lambda e: e.dma_start(out=lamT[:], in_=s5_lamT[oi]), [], ["lamT"], key="c9")
                _act(P, zt[0][:], lamT[:, 2, :], AF.Exp, ["lamT"], [("zt", 0)])
                dv(arT[:], zt[0][:], lamT[:, 0, :], ALU.mult, [("zt", 0), "lamT"], ["arT"])
                dv(aiT[:], zt[0][:], lamT[:, 1, :], ALU.mult, [("zt", 0), "lamT"], ["aiT"])
                _act(P, zt[1][:], arT[:], AF.Exp, ["arT"], [("zt", 1)])
                sincos(zt[2][:], zt[3][:], aiT[:], 1.0, ["aiT"], ("zt", 2), ("zt", 3))
                P.op("dve", lambda e: e.scalar_tensor_tensor(out=lbr[:], in0=zt[1][:], scalar=-1.0, in1=zt[3][:],
                                                             op0=ALU.mult, op1=ALU.mult), [("zt", 1), ("zt", 3)], ["lbr"])
                P.op("dve", lambda e: e.scalar_tensor_tensor(out=lbi[:], in0=zt[1][:], scalar=-1.0, in1=zt[2][:],
                                                             op0=ALU.mult, op1=ALU.mult), [("zt", 1), ("zt", 2)], ["lbi"])
                if _STOP == 2:
                    return trivial_o3()
                UC = hid[:, 0:8, :].rearrange("p a b -> p (a b)")
                GS = hid[:, 8:16, :].rearrange("p a b -> p (a b)")
                CwN = hid[:, 18, :]
                yk = lambda a, b: [("yS", j) for j in range(a, b)]
                for c in range(NCH):
                    c4 = slice(c * 4, (c + 1) * 4)
                    P.dma("sp", lambda e, c=c: e.dma_start(out=lamB[:], in_=s5_lamB[oi][:, :, c, :]), [], ["lamB"], key="ld5")
                    P.dma("sp", lambda e, c=c: e.dma_start(out=bTs[:], in_=s5_bT[oi][:, :, c, :]), [], ["bTs"], key="ld6")
                    P.dma("pool", lambda e, c=c: e.dma_start(out=Cw[:], in_=s5_cw[oi][c]), [], ["Cw"], key="ld7")
                    for gp_ in range(4):
                        P.op("act", lambda e, gp_=gp_: e.mul(out=CwN[:, gp_ * 128:(gp_ + 1) * 128], in_=Cw[:, gp_ * 2, :], mul=-1.0),
                             ["Cw"], hk(18, 19))
                    for gp_ in range(4):
                        P.op("act", lambda e, gp_=gp_: e.mul(out=Cw[:, gp_ * 2 + 1, :], in_=Cw[:, gp_ * 2 + 1, :], mul=-1.0),
                             ["Cw"], ["Cw"])
                    P.dma("sp", lambda e, c=c: e.dma_start(out=UC[:, 0:L], in_=xnT_d[c]), ["xnT_d"], hk(0, 8), key="ld8")
                    z = lambda i: zt[i][:]
                    zk = lambda i: ("zt", i)
                    _act(P, z(0), lamB[:, 2, :], AF.Exp, ["lamB"], [zk(0)])
                    dv(z(1), z(0), lamB[:, 0, :], ALU.mult, [zk(0), "lamB"], [zk(1)])
                    dv(z(2), z(0), lamB[:, 1, :], ALU.mult, [zk(0), "lamB"], [zk(2)])
                    _act(P, z(1), z(1), AF.Exp, [zk(1)], [zk(1)])
                    sincos(z(3), z(4), z(2), 1.0, [zk(2)], zk(3), zk(4))
                    P.op("dve", lambda e: e.scalar_tensor_tensor(out=z(5), in0=z(1), scalar=-1.0, in1=z(4),
                                                                 op0=ALU.mult, op1=ALU.mult), [zk(1), zk(4)], [zk(5)])
                    P.op("dve", lambda e: e.scalar_tensor_tensor(out=z(6), in0=z(1), scalar=-1.0, in1=z(3),
                                                                 op0=ALU.mult, op1=ALU.mult), [zk(1), zk(3)], [zk(6)])
                    ds(z(5), z(5), -1.0, None, ALU.add, None, [zk(5)], [zk(5)])
                    dv(z(0), lamB[:, 0, :], lamB[:, 0, :], ALU.mult, ["lamB"], [zk(0)])
                    dv(z(1), lamB[:, 1, :], lamB[:, 1, :], ALU.mult, ["lamB"], [zk(1)])
                    dv(z(0), z(0), z(1), ALU.add, [zk(0), zk(1)], [zk(0)])
                    P.op("dve", lambda e: e.reciprocal(out=z(0), in_=z(0)), [zk(0)], [zk(0)])
                    dv(z(1), z(5), lamB[:, 0, :], ALU.mult, [zk(5), "lamB"], [zk(1)])
                    dv(z(2), z(6), lamB[:, 1, :], ALU.mult, [zk(6), "lamB"], [zk(2)])
                    dv(z(1), z(1), z(2), ALU.add, [zk(1), zk(2)], [zk(1)])
                    dv(z(7), z(1), z(0), ALU.mult, [zk(1), zk(0)], [zk(7)])
                    dv(z(1), z(6), lamB[:, 0, :], ALU.mult, [zk(6), "lamB"], [zk(1)])
                    dv(z(2), z(5), lamB[:, 1, :], ALU.mult, [zk(5), "lamB"], [zk(2)])
                    dv(z(1), z(1), z(2), ALU.subtract, [zk(1), zk(2)], [zk(1)])
                    dv(z(8), z(1), z(0), ALU.mult, [zk(1), zk(0)], [zk(8)])
                    dv(z(1), z(7), bTs[:, 0, :], ALU.mult, [zk(7), "bTs"], [zk(1)])
                    dv(z(2), z(8), bTs[:, 1, :], ALU.mult, [zk(8), "bTs"], [zk(2)])
                    dv(z(3), z(1), z(2), ALU.subtract, [zk(1), zk(2)], [zk(3)])
                    dv(z(1), z(7), bTs[:, 1, :], ALU.mult, [zk(7), "bTs"], [zk(1)])
                    dv(z(2), z(8), bTs[:, 0, :], ALU.mult, [zk(8), "bTs"], [zk(2)])
                    dv(z(4), z(1), z(2), ALU.add, [zk(1), zk(2)], [zk(4)])
                    for ri in range(2):
                        for gg in range(8):
                            ds(Bblk[:, ri, gg, :], z(3 + ri), jc[:, 130 + gg:131 + gg], None, ALU.mult, None,
                               [zk(3 + ri), "jc"], ["Bblk"])
                    if _STOP == 3:
                        continue
                    lv = s5_lam[oi].rearrange("k (c n) -> k c n", c=NCH)
                    for k3 in range(3):
                        if _DBG == 2:
                            P.dma("sp", lambda e, k3=k3, c=c: e.dma_start(
                                out=lb3[0:1, k3, :], in_=lv[k3, c:c + 1, :]),
                                [], yk(12 + k3, 13 + k3), key="ld9_%d" % k3)
                        else:
                            P.dma("pool" if _DBG == 1 else "sp", lambda e, k3=k3, c=c: e.dma_start(
                                out=lb3[:, k3, :], in_=lv[k3, c:c + 1, :].partition_broadcast(128)),
                                [], yk(12 + k3, 13 + k3), key="ld9_%d" % k3)
                    _act(P, lb3[:, 2, :], lb3[:, 2, :], AF.Exp, yk(14, 15), yk(14, 15))
                    dv(lb3[:, 0, :], lb3[:, 0, :], lb3[:, 2, :], ALU.mult, yk(12, 13) + yk(14, 15), yk(12, 13))
                    dv(lb3[:, 1, :], lb3[:, 1, :], lb3[:, 2, :], ALU.mult, yk(13, 14) + yk(14, 15), yk(13, 14))
                    ds(pmg, lb3[:, 0, :], jcol, None, ALU.mult, None, yk(12, 13) + ["jc"], yk(2, 3))
                    _act(P, pmg, pmg, AF.Exp, yk(2, 3), yk(2, 3), scale=-1.0)
                    sincos(u1b, u2b, lb3[:, 1, :], jcol, yk(13, 14) + ["jc"], ("yS", 15), ("yS", 3), tb=sg[0][:], kt=[("sg", 0)])
                    P.op("dve", lambda e: e.scalar_tensor_tensor(out=Ptab[:, 0, :], in0=pmg, scalar=-1.0, in1=u2b,
                                                                 op0=ALU.mult, op1=ALU.mult), yk(2, 4), yk(4, 5))
                    dv(Ptab[:, 1, :], pmg, u1b, ALU.mult, yk(2, 3) + yk(15, 16), yk(5, 6))
                    for gp in range(4):
                        cg = c * 4 + gp
                        qs = slice(gp * 128, (gp + 1) * 128)
                        ds(pmg[:, qs], jrow, arT[:, cg:cg + 1], None, ALU.mult, None, ["jc", "arT"], yk(2, 3))
                        ds(u1b[:, qs], jrow, aiT[:, cg:cg + 1], None, ALU.mult, None, ["jc", "aiT"], yk(15, 16))
                    _act(P, pmg, pmg, AF.Exp, yk(2, 3), yk(2, 3))
                    ds(u2b, u1b, 1.5 * PI, None, ALU.add, None, yk(15, 16), yk(3, 4))
                    ds(u1b, u1b, PI, None, ALU.add, None, yk(15, 16), yk(15, 16))
                    red_sin(u1b, sg[0][:], yk(15, 16), [("sg", 0)])
                    red_sin(u2b, sg[0][:], yk(3, 4), [("sg", 0)])
                    P.op("dve", lambda e: e.scalar_tensor_tensor(out=Qtab[:, 0, :], in0=pmg, scalar=-1.0, in1=u2b,
                                                                 op0=ALU.mult, op1=ALU.mult), yk(2, 4), yk(6, 7))
                    P.op("dve", lambda e: e.scalar_tensor_tensor(out=Qtab[:, 1, :], in0=pmg, scalar=-1.0, in1=u1b,
                                                                 op0=ALU.mult, op1=ALU.mult), yk(2, 3) + yk(15, 16), yk(7, 8))
                    if _STOP == 4:
                        continue
                    q127r = Qtab[:, 0, :].rearrange("p (g j) -> p g j", j=128)[:, :, 127]
                    q127i = Qtab[:, 1, :].rearrange("p (g j) -> p g j", j=128)[:, :, 127]
                    dv(k4[0][:], lbr[:, c4], q127r, ALU.mult, ["lbr"] + yk(6, 7), [("k4", 0)])
                    dv(k4[1][:], lbi[:, c4], q127i, ALU.mult, ["lbi"] + yk(7, 8), [("k4", 1)])
                    dv(k4[2][:], lbr[:, c4], q127i, ALU.mult, ["lbr"] + yk(7, 8), [("k4", 2)])
                    dv(k4[3][:], lbi[:, c4], q127r, ALU.mult, ["lbi"] + yk(6, 7), [("k4", 3)])
                    dv(lam128[:, 0, :], k4[0][:], k4[1][:], ALU.subtract, [("k4", 0), ("k4", 1)], ["lam128"])
                    dv(lam128[:, 1, :], k4[2][:], k4[3][:], ALU.add, [("k4", 2), ("k4", 3)], ["lam128"])
                    P.op("dve", lambda e: e.memset(car[:], 0.0), [], ["car"])
                    def s5_front(k, c=c, c4=c4):
                        ks = slice(k * 128, (k + 1) * 128)
                        for ri in range(2):
                            P.op("pe", lambda e, ri=ri, ks=ks: e.matmul(
                                pb[1 + ri][:], UC[:, ks], Bblk[:, ri, :, :].rearrange("p a b -> p (a b)"),
                                start=True, stop=True), hk(0, 8) + ["Bblk"], pk(1 + ri))
                        t1, t2 = sg[0][:], sg[1][:]
                        dv(t1, pb[1][:], Ptab[:, 0, :], ALU.mult, pk(1) + yk(4, 5), [("sg", 0)])
                        dv(t2, pb[2][:], Ptab[:, 1, :], ALU.mult, pk(2) + yk(5, 6), [("sg", 1)])
                        dv(xt16[:, 0, :], t1, t2, ALU.subtract, [("sg", 0), ("sg", 1)], [("xt16", 0)])
                        dv(t1, pb[2][:], Ptab[:, 0, :], ALU.mult, pk(2) + yk(4, 5), [("sg", 0)])
                        dv(t2, pb[1][:], Ptab[:, 1, :], ALU.mult, pk(1) + yk(5, 6), [("sg", 1)])
                        dv(xt16[:, 1, :], t1, t2, ALU.add, [("sg", 0), ("sg", 1)], [("xt16", 1)])
                        cb = 3 if k % 2 == 0 else 6
                        for ri in range(2):
                            for gp in range(4):
                                P.op("pe", lambda e, ri=ri, gp=gp, cb=cb: e.matmul(
                                    pb[cb + ri][:, gp * 128:(gp + 1) * 128], xt16[:, ri, gp * 128:(gp + 1) * 128], tle[:],
                                    start=True, stop=True), [("xt16", ri), "tle"], pk(cb + ri))
                    def s5_mid(k, c=c, c4=c4):
                        par = k % 2
                        ccr_, cci_ = yS[:, 8 + 2 * par, :], yS[:, 9 + 2 * par, :]
                        kcr, kci = yk(8 + 2 * par, 9 + 2 * par), yk(9 + 2 * par, 10 + 2 * par)
                        for gp in range(4):
                            qs = slice(gp * 128, (gp + 1) * 128)
                            cb = 3 if par == 0 else 6
                            _act(P, ccr_[:, qs], pb[cb][:, qs], AF.Identity, pk(cb) + ["car"], kcr, bias=car[:, 0, gp:gp + 1])
                            _act(P, cci_[:, qs], pb[cb + 1][:, qs], AF.Identity, pk(cb + 1) + ["car"], kci, bias=car[:, 1, gp:gp + 1])
                        cr127 = ccr_.rearrange("p (g j) -> p g j", j=128)[:, :, 127]
                        ci127 = cci_.rearrange("p (g j) -> p g j", j=128)[:, :, 127]
                        dv(k4[0][:], lam128[:, 0, :], cr127, ALU.mult, ["lam128"] + kcr, [("k4", 0)])
                        dv(k4[1][:], lam128[:, 1, :], ci127, ALU.mult, ["lam128"] + kci, [("k4", 1)])
                        dv(k4[2][:], lam128[:, 0, :], ci127, ALU.mult, ["lam128"] + kci, [("k4", 2)])
                        dv(k4[3][:], lam128[:, 1, :], cr127, ALU.mult, ["lam128"] + kcr, [("k4", 3)])
                        dv(car[:, 0, :], k4[0][:], k4[1][:], ALU.subtract, [("k4", 0), ("k4", 1)], ["car"])
                        dv(car[:, 1, :], k4[2][:], k4[3][:], ALU.add, [("k4", 2), ("k4", 3)], ["car"])
                    def s5_back(k, c=c, c4=c4):
                        ks = slice(k * 128, (k + 1) * 128)
                        par = k % 2
                        ccr_, cci_ = yS[:, 8 + 2 * par, :], yS[:, 9 + 2 * par, :]
                        kcr, kci = yk(8 + 2 * par, 9 + 2 * par), yk(9 + 2 * par, 10 + 2 * par)
                        pv = lambda out, in0, in1, r, w: P.op(
                            "pool", lambda e: e.tensor_tensor(out=out, in0=in0, in1=in1, op=ALU.mult), r, w)
                        prods = [(sT16[:, 0, :], [("sT16", 0)], ccr_, kcr, 0, 0),
                                 (sT16[:, 1, :], [("sT16", 1)], cci_, kci, 1, 2),
                                 (hid[:, 16, :], hk(16, 17), cci_, kci, 0, 1),
                                 (hid[:, 17, :], hk(17, 18), ccr_, kcr, 1, 1)]
                        for dst, kd, src, ksrc, qi, wsel in prods:
                            pv(dst, src, Qtab[:, qi, :], ksrc + yk(6 + qi, 7 + qi), kd)
                        n = 0
                        for dst, kd, src, ksrc, qi, wsel in prods:
                            for gp in range(4):
                                if wsel == 2:
                                    wblk, wk = CwN[:, gp * 128:(gp + 1) * 128], hk(18, 19)
                                else:
                                    wblk, wk = Cw[:, gp * 2 + wsel, :], ["Cw"]
                                P.op("pe", lambda e, gp=gp, n=n, dst=dst, wblk=wblk: e.matmul(
                                    pb[5][:, 0:128], wblk, dst[:, gp * 128:(gp + 1) * 128],
                                    start=(n == 0), stop=(n == 15)), wk + kd, pk(5, 0))
                                n += 1
                        P.op("dve", lambda e, ks=ks, c=c: e.scalar_tensor_tensor(
                            out=ytmp[:], in0=UC[:, ks], scalar=dsk[:, oi * NCH + c:oi * NCH + c + 1], in1=pb[5][:, 0:128],
                            op0=ALU.mult, op1=ALU.add), hk(0, 8) + ["dsk"] + pk(5, 0), ["ytmp"])
                        _act(P, GS[:, ks], ytmp[:], AF.Gelu, ["ytmp"], hk(8 + k // 4, 9 + k // 4))
                    nk = L // 128
                    s5_front(0)
                    for k in range(nk):
                        if k + 1 < nk:
                            s5_front(k + 1)
                        s5_mid(k)
                        if k >= 1:
                            s5_back(k - 1)
                    s5_back(nk - 1)
                    if _STOP:
                        continue
                    P.dma("sp", lambda e, c=c: e.dma_start(out=geT_d[c], in_=GS[:, 0:L]), hk(8, 16), ["geT_d"], key="st6")
                if _STOP:
                    return trivial_o3()
                gv = geT_d.rearrange("c p t -> p c t")
                wv = w_glu[oi]
                for g in range(n_tg):
                    P.dma("sp", lambda e, g=g: e.dma_start(out=xn[:], in_=gv[:, :, g * TG:(g + 1) * TG]),
                          ["geT_d"], [("xn", c) for c in range(NCH)], key="ld3")
                    load_h(src, g)
                    for j in range(NCH):
                        i = rot("wi", NWI)
                        P.dma("pool", lambda e, i=i, j=j: [
                            e.dma_start(out=wi[i][:, 0, :, :], in_=wv[j].rearrange("p (c f) -> p c f", c=NCH)),
                            e.dma_start(out=wi[i][:, 1, :, :], in_=wv[NCH + j].rearrange("p (c f) -> p c f", c=NCH))],
                            [], [("wi", i)], key="wi%d" % i, ndma=2)
                        bv = (2 * j) % 4 + 4
                        bgt = bv + 1
                        for a, bank in ((0, bv), (1, bgt)):
                            for c in range(NCH):
                                P.op("pe", lambda e, i=i, c=c, a=a, bank=bank: e.matmul(
                                    pb[bank][:], wi[i][:, a, c, :], xn[:, c, :], start=(c == 0), stop=(c == NCH - 1)),
                                    [("wi", i), ("xn", c)], pk(bank))
                        si = rot("sg", 2)
                        _act(P, sg[si][:], pb[bgt][:], AF.Sigmoid, pk(bgt), [("sg", si)])
                        P.op("dve", lambda e, si=si, j=j, bv=bv: e.tensor_tensor(
                            out=yS[:, j, :], in0=sg[si][:], in1=pb[bv][:], op=ALU.mult),
                            [("sg", si)] + pk(bv), [("yS", j)])
                    post_residual(li, 1)
                    store_h(hT, g)

        ei_map, oi_map = {}, {}
        for li, kind in enumerate(layers):
            if kind == "even":
                ei_map[li] = len(ei_map)
            elif kind == "odd":
                oi_map[li] = len(oi_map)
        for li, kind in enumerate(layers):
            src = xT if li == 0 else hT
            last = (li == depth - 1)
            if kind == "even":
                even_mixer(li, ei_map[li], src)
                src = hT
            elif kind == "odd":
                odd_mixer(li, oi_map[li], src)
                src = hT
            for g in range(n_tg):
                load_h(src, g)
                prenorm(li, 2)
                ffn(li)
                post_residual(li, 3)
                store_h(yT if last else hT, g)

        P.emit(stack)
    nc.in_names_ = list(B.in_names)
    return nc


def _bf16(a):
    return np.asarray(a, dtype=np.float32).astype(ml_dtypes.bfloat16)


def host_consts():
    i = np.arange(128)
    cm = np.zeros((128, 4, 128), np.float32)
    cm[:, 0, :] = (i[:, None] >= i[None, :])
    cm[:, 1, :] = (i[:, None] <= i[None, :]) * (-1.0 / 16)
    cm[:, 2, :] = (i[:, None] > i[None, :]) * (-1.0 / 16)
    cm[:, 3, :] = (i[:, None] <= i[None, :])
    sbm = np.zeros((128, 4, TG), np.float32)
    for d in range(4):
        for tb in range(4):
            if tb > d:
                sbm[:, d, tb * 128:(tb + 1) * 128] = 1.0
            elif tb == d:
                sbm[:, d, tb * 128:(tb + 1) * 128] = (i[:, None] < i[None, :])
    return {"cm128": _bf16(cm), "sbmask": _bf16(sbm), "ones_bf": _bf16(np.ones((128, 128)))}


def _tile_w(w):
    n, k, N = w.shape
    t = w.reshape(n, k // 128, 128, N // 128, 128).transpose(0, 3, 2, 1, 4)
    return np.ascontiguousarray(t).reshape(n, N // 128, 128, k)


def host_layout(inputs, layers):
    depth = len(layers)
    f = lambda k: np.asarray(inputs[k], dtype=np.float32)
    g = f("norm_gains")[:depth]
    m = {"gains": np.ascontiguousarray(g.reshape(depth * 4, NCH, 128).transpose(2, 0, 1).reshape(128, depth * 4 * NCH)),
         "w_ffn_in": _tile_w(f("w_ffn_in")[:depth]), "w_ffn_out": f("w_ffn_out")[:depth]}
    ne = sum(1 for l in layers if l == "even")
    no = sum(1 for l in layers if l == "odd")
    if ne:
        w_in = f("w_in")[:ne]
        m["w_in_t128"] = _tile_w(w_in[:, :, :6144])
        tmcols = [2048 + 256 * b for b in range(4)] + [4096 + 256 * b for b in range(4)] + [3584, 3840]
        tm = np.stack([w_in[:, :, c0:c0 + 256] for c0 in tmcols], axis=1)
        m["w_in_tm"] = np.ascontiguousarray(
            tm.reshape(ne, 10, NCH, 128, 256).transpose(0, 1, 3, 2, 4)).reshape(ne, 10, 128, NCH * 256)
        lr = w_in[:, :, 6144:6160]
        m["w_in_lr"] = np.ascontiguousarray(lr.reshape(ne, NCH, 128, 16).transpose(0, 2, 1, 3)).reshape(ne, 128, NCH * 16)
        m["w_out"] = f("w_out")[:ne]
        m["wgu"] = np.ascontiguousarray(np.concatenate([f("w_gate_up")[:ne], f("b_gate")[:ne, None, :]], axis=1))
        gg = f("gla_norm_gain")[:ne]
        m["gla_gain"] = np.ascontiguousarray(gg.reshape(ne, 8, 128).transpose(2, 0, 1).reshape(128, ne * 8))
    if no:
        lre, lim, ls = f("s5_lambda_re")[:no], f("s5_lambda_im")[:no], f("s5_log_step")[:no]
        lse = np.broadcast_to(ls[:, :, None], lre.shape)
        lam3 = np.stack([lre, lim, lse], axis=1)
        m["s5_lam"] = np.ascontiguousarray(lam3.reshape(no, 3, 8192))
        t = lam3.reshape(no, 3, 64, 2, 64)
        m["s5_lamT"] = np.ascontiguousarray(t.transpose(0, 3, 4, 1, 2).reshape(no, 128, 3, 64))
        t = lam3.reshape(no, 3, NCH, 8, 1, 64)
        t = np.broadcast_to(t, (no, 3, NCH, 8, 16, 64))
        m["s5_lamB"] = np.ascontiguousarray(t.transpose(0, 3, 4, 1, 2, 5).reshape(no, 128, 3, NCH, 64))
        b2 = np.stack([f("s5_b_re")[:no], f("s5_b_im")[:no]], axis=1)
        t = b2.reshape(no, 2, NCH, 8, 64, 16)
        m["s5_bT"] = np.ascontiguousarray(t.transpose(0, 3, 5, 1, 2, 4).reshape(no, 128, 2, NCH, 64))
        c2 = np.stack([f("s5_c_re")[:no], f("s5_c_im")[:no]], axis=1)
        cw = np.zeros((no, NCH, 2, 64, 4, 2, 8, 16), np.float32)
        t = c2.reshape(no, 2, NCH, 4, 2, 16, 64)
        for gp in range(4):
            for g2 in range(2):
                cw[:, :, g2, :, gp, :, 2 * gp + g2, :] = t[:, :, :, gp, g2].transpose(0, 2, 4, 1, 3)
        m["s5_cw"] = np.ascontiguousarray(cw.reshape(no, NCH, 128, 8, 128))
        m["s5_dT"] = np.ascontiguousarray(f("s5_d")[:no].reshape(no, NCH, 128).transpose(2, 0, 1).reshape(128, no * NCH))
        m["w_glu"] = _tile_w(f("w_glu")[:no])
        jc = np.zeros((128, 138), np.float32)
        jc[:, 0:128] = np.arange(128)[None, :]
        jc[:, 128] = np.arange(128)
        jc[:, 129] = 1.0
        jc[:, 130:138] = (np.arange(128)[:, None] // 16 == np.arange(8)[None, :])
        m["jconst"] = jc
        i = np.arange(128)
        m["trile"] = _bf16((i[:, None] <= i[None, :]).astype(np.float32))
    m.update(host_consts())
    return m, no


_CACHE = {}


def kernel(**inputs):
    layers = ["even", "odd"] * (DEPTH // 2)
    x = np.asarray(inputs["x"], dtype=np.float32)
    m, _ = host_layout(inputs, layers)
    if "nc" not in _CACHE:
        _CACHE["nc"] = build(SEQ, layers)
    nc = _CACHE["nc"]
    in_maps = []
    for b in range(BATCH):
        mm = dict(m)
        mm["xT"] = np.ascontiguousarray(x[b].T)
        in_maps.append(mm)
    res = run_bass_kernel_spmd(nc, in_maps, core_ids=list(range(BATCH)))
    out = np.stack([np.ascontiguousarray(res.results[b]["yT"].T) for b in range(BATCH)], axis=0)
    return out.astype(np.float32)
```
